# Optimizing a Trainium2 kernel written in Bass

```python
import jax, jax.numpy as jnp
from jax import lax
import numpy as np

D_MODEL = 1024
BATCH = 8
SEQ = 2048
DEPTH = 1
DEC_BATCH = 128
DEC_SEQ = 4
PAST_LEN = 16384
PAGE_SIZE = 128

GLA_HEADS = 4
GLA_KEY_DIM = D_MODEL // 2
GLA_VAL_DIM = D_MODEL
GLA_DK = GLA_KEY_DIM // GLA_HEADS
GLA_DV = GLA_VAL_DIM // GLA_HEADS
GLA_RANK = 16
GLA_TAU = 16.0
GLA_CHUNK = 64
CONV_DIM = D_MODEL
CONV_WIDTH = 31
PLE_DIM = 256
EPS = 1e-6

SPLITS = (GLA_KEY_DIM, GLA_KEY_DIM, GLA_VAL_DIM, GLA_VAL_DIM, GLA_RANK, 2 * CONV_DIM, CONV_DIM, D_MODEL, D_MODEL)
D_IN = sum(SPLITS)
SPLIT_POINTS = [int(s) for s in np.cumsum(SPLITS)[:-1]]

kernel_name = 'gla_conformer_parallel_hybrid_step'


def rmsnorm(x, g):
    xf = x.astype(jnp.float32)
    y = xf * lax.rsqrt(jnp.mean(xf * xf, axis=-1, keepdims=True) + EPS)
    return (y * g.astype(jnp.float32)).astype(x.dtype)


def layernorm(x, g, b):
    xf = x.astype(jnp.float32)
    mu = jnp.mean(xf, axis=-1, keepdims=True)
    xc = xf - mu
    y = xc * lax.rsqrt(jnp.mean(xc * xc, axis=-1, keepdims=True) + EPS)
    return (y * g.astype(jnp.float32) + b.astype(jnp.float32)).astype(x.dtype)


def gla_chunked(q, k, v, log_a, s0):
    bsz, t_len = q.shape[0], q.shape[1]
    c = min(GLA_CHUNK, t_len)
    n_chunks = -(-t_len // c)
    pad = n_chunks * c - t_len

    def to_chunks(a):
        a = jnp.pad(a.astype(jnp.float32), ((0, 0), (0, pad), (0, 0), (0, 0)))
        return jnp.moveaxis(a.reshape(bsz, n_chunks, c, a.shape[2], a.shape[3]), 1, 0)

    causal = jnp.tril(jnp.ones((c, c), dtype=bool))[None, :, :, None, None]

    def step(s, inp):
        qc, kc, vc, ac = inp
        b = jnp.cumsum(ac, axis=1)
        o_inter = jnp.einsum('bthk,bhkv->bthv', qc * jnp.exp(b), s)
        diff = b[:, :, None] - b[:, None, :]
        decay = jnp.exp(jnp.where(causal, diff, -jnp.inf))
        att = jnp.einsum('bthk,bshk,btshk->bhts', qc, kc, decay)
        o_intra = jnp.einsum('bhts,bshv->bthv', att, vc)
        b_last = b[:, -1]
        k_dec = kc * jnp.exp(b_last[:, None] - b)
        s_new = jnp.exp(b_last)[..., None] * s + jnp.einsum('bshk,bshv->bhkv', k_dec, vc)
        return s_new, o_inter + o_intra

    xs = (to_chunks(q), to_chunks(k), to_chunks(v), to_chunks(log_a))
    s_fin, o = lax.scan(step, s0.astype(jnp.float32), xs)
    o = jnp.moveaxis(o, 0, 1).reshape(bsz, n_chunks * c, GLA_HEADS, GLA_DV)[:, :t_len]
    return o, s_fin


def causal_dwconv(u, buf, w, b):
    ext = jnp.concatenate([buf.astype(u.dtype), u], axis=1)
    out = lax.conv_general_dilated(ext, w[:, None, :].astype(u.dtype), window_strides=(1,), padding='VALID',
                                   dimension_numbers=('NWC', 'WIO', 'NWC'), feature_group_count=u.shape[-1])
    return out + b, ext[:, -(CONV_WIDTH - 1):]


def hybrid_layer(x, p, s0, conv_buf, g_pre, w_in, w_a_up, b_a_up, g_gla, w_a_out,
                 w_dw, b_dw, g_ln, b_ln, w_b_out, w_o, w_pe, w_pg):
    bsz, t_len, _ = x.shape
    h = rmsnorm(x, g_pre)
    proj = h @ w_in
    q, k, v, z_a, r, glu_in, z_b, g_a, g_b = jnp.split(proj, SPLIT_POINTS, axis=-1)
    q = q.reshape(bsz, t_len, GLA_HEADS, GLA_DK) * (GLA_DK ** -0.5)
    k = k.reshape(bsz, t_len, GLA_HEADS, GLA_DK)
    v = v.reshape(bsz, t_len, GLA_HEADS, GLA_DV)
    log_a = jax.nn.log_sigmoid((r @ w_a_up + b_a_up).astype(jnp.float32)) / GLA_TAU
    log_a = log_a.reshape(bsz, t_len, GLA_HEADS, GLA_DK)
    o, s_new = gla_chunked(q, k, v, log_a, s0)
    o = rmsnorm(o, g_gla).reshape(bsz, t_len, GLA_VAL_DIM).astype(x.dtype)
    y_a = (o * jax.nn.silu(z_a)) @ w_a_out
    glu_a, glu_g = jnp.split(glu_in, 2, axis=-1)
    u = glu_a * jax.nn.sigmoid(glu_g)
    cv, new_buf = causal_dwconv(u, conv_buf, w_dw, b_dw)
    cv = jax.nn.silu(layernorm(cv, g_ln, b_ln))
    y_b = (cv * jax.nn.silu(z_b)) @ w_b_out
    merged = jax.nn.sigmoid(g_a) * y_a + jax.nn.sigmoid(g_b) * y_b
    x = x + merged @ w_o
    x = x + jax.nn.sigmoid(x @ w_pg) * (p @ w_pe)
    return x, s_new, new_buf


def setup_inputs(seed: int = 0) -> dict:
    key = jax.random.key(seed)
    ks = jax.random.split(key, 24)
    f32 = jnp.float32
    nrm = lambda kk, shape, scale: jax.random.normal(kk, shape, f32) * scale
    return {
        'x_prompt': nrm(ks[0], (BATCH, SEQ, D_MODEL), 1.0),
        'x_sample': nrm(ks[1], (DEC_BATCH, DEC_SEQ, D_MODEL), 1.0),
        'state_gla': nrm(ks[2], (DEPTH, DEC_BATCH, GLA_HEADS, GLA_DK, GLA_DV), 1.0),
        'state_conv': nrm(ks[3], (DEPTH, DEC_BATCH, CONV_WIDTH - 1, CONV_DIM), 0.5),
        'p_prompt': nrm(ks[4], (DEPTH, BATCH, SEQ, PLE_DIM), 1.0),
        'p_sample': nrm(ks[5], (DEPTH, DEC_BATCH, DEC_SEQ, PLE_DIM), 1.0),
        'g_pre': 1.0 + nrm(ks[6], (DEPTH, D_MODEL), 0.02),
        'w_in': nrm(ks[7], (DEPTH, D_MODEL, D_IN), D_MODEL ** -0.5),
        'w_a_up': nrm(ks[8], (DEPTH, GLA_RANK, GLA_KEY_DIM), GLA_RANK ** -0.5),
        'b_a_up': nrm(ks[9], (DEPTH, GLA_KEY_DIM), 0.1),
        'g_gla': 1.0 + nrm(ks[10], (DEPTH, GLA_DV), 0.02),
        'w_a_out': nrm(ks[11], (DEPTH, GLA_VAL_DIM, D_MODEL), GLA_VAL_DIM ** -0.5),
        'w_dw': nrm(ks[12], (DEPTH, CONV_WIDTH, CONV_DIM), CONV_WIDTH ** -0.5),
        'b_dw': nrm(ks[13], (DEPTH, CONV_DIM), 0.02),
        'g_ln': 1.0 + nrm(ks[14], (DEPTH, CONV_DIM), 0.02),
        'b_ln': nrm(ks[15], (DEPTH, CONV_DIM), 0.02),
        'w_b_out': nrm(ks[16], (DEPTH, CONV_DIM, D_MODEL), CONV_DIM ** -0.5),
        'w_o': nrm(ks[17], (DEPTH, D_MODEL, D_MODEL), D_MODEL ** -0.5),
        'w_pe': nrm(ks[18], (DEPTH, PLE_DIM, D_MODEL), PLE_DIM ** -0.5),
        'w_pg': nrm(ks[19], (DEPTH, D_MODEL, D_MODEL), D_MODEL ** -0.5),
        'g_final': 1.0 + nrm(ks[20], (D_MODEL,), 0.02),
    }


def reference(x_prompt, x_sample, state_gla, state_conv, p_prompt, p_sample, g_pre, w_in, w_a_up, b_a_up,
              g_gla, w_a_out, w_dw, b_dw, g_ln, b_ln, w_b_out, w_o, w_pe, w_pg, g_final):
    bp = x_prompt.shape[0]
    xp, xs = x_prompt, x_sample
    gla_p, conv_p, gla_s, conv_s = [], [], [], []
    for i in range(DEPTH):
        lw = (g_pre[i], w_in[i], w_a_up[i], b_a_up[i], g_gla[i], w_a_out[i], w_dw[i], b_dw[i],
              g_ln[i], b_ln[i], w_b_out[i], w_o[i], w_pe[i], w_pg[i])
        s0_p = jnp.zeros((bp, GLA_HEADS, GLA_DK, GLA_DV), jnp.float32)
        buf_p = jnp.zeros((bp, CONV_WIDTH - 1, CONV_DIM), xp.dtype)
        xp, sp, bfp = hybrid_layer(xp, p_prompt[i], s0_p, buf_p, *lw)
        xs, ss, bfs = hybrid_layer(xs, p_sample[i], state_gla[i], state_conv[i], *lw)
        gla_p.append(sp.astype(state_gla.dtype))
        conv_p.append(bfp.astype(state_conv.dtype))
        gla_s.append(ss.astype(state_gla.dtype))
        conv_s.append(bfs.astype(state_conv.dtype))
    y_prompt = rmsnorm(xp, g_final)
    y_sample = rmsnorm(xs, g_final)
    gla_state_prompt = jnp.stack(gla_p)
    conv_state_prompt = jnp.stack(conv_p)
    gla_state_sample = jnp.stack(gla_s)
    conv_state_sample = jnp.stack(conv_s)
    return (y_prompt, y_sample, gla_state_prompt, conv_state_prompt, gla_state_sample, conv_state_sample)
```

```python
import numpy as np
from contextlib import ExitStack
import concourse.bass as bass
import concourse.mybir as mybir
from concourse.bass_utils import run_bass_kernel_spmd

F32 = mybir.dt.float32
BF16 = mybir.dt.bfloat16
AF = mybir.ActivationFunctionType
ALU = mybir.AluOpType

NCORES = 8
D = 1024
SEQ = 2048
NSEQ_S = 16
TS_S = 64
EPS = 1e-6
NSLAB_W = 25
SLAB_E = 4096
DG_E = 31 * 128
NB = 3


class Buf:
    __slots__ = ("name", "w", "r", "sem", "cnt", "const", "subs")

    def __init__(self, name, const=False):
        self.name = name
        self.w = None
        self.r = {}
        self.sem = None
        self.cnt = 0
        self.const = const
        self.subs = {}


class Eng:
    def __init__(self, name, h, sem):
        self.name = name
        self.h = h
        self.sem = sem
        self.cnt = 0
        self.seen = {}


class KB:
    def __init__(self, nc, es):
        self.nc = nc
        self.es = es
        self.nsem = 0
        self.pe = Eng("pe", nc.tensor, self.new_sem("pe"))
        self.act = Eng("act", nc.scalar, self.new_sem("act"))
        self.dve = Eng("dve", nc.vector, self.new_sem("dve"))
        self.pool = Eng("pool", nc.gpsimd, self.new_sem("pool"))
        self.sp = Eng("sp", nc.sync, self.new_sem("sp"))
        self.engs = [self.pe, self.act, self.dve, self.pool, self.sp]
        self.dma_holders = []

    def new_sem(self, name):
        self.nsem += 1
        return self.es.enter_context(self.nc.semaphore(f"s{self.nsem}_{name}"))

    def _waits(self, eng, reads, writes):
        deps = {}

        def add(d):
            if d is None:
                return
            s, v = d
            k = id(s)
            if k not in deps or deps[k][1] < v:
                deps[k] = (s, v)

        for b in reads:
            add(b.w)
        for b in writes:
            add(b.w)
            for d in b.r.values():
                add(d)
        for k, (s, v) in deps.items():
            if eng is self.pe and s is self.pe.sem:
                continue
            if eng.seen.get(k, 0) >= v:
                continue
            eng.h.wait_ge(s, v)
            eng.seen[k] = v

    def _mark(self, d, reads, writes):
        k = id(d[0])
        for b in reads:
            if not b.const:
                b.r[k] = d
        for b in writes:
            b.w = d
            b.r = {}

    def op(self, eng, fn, reads=(), writes=()):
        self._waits(eng, reads, writes)
        ins = fn(eng.h)
        eng.cnt += 1
        ins.then_inc(eng.sem, 1)
        self._mark((eng.sem, eng.cnt), reads, writes)

    def mm(self, fns, reads=(), writes=()):
        eng = self.pe
        self._waits(eng, reads, writes)
        ins = None
        for f in fns:
            ins = f(eng.h)
        eng.cnt += 1
        ins.then_inc(eng.sem, 1)
        self._mark((eng.sem, eng.cnt), reads, writes)

    def dma(self, q, out, in_, reads, writes, holder):
        self._waits(q, reads, writes)
        if q.name not in holder.subs:
            holder.subs[q.name] = Buf(holder.name + "_" + q.name)
        holder = holder.subs[q.name]
        if holder.sem is None:
            holder.sem = self.new_sem("d_" + holder.name)
            self.dma_holders.append(holder)
        q.h.dma_start(out=out, in_=in_).then_inc(holder.sem, 16)
        holder.cnt += 16
        self._mark((holder.sem, holder.cnt), reads, writes)

    def barrier(self):
        for e in self.engs:
            for o in self.engs:
                if o is e or o.cnt == 0:
                    continue
                k = id(o.sem)
                if e.seen.get(k, 0) < o.cnt:
                    e.h.wait_ge(o.sem, o.cnt)
                    e.seen[k] = o.cnt
            for hld in self.dma_holders:
                k = id(hld.sem)
                if e.seen.get(k, 0) < hld.cnt:
                    e.h.wait_ge(hld.sem, hld.cnt)
                    e.seen[k] = hld.cnt


class TB:
    def __init__(self, t, bufs):
        self.t = t
        self.b = bufs


def build_program():
    nc = bass.Bass("TRN2", target_bir_lowering=False)
    dti = lambda n, s, d=F32: nc.dram_tensor(n, s, d, kind="ExternalInput").ap()
    dto = lambda n, s, d=F32: nc.dram_tensor(n, s, d, kind="ExternalOutput").ap()
    x_p = dti("x_p", [SEQ, D])
    x_s = dti("x_s", [TS_S, D])
    p_p = dti("p_p", [SEQ, 256])
    p_s = dti("p_s", [TS_S, 256])
    sgla = dti("sgla", [NSEQ_S, 4, 128, 256])
    sconv = dti("sconv", [NSEQ_S, 30, D])
    wslab = dti("wslab", [NSLAB_W, 128, SLAB_E])
    wr_d = dti("wr", [128, 8, 16])
    wup_d = dti("wup", [17, 512])
    cst_d = dti("cst", [128, 768])
    cstb_d = dti("cstb", [128, 2048])
    vec_d = dti("vec", [128, 2048 + 288])
    y_p = dto("y_p", [SEQ, D])
    y_s = dto("y_s", [TS_S, D])
    gsp = dto("gsp", [4, 128, 256])
    csp = dto("csp", [30, D])
    gss = dto("gss", [NSEQ_S, 4, 128, 256])
    css = dto("css", [NSEQ_S, 30, D])
    wdg = nc.dram_tensor("wdg", [8, 128, DG_E], BF16, kind="Internal").ap()
    wbf = nc.dram_tensor("wbf", [NSLAB_W, 128, SLAB_E], BF16, kind="Internal").ap()

    with ExitStack() as es:
        kb = KB(nc, es)
        pe, act, dve, pool, sp = kb.pe, kb.act, kb.dve, kb.pool, kb.sp
        cnt = [0]

        def sbt(es_, shape, dt, nb=1, const=False):
            cnt[0] += 1
            t = es_.enter_context(nc.sbuf_tensor(f"t{cnt[0]}", shape, dt))
            return TB(t, [Buf(f"t{cnt[0]}_{i}", const) for i in range(nb)])

        cst = sbt(es, [128, 768], F32)
        cstb = sbt(es, [128, 2048], BF16)
        vec = sbt(es, [128, 2048 + 288], F32)
        wr = sbt(es, [128, 8, 16], BF16)
        wup = sbt(es, [32, 512], BF16)
        slabs = [sbt(es, [128, SLAB_E], BF16) for _ in range(NB)]
        S = sbt(es, [128, 4, 256], F32)
        Sb = sbt(es, [128, 4, 256], BF16)
        small = sbt(es, [128, 64], F32, nb=16)
        junk = sbt(es, [128, 1024], BF16)
        u2T_s = sbt(es, [128, 8, 544], BF16, 8)
        xt_s = sbt(es, [128, 1, D], F32, 1)
        hT_s = sbt(es, [128, 8, TS_S], BF16, 1)
        rT_s = sbt(es, [32, TS_S], BF16, 1)
        Lt_s = sbt(es, [128, 1, 512], F32, 1)
        Eb_s = sbt(es, [128, 4, TS_S], F32, 4)
        Enb_s = sbt(es, [128, 4, TS_S], F32, 4)
        ED_s = sbt(es, [128, 512], F32, 1)
        PS = []
        for i in range(7):
            t = es.enter_context(nc.psum_tensor(f"ps{i}", [128, 512], F32))
            PS.append(TB(t, [Buf(f"ps{i}")]))
        PTt = es.enter_context(nc.psum_tensor("pst", [128, 8, 128], BF16))
        PT = TB(PTt, [Buf("pst")])
        free_ps = list(range(7))

        def acq():
            return PS[free_ps.pop(0)]

        def rel(p):
            free_ps.append(PS.index(p))

        ident_f = cst.t[:, 0:128]
        triC = cst.t[:, 128:256]
        triU = cst.t[:, 256:384]
        triC_s = cst.t[:, 384:448]
        triU_s = cst.t[:, 448:512]
        ones_f = cst.t[:, 512:640]
        neghalf = cst.t[:, 640:704]
        rowmask = cst.t[:, 704:720]
        epsb = cst.t[:, 720:721]
        ident_b = cstb.t[:, 0:128]
        ones_b = cstb.t[:, 128:256]
        mask4 = cstb.t[:, 256:768]
        mask4_s = cstb.t[:, 768:1024]
        colmask = cstb.t[:, 1024:2048]
        gpre_b = vec.t[:, 0:1024]
        gfin_b = vec.t[:, 1024:2048]
        bdwT = vec.t[:, 2048:2056]
        glnT = vec.t[:, 2056:2064]
        blnT = vec.t[:, 2064:2072]
        ggla = vec.t[:, 2072:2074]
        wdwT = vec.t[:, 2080:2080 + 248]
        CB = [cst.b[0], cstb.b[0], vec.b[0]]

        kb.dma(sp, cst.t[:], cst_d, [], cst.b, cst.b[0])
        kb.dma(sp, vec.t[:], vec_d, [], vec.b, vec.b[0])
        kb.dma(pool, cstb.t[:], cstb_d, [], cstb.b, cstb.b[0])
        kb.dma(pool, wr.t[:], wr_d, [], wr.b, wr.b[0])
        kb.op(dve, lambda h: h.memset(wup.t[:], 1.0), [], wup.b)
        kb.dma(pool, wup.t[0:17, :], wup_d, [], wup.b, wup.b[0])
        for b_ in (cst.b[0], cstb.b[0], vec.b[0], wr.b[0]):
            b_.const = True
        kb.op(dve, lambda h: h.memset(S.t[:], 0.0), [], S.b)
        kb.op(dve, lambda h: h.memset(rT_s.t[:], 1.0), [], rT_s.b)
        kb.op(dve, lambda h: h.memset(Sb.t[:], 0.0), [], Sb.b)

        slab_seq = []
        state = {"issued": 0, "cur": -1}

        castb = [Buf(f"cast{i}") for i in range(NSLAB_W)]

        def tile_slab_list(diagb, first=False):
            l = []
            if not first:
                for i in range(10):
                    l.append((wbf[i], SLAB_E, castb[i], -1))
                for c in range(8):
                    l.append((wdg[c], DG_E, diagb[c], -1))
                for i in range(10, 25):
                    l.append((wbf[i], SLAB_E, castb[i], -1))
                return l
            for i in range(10):
                l.append((wslab[i], SLAB_E, None, i))
            for c in range(8):
                l.append((wdg[c], DG_E, diagb[c], -1))
            for i in range(10, 25):
                l.append((wslab[i], SLAB_E, None, i))
            return l

        def get_slab(live=1):
            state["cur"] += 1
            i = state["cur"]
            issue_slabs(min(i + NB - live + 1, len(slab_seq)))
            return slabs[i % NB]

        def issue_slabs(upto):
            while state["issued"] < upto:
                k = state["issued"]
                src, ne, sb_, wi = slab_seq[k]
                sl = slabs[k % NB]
                if sb_ is None:
                    kb.dma(pool, sl.t[:, 0:ne], src, [], sl.b, sl.b[0])
                    kb.dma(sp, wbf[wi], sl.t[:, 0:ne], sl.b, [castb[wi]], castb[wi])
                else:
                    kb.dma(sp, sl.t[:, 0:ne], src, [sb_], sl.b, sl.b[0])
                state["issued"] += 1

        diagb = [Buf(f"diag{c}") for c in range(8)]
        pending_builds = []
        deferred = []

        def hook():
            if deferred:
                deferred.pop(0)()
        for t_ in range(5):
            slab_seq.extend(tile_slab_list(diagb, first=(t_ == 0)))

        def smallv(i, n=4):
            return small.t[:, 4 * i:4 * i + n]

        def tile_pro_load(B, T, TS, NS, xsrc, tok0, xt):
            kb.dma(pool, xt.t[0:TS, :, :], xsrc[tok0:tok0 + T, :].rearrange("(j p) d -> p j d", p=TS), [], xt.b, xt.b[0])

        def make_pro(B, T, TS, NS, xt):
            col = lambda j: slice(j * TS, (j + 1) * TS)
            hT, hbs = B["hT"], B["hb"]
            ssb, rsb = small.b[0], small.b[1]

            def stats():
                for j in range(NS):
                    hbj = hbs[j % len(hbs)]
                    kb.op(dve, lambda h, j=j, hbj=hbj: h.scalar_tensor_tensor(out=hbj.t[0:TS, :], in0=xt.t[0:TS, j, :], scalar=1.0, in1=xt.t[0:TS, j, :], op0=ALU.mult, op1=ALU.mult, accum_out=smallv(0)[0:TS, j:j + 1]),
                          [xt.b[j]], hbj.b + [ssb])
                kb.op(pool, lambda h: h.tensor_scalar(out=smallv(1)[0:TS, 0:NS], in0=smallv(0)[0:TS, 0:NS], scalar1=1.0 / D, scalar2=EPS, op0=ALU.mult, op1=ALU.add), [ssb], [rsb])
                kb.op(pool, lambda h: h.tensor_tensor(out=smallv(1)[0:TS, 0:NS], in0=smallv(1)[0:TS, 0:NS], in1=neghalf[0:TS, 0:NS], op=ALU.pow), [rsb, cst.b[0]], [rsb])

            def hcomp(j):
                hb = hbs[j % len(hbs)]
                kb.op(dve, lambda h: h.scalar_tensor_tensor(out=hb.t[0:TS, :], in0=xt.t[0:TS, j, :], scalar=smallv(1)[0:TS, j:j + 1], in1=gpre_b[0:TS, :], op0=ALU.mult, op1=ALU.mult),
                      [xt.b[j], rsb, vec.b[0]], hb.b)

            def tr(j):
                hb = hbs[j % len(hbs)]
                kb.mm([lambda h, kc=kc: h.transpose(out=PT.t[:, kc, 0:TS], in_=hb.t[0:TS, kc * 128:(kc + 1) * 128], identity=ident_b[0:TS, 0:TS]) for kc in range(8)],
                      hb.b + [cstb.b[0]], PT.b)
                kb.op(act, lambda h: h.activation(out=hT.t[:, :, col(j)], in_=PT.t[:, :, 0:TS], func=AF.Copy), PT.b, hT.b)
            return stats, hcomp, tr

        def tile_pro_compute(B, T, TS, NS, xt):
            st_, hc_, tr_ = make_pro(B, T, TS, NS, xt)
            st_()
            for j in range(NS):
                hc_(j)
                tr_(j)

        def sample_prechain():
            n = TS_S
            P = acq()
            kb.mm([lambda h, kc=kc, P=P: h.matmul(P.t[0:16, 0:n], lhsT=wr.t[:, kc, :], rhs=hT_s.t[:, kc, :], start=(kc == 0), stop=(kc == 7)) for kc in range(8)], hT_s.b + wr.b, P.b)
            kb.op(act, lambda h, P=P: h.activation(out=rT_s.t[0:16, 0:n], in_=P.t[0:16, 0:n], func=AF.Copy), P.b, rT_s.b)
            rel(P)
            P = acq()
            kb.mm([lambda h, P=P: h.matmul(P.t[0:n, 0:512], lhsT=rT_s.t[0:17, 0:n], rhs=wup.t[0:17, :], start=True, stop=True)], rT_s.b + wup.b, P.b)
            kb.op(act, lambda h, P=P: h.activation(out=Lt_s.t[0:n, 0, :], in_=P.t[0:n, 0:512], func=AF.Exp, scale=-1.0), P.b, Lt_s.b)
            rel(P)
            kb.op(act, lambda h: h.activation(out=Lt_s.t[0:n, 0, :], in_=Lt_s.t[0:n, 0, :], func=AF.Ln, bias=1.0), Lt_s.b, Lt_s.b)
            for hd in range(4):
                P = acq()
                kb.mm([lambda h, P=P, hd=hd: h.matmul(P.t[:, 0:n], lhsT=Lt_s.t[0:n, 0, hd * 128:(hd + 1) * 128], rhs=triC_s[0:n, 0:n], start=True, stop=True)], Lt_s.b + [cst.b[0]], P.b)
                kb.op(act, lambda h, P=P, hd=hd: h.activation(out=Eb_s.t[:, hd, :], in_=P.t[:, 0:n], func=AF.Exp), P.b, [Eb_s.b[hd]])
                kb.op(act, lambda h, P=P, hd=hd: h.activation(out=Enb_s.t[:, hd, :], in_=P.t[:, 0:n], func=AF.Exp, scale=-1.0), P.b, [Enb_s.b[hd]])
                rel(P)
            P = acq()
            kb.mm([lambda h, P=P: h.matmul(P.t[0:n, 0:512], lhsT=triU_s[0:n, 0:n], rhs=Lt_s.t[0:n, 0, :], start=True, stop=True)], Lt_s.b + [cst.b[0]], P.b)
            kb.op(act, lambda h, P=P: h.activation(out=ED_s.t[0:n, :], in_=P.t[0:n, 0:512], func=AF.Exp), P.b, ED_s.b)
            rel(P)

        def run_tile(es_t, B, T, TS, NS, sample, xsrc, psrc, ydst, tok0, last, xt, next_pro, pre=False):
            col = lambda j: slice(j * TS, (j + 1) * TS)
            pt, hT, hbs, rT, Lt, Eb = B["pt"], B["hT"], B["hb"], B["rT"], B["Lt"], B["Eb"]
            FT, x1T, sgs = B["FT"], B["x1T"], B["sgs"]
            ftc = [0]

            def ft():
                ftc[0] += 1
                k = ftc[0] % 4
                return TB(FT.t[:, k, :], [FT.b[k]])
            kdec, ktT, qtT, vt, zaT, attm, on, ogT = B["kdec"], B["ktT"], B["qtT"], B["vt"], B["zaT"], B["attm"], B["on"], B["ogT"]
            u2T, cv, sqs, meanb, rstdb = B["u2T"], B["cv"], B["sq"], B["meanb"], B["rstdb"]
            zbT, sls, gaT, gbT, mTb, x1bs, pb, pT, uf, ut = B["zbT"], B["sl"], B["gaT"], B["gbT"], B["mTb"], B["hb"], B["pb"], B["pT"], B["uf"], B["ut"]
            tc_ = triC_s if sample else triC
            tu_ = triU_s if sample else triU
            rot = {"sq": 0, "sl": 0, "x1b": 0, "sg": 0}

            def nxt(name, lst):
                rot[name] += 1
                return lst[rot[name] % len(lst)]

            if next_pro is not None:
                next_pro[0]()
            kb.dma(pool, pt.t[0:TS, :, :], psrc[tok0:tok0 + T, :].rearrange("(j p) d -> p j d", p=TS), [], pt.b, pt.b[0])

            def proj_fm(slab, nch, rhs, rhs_bufs, evac, lhs_off=0, hookb=False):
                for i in range(nch):
                    for _ in range(2):
                        if pending_builds:
                            pending_builds.pop(0)()
                    P = acq()
                    kb.mm([lambda h, kc=kc, i=i, P=P: h.matmul(P.t[:, 0:T], lhsT=slab.t[:, kc * 512 + lhs_off + i * 128: kc * 512 + lhs_off + (i + 1) * 128], rhs=rhs.t[:, kc, :], start=(kc == 0), stop=(kc == 7)) for kc in range(8)],
                          slab.b + rhs_bufs, P.b)
                    evac(i, P)
                    rel(P)

            def proj_tm(slab, lhs, lhs_bufs, j, evac):
                for _ in range(2):
                    if pending_builds:
                        pending_builds.pop(0)()
                P = acq()
                kb.mm([lambda h, kc=kc, P=P: h.matmul(P.t[0:TS, 0:512], lhsT=lhs.t[:, kc, col(j)], rhs=slab.t[:, kc * 512:(kc + 1) * 512], start=(kc == 0), stop=(kc == 7)) for kc in range(8)],
                      slab.b + lhs_bufs, P.b)
                evac(P)
                rel(P)

            vgroups = []
            vs_ = {}

            def vgroup(half, j):
                if j == 0:
                    vs_[half] = get_slab()
                slabV = vs_[half]
                proj_tm(slabV, hT, hT.b, j, lambda P: kb.op(act, lambda h: h.activation(out=vt.t[0:TS, j, half * 512:(half + 1) * 512], in_=P.t[0:TS, 0:512], func=AF.Copy), P.b, [vt.b[2 * j + half]]))
            for half in range(2):
                for j in range(NS):
                    vgroups.append(lambda half=half, j=j: vgroup(half, j))

            def vfill(n):
                while n > 0 and vgroups:
                    vgroups.pop(0)()
                    n -= 1
            if not pre:
                P = acq()
                kb.mm([lambda h, kc=kc, P=P: h.matmul(P.t[0:16, 0:T], lhsT=wr.t[:, kc, :], rhs=hT.t[:, kc, :], start=(kc == 0), stop=(kc == 7)) for kc in range(8)], hT.b + wr.b, P.b)
                kb.op(act, lambda h, P=P: h.activation(out=rT.t[0:16, 0:T], in_=P.t[0:16, 0:T], func=AF.Copy), P.b, rT.b)
                rel(P)
                vfill(2)
                for j in range(NS):
                    P = acq()
                    kb.mm([lambda h, P=P, j=j: h.matmul(P.t[0:TS, 0:512], lhsT=rT.t[0:17, col(j)], rhs=wup.t[0:17, :], start=True, stop=True)], rT.b + wup.b, P.b)
                    kb.op(act, lambda h, P=P, j=j: h.activation(out=Lt.t[0:TS, j, :], in_=P.t[0:TS, 0:512], func=AF.Exp, scale=-1.0), P.b, [Lt.b[j]])
                    rel(P)
                    kb.op(act, lambda h, j=j: h.activation(out=Lt.t[0:TS, j, :], in_=Lt.t[0:TS, j, :], func=AF.Ln, bias=1.0), [Lt.b[j]], [Lt.b[j]])
            vfill(100)
            slabK = get_slab()

            def p_copy(j):
                kb.op(act, lambda h: h.activation(out=pb.t[0:TS, :], in_=pt.t[0:TS, j, :], func=AF.Copy), [pt.b[0]], pb.b)

            def p_tr(j):
                kb.mm([lambda h, kc=kc: h.transpose(out=PT.t[:, kc, 0:TS], in_=pb.t[0:TS, kc * 128:(kc + 1) * 128], identity=ident_b[0:TS, 0:TS]) for kc in range(2)], pb.b + [cstb.b[0]], PT.b)
                kb.op(dve, lambda h: h.tensor_copy(out=pT.t[:, :, col(j)], in_=PT.t[:, 0:2, 0:TS]), PT.b, pT.b)
            for hd in range(4):
                if pre:
                    enb = TB(Enb_s.t[:, hd, :], [Enb_s.b[hd]])
                else:
                    P = acq()
                    kb.mm([lambda h, P=P, j=j, hd=hd: h.matmul(P.t[:, col(j)], lhsT=Lt.t[0:TS, j, hd * 128:(hd + 1) * 128], rhs=tc_[0:TS, 0:TS], start=True, stop=True) for j in range(NS)],
                          Lt.b[0:NS] + [cst.b[0]], P.b)
                    enb = ft()
                    kb.op(act, lambda h, P=P, hd=hd: h.activation(out=Eb.t[:, hd, :], in_=P.t[:, 0:T], func=AF.Exp), P.b, [Eb.b[hd]])
                    kb.op(act, lambda h, P=P, enb=enb: h.activation(out=enb.t[:, 0:T], in_=P.t[:, 0:T], func=AF.Exp, scale=-1.0), P.b, enb.b)
                    rel(P)
                P = acq()
                kb.mm([lambda h, kc=kc, hd=hd, P=P: h.matmul(P.t[:, 0:T], lhsT=slabK.t[:, kc * 512 + hd * 128: kc * 512 + (hd + 1) * 128], rhs=hT.t[:, kc, :], start=(kc == 0), stop=(kc == 7)) for kc in range(8)],
                      slabK.b + hT.b, P.b)
                kb.op(dve, lambda h, P=P, hd=hd, enb=enb: h.tensor_tensor(out=ktT.t[:, hd, :], in0=P.t[:, 0:T], in1=enb.t[:, 0:T], op=ALU.mult), P.b + enb.b, [ktT.b[hd]])
                rel(P)
            if sample:
                for j in range(NS):
                    if pre:
                        ed = ED_s
                    else:
                        P = acq()
                        kb.mm([lambda h, P=P, j=j: h.matmul(P.t[0:TS, 0:512], lhsT=tu_[0:TS, 0:TS], rhs=Lt.t[0:TS, j, :], start=True, stop=True)], [Lt.b[j], cst.b[0]], P.b)
                        ed = ft()
                        kb.op(act, lambda h, P=P, ed=ed: h.activation(out=ed.t[0:TS, :], in_=P.t[0:TS, 0:512], func=AF.Exp), P.b, ed.b)
                        rel(P)
                    proj_tm(slabK, hT, hT.b, j, lambda P, j=j, ed=ed: kb.op(dve, lambda h: h.tensor_tensor(out=kdec.t[0:TS, j, :], in0=P.t[0:TS, 0:512], in1=ed.t[0:TS, :], op=ALU.mult), P.b + ed.b, [kdec.b[j]]))
            slabQ = get_slab()
            proj_fm(slabQ, 4, hT, hT.b, lambda i, P: kb.op(dve, lambda h: h.scalar_tensor_tensor(out=qtT.t[:, i, :], in0=P.t[:, 0:T], scalar=float(128 ** -0.5), in1=Eb.t[:, i, :], op0=ALU.mult, op1=ALU.mult), P.b + [Eb.b[i]], [qtT.b[i]]))
            if not sample:
                def kd_rescale(j):
                    kd = sgs[j % 2]
                    for hd in range(4):
                        kb.op(dve, lambda h, hd=hd, kd=kd: h.tensor_scalar(out=kd.t[:, hd * TS:(hd + 1) * TS], in0=ktT.t[:, hd, col(j)], scalar1=Eb.t[:, hd, j * TS + TS - 1:j * TS + TS], scalar2=None, op0=ALU.mult),
                              [ktT.b[hd], Eb.b[hd]], kd.b)

                def kd_tr(j):
                    kd = sgs[j % 2]
                    kb.mm([lambda h, hd=hd: h.transpose(out=PT.t[:, hd, 0:TS], in_=kd.t[:, hd * TS:(hd + 1) * TS], identity=ident_b) for hd in range(4)], kd.b + [cstb.b[0]], PT.b)
                    kb.op(act, lambda h: h.activation(out=kdec.t[0:TS, j, :].rearrange("p (a b) -> p a b", a=4), in_=PT.t[:, 0:4, 0:TS], func=AF.Copy), PT.b, [kdec.b[j]])
                kd_sched = [lambda: (kd_rescale(0), kd_rescale(1)), lambda: (kd_tr(0), kd_rescale(2)), lambda: (kd_tr(1), kd_rescale(3)), lambda: kd_tr(2), lambda: kd_tr(3)]
            else:
                kd_sched = []
            for half in range(2):
                sl_ = get_slab()
                for i2 in range(2):
                    if kd_sched:
                        kd_sched.pop(0)()
                    proj_fm(sl_, 2, hT, hT.b, lambda i, P, half=half, i2=i2: kb.op(act, lambda h: h.activation(out=zaT.t[:, half * 4 + i2 * 2 + i, :], in_=P.t[:, 0:T], func=AF.Silu), P.b, [zaT.b[half * 4 + i2 * 2 + i]]), lhs_off=i2 * 256, hookb=True)
            while kd_sched:
                kd_sched.pop(0)()

            if sample:
                qpad, kdpad, S0f, S0ball, Sout = B["qpad"], B["kdpad"], B["S0f"], B["S0b_all"], B["Sout"]
                for hd in range(4):
                    kb.op(dve, lambda h, hd=hd: h.tensor_tensor(out=qpad.t[:, hd, :, :], in0=qtT.t[:, hd, 0:64].unsqueeze(1).broadcast_to([128, 16, 64]), in1=colmask.rearrange("p (a b) -> p a b", a=16), op=ALU.mult),
                          [qtT.b[hd], cstb.b[0]], [qpad.b[hd]])
                for sq_ in range(NSEQ_S):
                    kb.op(act, lambda h, sq_=sq_: h.activation(out=kdpad.t[0:64, sq_, :], in_=kdec.t[0:64, 0, :], func=AF.Copy, scale=rowmask[0:64, sq_:sq_ + 1]), [kdec.b[0], cst.b[0]], [kdpad.b[sq_]])
            if pending_builds or (tok0 == 0 and not sample):
                while pending_builds:
                    pending_builds.pop(0)()
                if tok0 == 0 and not sample:
                    kb.op(dve, lambda h: h.memset(u2T.t[:], 0.0), [], u2T.b)
            sso, rso = small.b[2], small.b[3]
            fillers = []
            if sample:
                uview = lambda c, a, b_: u2T.t[:, c, a * 16:b_ * 16]

            def ev_u(c, P, sgx):
                if sample:
                    kb.op(dve, lambda h: h.scalar_tensor_tensor(out=uview(c, 30, 34), in0=sgx.t[:, 0:64], scalar=1.0, in1=P.t[:, 0:64], op0=ALU.add, op1=ALU.mult), P.b + sgx.b, [u2T.b[c]])
                    kb.op(dve, lambda h: h.scalar_tensor_tensor(out=uf.t[:, c, 0:64], in0=sgx.t[:, 0:64], scalar=1.0, in1=P.t[:, 0:64], op0=ALU.add, op1=ALU.mult), P.b + sgx.b, [uf.b[c]])
                else:
                    kb.op(dve, lambda h: h.scalar_tensor_tensor(out=u2T.t[:, c, 30:30 + T], in0=sgx.t[:, 0:T], scalar=1.0, in1=P.t[:, 0:T], op0=ALU.add, op1=ALU.mult), P.b + sgx.b, [u2T.b[c]])
                    if last:
                        kb.op(dve, lambda h: h.scalar_tensor_tensor(out=uf.t[:, c, 0:32], in0=sgx.t[:, T - 32:T], scalar=1.0, in1=P.t[:, T - 32:T], op0=ALU.add, op1=ALU.mult), P.b + sgx.b, [uf.b[c]])

            hold = {}

            def glu_filler(half, i):
                ii = i % 2
                if ii == 0:
                    hold["s"] = get_slab()
                slg = sla = hold["s"]
                P = acq()
                kb.mm([lambda h, kc=kc, P=P: h.matmul(P.t[:, 0:T], lhsT=slg.t[:, kc * 512 + ii * 128: kc * 512 + (ii + 1) * 128], rhs=hT.t[:, kc, :], start=(kc == 0), stop=(kc == 7)) for kc in range(8)], slg.b + hT.b, P.b)
                sgx = nxt("sg", sgs)
                kb.op(act, lambda h, P=P: h.activation(out=sgx.t[:, 0:T], in_=P.t[:, 0:T], func=AF.Tanh, scale=0.5), P.b, sgx.b)
                rel(P)
                P = acq()
                kb.mm([lambda h, kc=kc, P=P: h.matmul(P.t[:, 0:T], lhsT=sla.t[:, kc * 512 + (2 + ii) * 128: kc * 512 + (3 + ii) * 128], rhs=hT.t[:, kc, :], start=(kc == 0), stop=(kc == 7)) for kc in range(8)], sla.b + hT.b, P.b)
                ev_u(half * 4 + i, P, sgx)
                rel(P)

            cst_ = {}

            def conv_filler(c):
                if c == 0:
                    cst_["Pm"], cst_["Pq"] = acq(), acq()
                Pm, Pq = cst_["Pm"], cst_["Pq"]
                dg = get_slab()
                P = acq()
                if sample:
                    fns = [lambda h, tap=tap, P=P, dg=dg, c=c: h.matmul(P.t[:, 0:64], lhsT=dg.t[:, tap * 128:(tap + 1) * 128], rhs=uview(c, tap, tap + 4), start=(tap == 0), stop=(tap == 30)) for tap in range(31)]
                else:
                    fns = [lambda h, tap=tap, P=P, dg=dg, c=c: h.matmul(P.t[:, 0:T], lhsT=dg.t[:, tap * 128:(tap + 1) * 128], rhs=u2T.t[:, c, tap:tap + T], start=(tap == 0), stop=(tap == 30)) for tap in range(31)]
                kb.mm(fns, dg.b + [u2T.b[c]], P.b)
                kb.op(act, lambda h, P=P, c=c: h.activation(out=cv.t[:, c, :], in_=P.t[:, 0:T], func=AF.Identity, bias=bdwT[:, c:c + 1]), P.b + [vec.b[0]], [cv.b[c]])
                sq = nxt("sq", sqs)
                kb.op(act, lambda h, P=P, c=c, sq=sq: h.activation(out=sq.t[:, 0:T], in_=P.t[:, 0:T], func=AF.Square, bias=bdwT[:, c:c + 1]), P.b + [vec.b[0]], sq.b)
                rel(P)
                if "pend" in cst_:
                    cst_.pop("pend")()

                def stat_mm(c=c, sq=sq):
                    kb.mm([lambda h: h.matmul(Pm.t[:, 0:T], lhsT=ones_b, rhs=cv.t[:, c, :], start=(c == 0), stop=(c == 7))], [cv.b[c], cstb.b[0]], Pm.b)
                    kb.mm([lambda h: h.matmul(Pq.t[:, 0:T], lhsT=ones_b, rhs=sq.t[:, 0:T], start=(c == 0), stop=(c == 7))], sq.b + [cstb.b[0]], Pq.b)
                cst_["pend"] = stat_mm
                if not sample and not last:
                    kb.op(act, lambda h, c=c: h.activation(out=u2T.t[:, c, 0:30], in_=u2T.t[:, c, T:T + 30], func=AF.Copy), [u2T.b[c]], [u2T.b[c]])

            for half in range(2):
                for i in range(4):
                    fillers.append(lambda half=half, i=i: glu_filler(half, i))
            for c in range(8):
                fillers.append(lambda c=c: conv_filler(c))

            sfillers = []

            def sfill(n):
                while n > 0 and sfillers:
                    sfillers.pop(0)()
                    n -= 1

            def fill(n, limit=0):
                while n > 0 and len(fillers) > limit:
                    fillers.pop(0)()
                    n -= 1

            for j in range(NS):
                Pa = acq()
                kb.mm([lambda h, hd=hd, Pa=Pa, j=j: h.matmul(Pa.t[0:TS, hd * TS:(hd + 1) * TS], lhsT=ktT.t[:, hd, col(j)], rhs=qtT.t[:, hd, col(j)], start=True, stop=True) for hd in range(4)],
                      ktT.b[0:4] + qtT.b[0:4], Pa.b)
                mk_ = mask4_s[0:64, :] if sample else mask4
                kb.op(dve, lambda h, Pa=Pa: h.tensor_tensor(out=attm.t[0:TS, 0:4 * TS], in0=Pa.t[0:TS, 0:4 * TS], in1=mk_[0:TS, 0:4 * TS], op=ALU.mult), Pa.b + [cstb.b[0]], attm.b)
                rel(Pa)
                fill(1)
                if not sample:
                    Po = [acq(), acq()]
                    oview = lambda hd: Po[hd // 2].t[0:TS, (hd % 2) * 256:(hd % 2 + 1) * 256]
                    obuf = lambda hd: Po[hd // 2].b
                    for hd in range(4):
                        kb.mm([lambda h, hd=hd: h.matmul(oview(hd), lhsT=qtT.t[:, hd, col(j)], rhs=Sb.t[:, hd, :], start=True, stop=False),
                               lambda h, hd=hd: h.matmul(oview(hd), lhsT=attm.t[0:TS, hd * TS:(hd + 1) * TS], rhs=vt.t[0:TS, j, hd * 256:(hd + 1) * 256], start=False, stop=True)],
                              [qtT.b[hd]] + Sb.b + attm.b + [vt.b[2 * j], vt.b[2 * j + 1]], obuf(hd))
                    Pd = [acq(), acq()]
                    for hd in range(4):
                        kb.mm([lambda h, hd=hd: h.matmul(Pd[hd // 2].t[:, (hd % 2) * 256:(hd % 2 + 1) * 256], lhsT=kdec.t[0:TS, j, hd * 128:(hd + 1) * 128], rhs=vt.t[0:TS, j, hd * 256:(hd + 1) * 256], start=True, stop=True)],
                              [kdec.b[j], vt.b[2 * j], vt.b[2 * j + 1]], Pd[hd // 2].b)
                else:
                    Po = [acq(), acq(), acq(), acq()]
                    oview = lambda hd: Po[hd].t[0:TS, 0:256]
                    obuf = lambda hd: Po[hd].b

                    def ld_s0(q_):
                        rr = q_ % len(S0f)
                        kb.dma(pool, S0f[rr].t[:], sgla[q_].rearrange("h k d -> k h d"), [], S0f[rr].b, S0f[rr].b[0])
                    for q_ in range(len(S0f)):
                        ld_s0(q_)
                    for sq_ in range(NSEQ_S):
                        kb.mm([lambda h, hd=hd, sq_=sq_: h.matmul(oview(hd), lhsT=qpad.t[:, hd, sq_, :], rhs=S0ball.t[:, sq_, hd * 256:(hd + 1) * 256], start=(sq_ == 0), stop=False) for hd in range(4)],
                              qpad.b + [S0ball.b[sq_]], [Po[0].b[0], Po[1].b[0], Po[2].b[0], Po[3].b[0]])

                        def supd(sq_=sq_):
                            r_ = sq_ % len(S0f)
                            Pd = [acq(), acq()]
                            for hd in range(4):
                                kb.mm([lambda h, hd=hd: h.matmul(Pd[hd // 2].t[:, (hd % 2) * 256:(hd % 2 + 1) * 256], lhsT=kdpad.t[0:64, sq_, hd * 128:(hd + 1) * 128], rhs=vt.t[0:64, 0, hd * 256:(hd + 1) * 256], start=True, stop=True)],
                                      [kdpad.b[sq_], vt.b[0], vt.b[1]], Pd[hd // 2].b)
                            so = Sout[sq_ % len(Sout)]
                            for hd in range(4):
                                kb.op(dve, lambda h, hd=hd: h.scalar_tensor_tensor(out=so.t[:, hd, :], in0=S0f[r_].t[:, hd, :], scalar=Eb.t[:, hd, 48 + sq_:49 + sq_], in1=Pd[hd // 2].t[:, (hd % 2) * 256:(hd % 2 + 1) * 256], op0=ALU.mult, op1=ALU.add),
                                      S0f[r_].b + [Eb.b[hd]] + Pd[hd // 2].b, so.b)
                            rel(Pd[0]); rel(Pd[1])
                            kb.dma(pool, gss[sq_].rearrange("h k d -> k h d"), so.t[:], so.b, [], B["soh"][sq_ % len(Sout)])
                            if sq_ + len(S0f) < NSEQ_S:
                                ld_s0(sq_ + len(S0f))
                        sfillers.append(supd)
                        if sq_ % 4 == 3:
                            fill(1)
                    for hd in range(4):
                        kb.mm([lambda h, hd=hd: h.matmul(oview(hd), lhsT=attm.t[0:TS, hd * TS:(hd + 1) * TS], rhs=vt.t[0:TS, 0, hd * 256:(hd + 1) * 256], start=False, stop=True)],
                              attm.b + [vt.b[0], vt.b[1]], obuf(hd))
                for hd in range(4):
                    kb.op(act, lambda h, hd=hd: h.activation(out=junk.t[0:TS, 0:256], in_=oview(hd), func=AF.Square, accum_out=smallv(2)[0:TS, hd:hd + 1]), obuf(hd), [junk.b[0], sso])
                kb.op(pool, lambda h: h.tensor_scalar(out=smallv(3)[0:TS, :], in0=smallv(2)[0:TS, :], scalar1=1.0 / 256, scalar2=EPS, op0=ALU.mult, op1=ALU.add), [sso], [rso])
                kb.op(pool, lambda h: h.tensor_tensor(out=smallv(3)[0:TS, :], in0=smallv(3)[0:TS, :], in1=neghalf[0:TS, 0:4], op=ALU.pow), [rso, cst.b[0]], [rso])
                for hd in range(4):
                    kb.op(act, lambda h, hd=hd: h.activation(out=on.t[0:TS, hd * 256:(hd + 1) * 256], in_=oview(hd), func=AF.Copy, scale=smallv(3)[0:TS, hd:hd + 1]), obuf(hd) + [rso], on.b)
                for p_ in Po:
                    rel(p_)
                if not sample:
                    for hd in range(4):
                        kb.op(dve, lambda h, hd=hd: h.scalar_tensor_tensor(out=S.t[:, hd, :], in0=S.t[:, hd, :], scalar=Eb.t[:, hd, j * TS + TS - 1:j * TS + TS], in1=Pd[hd // 2].t[:, (hd % 2) * 256:(hd % 2 + 1) * 256], op0=ALU.mult, op1=ALU.add),
                              S.b + [Eb.b[hd]] + Pd[hd // 2].b, S.b)
                    rel(Pd[0]); rel(Pd[1])
                    kb.op(act, lambda h: h.activation(out=Sb.t[:], in_=S.t[:], func=AF.Copy), S.b, Sb.b)
                fill(3)
                kb.mm([lambda h, kc=kc: h.transpose(out=PT.t[:, kc, 0:TS], in_=on.t[0:TS, kc * 128:(kc + 1) * 128], identity=ident_b[0:TS, 0:TS]) for kc in range(8)], on.b + [cstb.b[0]], PT.b)
                for c in range(2):
                    kb.op(dve, lambda h, c=c: h.scalar_tensor_tensor(out=ogT.t[:, c::2, col(j)], in0=PT.t[:, c::2, 0:TS], scalar=ggla[:, c:c + 1], in1=zaT.t[:, c::2, col(j)], op0=ALU.mult, op1=ALU.mult),
                          PT.b + [vec.b[0]] + zaT.b[c::2], ogT.b[c::2])
            while fillers:
                fill(2)
                sfill(1)
            hook()
            if last:
                kb.dma(pool, gsp.rearrange("h k d -> k h d"), S.t[:], S.b, [], Buf("gsph"))
            cst_.pop("pend")()
            Pm, Pq = cst_["Pm"], cst_["Pq"]
            kb.op(act, lambda h: h.activation(out=meanb.t[:, 0:T], in_=Pm.t[:, 0:T], func=AF.Copy, scale=1.0 / D), Pm.b, meanb.b)
            msq = ft()
            kb.op(dve, lambda h: h.tensor_tensor(out=msq.t[:, 0:T], in0=meanb.t[:, 0:T], in1=meanb.t[:, 0:T], op=ALU.mult), meanb.b, msq.b)
            kb.op(dve, lambda h: h.scalar_tensor_tensor(out=rstdb.t[:, 0:T], in0=Pq.t[:, 0:T], scalar=1.0 / D, in1=msq.t[:, 0:T], op0=ALU.mult, op1=ALU.subtract), Pq.b + msq.b, rstdb.b)
            rel(Pm); rel(Pq)
            kb.op(act, lambda h: h.activation(out=rstdb.t[:, 0:T], in_=rstdb.t[:, 0:T], func=AF.Ln, bias=epsb[:, 0:1]), rstdb.b + [cst.b[0]], rstdb.b)
            kb.op(act, lambda h: h.activation(out=rstdb.t[:, 0:T], in_=rstdb.t[:, 0:T], func=AF.Exp, scale=-0.5), rstdb.b, rstdb.b)
            sfill(1)
            for half in range(2):
                sl_ = get_slab()
                proj_fm(sl_, 4, hT, hT.b, lambda i, P, half=half: kb.op(act, lambda h: h.activation(out=zbT.t[:, half * 4 + i, :], in_=P.t[:, 0:T], func=AF.Silu), P.b, [zbT.b[half * 4 + i]]))
            def ln_ab(c):
                tmp = ft()
                kb.op(dve, lambda h: h.tensor_tensor(out=tmp.t[:, 0:T], in0=cv.t[:, c, :], in1=meanb.t[:, 0:T], op=ALU.subtract), [cv.b[c]] + meanb.b, tmp.b)
                kb.op(dve, lambda h: h.tensor_tensor(out=tmp.t[:, 0:T], in0=tmp.t[:, 0:T], in1=rstdb.t[:, 0:T], op=ALU.mult), tmp.b + rstdb.b, tmp.b)
                sl = sls[c % len(sls)]
                kb.op(act, lambda h: h.activation(out=sl.t[:, 0:T], in_=tmp.t[:, 0:T], func=AF.Silu, scale=glnT[:, c:c + 1], bias=blnT[:, c:c + 1]), tmp.b + [vec.b[0]], sl.b)

            def ln_c(c):
                sl = sls[c % len(sls)]
                kb.op(dve, lambda h: h.tensor_tensor(out=zbT.t[:, c, :], in0=sl.t[:, 0:T], in1=zbT.t[:, c, :], op=ALU.mult), sl.b + [zbT.b[c]], [zbT.b[c]])

            def gate_group(slab_, i, dst, di):
                P = acq()
                kb.mm([lambda h, kc=kc, P=P: h.matmul(P.t[:, 0:T], lhsT=slab_.t[:, kc * 512 + i * 128: kc * 512 + (i + 1) * 128], rhs=hT.t[:, kc, :], start=(kc == 0), stop=(kc == 7)) for kc in range(8)], slab_.b + hT.b, P.b)
                kb.op(act, lambda h, P=P: h.activation(out=dst.t[:, di, :], in_=P.t[:, 0:T], func=AF.Tanh, scale=0.5), P.b, [dst.b[di]])
                rel(P)

            sA0 = get_slab(1)
            sA1 = get_slab(2)
            for c in range(8):
                ln_ab(c)
                if c >= 1:
                    ln_c(c - 1)
                gate_group(sA0 if c < 4 else sA1, c % 4, gaT, c)
                if c % 4 == 3:
                    sfill(1)
            ln_c(7)
            for half in range(2):
                sa = get_slab()
                for i in range(4):
                    oc = half * 4 + i
                    Pa = acq()
                    kb.mm([lambda h, kc=kc, i=i, Pa=Pa: h.matmul(Pa.t[:, 0:T], lhsT=sa.t[:, kc * 512 + i * 128: kc * 512 + (i + 1) * 128], rhs=ogT.t[:, kc, :], start=(kc == 0), stop=(kc == 7)) for kc in range(8)], sa.b + ogT.b, Pa.b)
                    kb.op(dve, lambda h, Pa=Pa, oc=oc: h.scalar_tensor_tensor(out=mTb.t[:, oc, :], in0=gaT.t[:, oc, :], scalar=1.0, in1=Pa.t[:, 0:T], op0=ALU.add, op1=ALU.mult), Pa.b + [gaT.b[oc]], [mTb.b[oc]])
                    rel(Pa)
                sfill(1)
            hook()
            for half in range(2):
                s2 = get_slab()
                for i in range(4):
                    g_ = half * 4 + i
                    if g_ < NS:
                        p_copy(g_)
                    gate_group(s2, i, gbT, half * 4 + i)
                    if g_ < NS:
                        p_tr(g_)
            sfill(1)
            if next_pro is not None:
                next_pro[1]()
                next_pro[2](0); next_pro[2](1)
            for half in range(2):
                sbb = get_slab()
                for i in range(4):
                    oc = half * 4 + i
                    Pb = acq()
                    kb.mm([lambda h, kc=kc, i=i, Pb=Pb: h.matmul(Pb.t[:, 0:T], lhsT=sbb.t[:, kc * 512 + i * 128: kc * 512 + (i + 1) * 128], rhs=zbT.t[:, kc, :], start=(kc == 0), stop=(kc == 7)) for kc in range(8)], sbb.b + zbT.b, Pb.b)
                    tb_ = ft()
                    kb.op(dve, lambda h, Pb=Pb, oc=oc, tb_=tb_: h.scalar_tensor_tensor(out=tb_.t[:, 0:T], in0=gbT.t[:, oc, :], scalar=1.0, in1=Pb.t[:, 0:T], op0=ALU.add, op1=ALU.mult), Pb.b + [gbT.b[oc]], tb_.b)
                    rel(Pb)
                    kb.op(dve, lambda h, oc=oc, tb_=tb_: h.tensor_tensor(out=mTb.t[:, oc, :], in0=tb_.t[:, 0:T], in1=mTb.t[:, oc, :], op=ALU.add), tb_.b + [mTb.b[oc]], [mTb.b[oc]])
            np_ = next_pro

            def wo(so_, half, j):
                proj_tm(so_, mTb, mTb.b, j, lambda P: kb.op(dve, lambda h: h.scalar_tensor_tensor(out=xt.t[0:TS, j, half * 512:(half + 1) * 512], in0=P.t[0:TS, 0:512], scalar=0.5, in1=xt.t[0:TS, j, half * 512:(half + 1) * 512], op0=ALU.mult, op1=ALU.add), P.b + [xt.b[j]], [xt.b[j]]))

            x1bh = {}

            def x1_copy(j):
                x1b = nxt("x1b", x1bs)
                x1bh[j] = x1b
                kb.op(act, lambda h: h.activation(out=x1b.t[0:TS, :], in_=xt.t[0:TS, j, :], func=AF.Copy), [xt.b[j]], x1b.b)

            def x1_tr(j):
                x1b = x1bh[j]
                kb.mm([lambda h, kc=kc: h.transpose(out=PT.t[:, kc, 0:TS], in_=x1b.t[0:TS, kc * 128:(kc + 1) * 128], identity=ident_b[0:TS, 0:TS]) for kc in range(8)], x1b.b + [cstb.b[0]], PT.b)
                kb.op(dve, lambda h: h.tensor_copy(out=x1T.t[:, :, col(j)], in_=PT.t[:, :, 0:TS]), PT.b, x1T.b)

            sfill(2)
            so0 = get_slab()
            for j in range(NS):
                wo(so0, 0, j)
                if np_ is not None and j == 1:
                    np_[3](0); np_[3](1); np_[2](2); np_[2](3)
            if np_ is not None:
                np_[3](2); np_[3](3)
                if len(np_) > 4:
                    np_[4]()
            so1 = get_slab()
            hook()
            for j in range(NS):
                wo(so1, 1, j)
                x1_copy(j)
                if j >= 1:
                    x1_tr(j - 1)
            x1_tr(NS - 1)
            sfill(2)
            sg0, sg1, spe = get_slab(1), get_slab(2), get_slab(3)
            for half in range(2):
                sgl = sg0 if half == 0 else sg1
                for j in range(NS):
                    sgt = ft()
                    proj_tm(sgl, x1T, x1T.b, j, lambda P, sgt=sgt: kb.op(act, lambda h: h.activation(out=sgt.t[0:TS, :], in_=P.t[0:TS, 0:512], func=AF.Tanh, scale=0.5), P.b, sgt.b))
                    P = acq()
                    kb.mm([lambda h, kc=kc, P=P, j=j, half=half: h.matmul(P.t[0:TS, 0:512], lhsT=pT.t[:, kc, col(j)], rhs=spe.t[:, kc * 1024 + half * 512: kc * 1024 + (half + 1) * 512], start=(kc == 0), stop=(kc == 1)) for kc in range(2)],
                          spe.b + pT.b, P.b)
                    kb.op(dve, lambda h, P=P, sgt=sgt: h.scalar_tensor_tensor(out=sgt.t[0:TS, :], in0=sgt.t[0:TS, :], scalar=1.0, in1=P.t[0:TS, 0:512], op0=ALU.add, op1=ALU.mult), P.b + sgt.b, sgt.b)
                    rel(P)
                    kb.op(dve, lambda h, j=j, half=half, sgt=sgt: h.scalar_tensor_tensor(out=xt.t[0:TS, j, half * 512:(half + 1) * 512], in0=sgt.t[0:TS, :], scalar=0.5, in1=xt.t[0:TS, j, half * 512:(half + 1) * 512], op0=ALU.mult, op1=ALU.add), [xt.b[j]] + sgt.b, [xt.b[j]])
            sfill(100)
            hook()
            ss2, rs2 = small.b[4], small.b[5]
            for j in range(NS):
                kb.op(act, lambda h, j=j: h.activation(out=junk.t[0:TS, :], in_=xt.t[0:TS, j, :], func=AF.Square, accum_out=smallv(4)[0:TS, j:j + 1]), [xt.b[j]], [junk.b[0], ss2])
            kb.op(pool, lambda h: h.tensor_scalar(out=smallv(5)[0:TS, 0:NS], in0=smallv(4)[0:TS, 0:NS], scalar1=1.0 / D, scalar2=EPS, op0=ALU.mult, op1=ALU.add), [ss2], [rs2])
            kb.op(pool, lambda h: h.tensor_tensor(out=smallv(5)[0:TS, 0:NS], in0=smallv(5)[0:TS, 0:NS], in1=neghalf[0:TS, 0:NS], op=ALU.pow), [rs2, cst.b[0]], [rs2])
            for j in range(NS):
                kb.op(dve, lambda h, j=j: h.scalar_tensor_tensor(out=xt.t[0:TS, j, :], in0=xt.t[0:TS, j, :], scalar=smallv(5)[0:TS, j:j + 1], in1=gfin_b[0:TS, :], op0=ALU.mult, op1=ALU.mult), [xt.b[j], rs2, vec.b[0]], [xt.b[j]])
                kb.dma(pool, ydst[tok0 + j * TS:tok0 + (j + 1) * TS, :], xt.t[0:TS, j, :], [xt.b[j]], [], B["yh"])
            if sample or last:
                nr = 64 if sample else 32
                Pu = [acq(), acq()]
                for c in range(8):
                    kb.mm([lambda h, c=c: h.transpose(out=Pu[c // 4].t[0:nr, (c % 4) * 128:(c % 4 + 1) * 128], in_=uf.t[:, c, 0:nr], identity=ident_f)], [uf.b[c], cst.b[0]], Pu[c // 4].b)
                for q_ in range(2):
                    kb.op(act, lambda h, q_=q_: h.activation(out=ut.t[0:nr, q_ * 512:(q_ + 1) * 512], in_=Pu[q_].t[0:nr, 0:512], func=AF.Copy, scale=0.5), Pu[q_].b, ut.b)
                rel(Pu[0]); rel(Pu[1])
                if sample:
                    for i in range(4):
                        kb.dma(pool, css[:, 26 + i, :], ut.t[i * 16:(i + 1) * 16, :], ut.b, [], B["uth"])
                else:
                    kb.dma(pool, csp[:, :], ut.t[2:32, :], ut.b, [], B["uth"])

        def alloc_bufs(es_t, T, TS, NS, sample):
            B = {}
            mk = lambda shape, dt, nb=1: sbt(es_t, shape, dt, nb)
            if not sample:
                A1 = mk([128, 8, T], F32, 8)
                A2 = mk([128, 16, T], BF16, 16)
                A3 = mk([128, 12, T], BF16, 12)
                B["Lt"] = TB(A1.t[:, 0:4, :], A1.b[0:4])
                B["Eb"] = TB(A1.t[:, 4:8, :], A1.b[4:8])
                B["cv"] = TB(A1.t[:, 0:4, :].bitcast(BF16).rearrange("p a (two c) -> p (a two) c", two=2), [A1.b[c_ // 2] for c_ in range(8)])
                B["ktT"] = TB(A2.t[:, 0:4, :], A2.b[0:4])
                B["qtT"] = TB(A2.t[:, 4:8, :], A2.b[4:8])
                B["zaT"] = TB(A2.t[:, 8:16, :], A2.b[8:16])
                B["zbT"] = TB(A2.t[:, 0:8, :], A2.b[0:8])
                B["mTb"] = TB(A2.t[:, 8:16, :], A2.b[8:16])
                B["kdec"] = TB(A3.t[:, 0:4, :], A3.b[0:4])
                B["vt"] = TB(A3.t[:, 4:12, :].rearrange("p (j two) c -> p j (two c)", two=2), A3.b[4:12])
                B["gaT"] = TB(A3.t[:, 0:8, :], A3.b[0:8])
                B["gbT"] = TB(A3.t[:, 0:8, :], A3.b[0:8])
                B["A2"] = A2
            else:
                B["Lt"] = Lt_s
                B["Eb"] = Eb_s
                B["cv"] = mk([128, 8, T], BF16, 8)
                B["ktT"] = mk([128, 4, T], BF16, 4)
                B["qtT"] = mk([128, 4, T], BF16, 4)
                B["zaT"] = mk([128, 8, T], BF16, 8)
                B["zbT"] = mk([128, 8, T], BF16, 8)
                B["mTb"] = mk([128, 8, T], BF16, 8)
                B["kdec"] = mk([128, 1, 512], BF16, 1)
                B["vt"] = mk([128, 1, 1024], BF16, 2)
                B["gaT"] = mk([128, 8, T], BF16, 8)
                B["gbT"] = mk([128, 8, T], BF16, 8)
                B["qpad"] = mk([128, 4, 16, 64], BF16, 4)
                B["kdpad"] = mk([128, 16, 512], BF16, 16)
                B["S0f"] = [mk([128, 4, 256], F32) for _ in range(3)]
                B["S0b_all"] = mk([128, 16, 1024], BF16, 16)
                B["Sout"] = [mk([128, 4, 256], F32) for _ in range(2)]
                B["soh"] = [Buf("soh0"), Buf("soh1")]
            B["xt"] = [xt_s] if sample else [mk([128, NS, D], F32, NS) for _ in range(2)]
            B["x1T"] = TB(B["A2"].t[:, 0:8, :], B["A2"].b[0:8]) if not sample else mk([128, 8, T], BF16, 1)
            B["sgs"] = [mk([128, T], BF16) for _ in range(2)]
            B["FT"] = mk([128, 4, 512], F32, 4)
            B["pt"] = mk([128, NS, 256], F32, 1)
            B["hT"] = hT_s if sample else mk([128, 8, T], BF16, 1)
            B["hb"] = [mk([128, D], BF16) for _ in range(2)]
            B["rT"] = rT_s if sample else mk([32, T], BF16, 1)
            B["attm"] = mk([128, 512], BF16, 1)
            B["on"] = mk([128, D], BF16, 1)
            B["ogT"] = mk([128, 8, T], BF16, 8)
            B["u2T"] = u2T_s if sample else mk([128, 8, 30 + T + 2], BF16, 8)
            B["sq"] = [mk([128, T], BF16) for _ in range(3)]
            B["meanb"] = mk([128, T], F32)
            B["rstdb"] = mk([128, T], F32)
            B["sl"] = [mk([128, T], BF16) for _ in range(2)]
            B["pb"] = mk([128, 256], BF16)
            B["pT"] = mk([128, 2, T], BF16, 1)
            B["uf"] = mk([128, 8, 64], F32, 8)
            B["ut"] = TB(B["FT"].t[:, 0:2, :].rearrange("p a b -> p (a b)"), B["FT"].b[0:2])
            B["yh"] = Buf("yh_s" if sample else "yh_p")
            B["uth"] = Buf("uth_s" if sample else "uth_p")
            if not sample:
                kb.op(dve, lambda h: h.memset(B["rT"].t[:], 1.0), [], B["rT"].b)
            if not sample:
                kb.op(dve, lambda h: h.memset(B["u2T"].t[:], 0.0), [], B["u2T"].b)
            return B

        with ExitStack() as es_p:
            B = alloc_bufs(es_p, 512, 128, 4, False)
            xts = B["xt"]
            tile_pro_load(B, 512, 128, 4, x_p, 0, xts[0])
            issue_slabs(NB)
            tile_pro_compute(B, 512, 128, 4, xts[0])
            def mk_build(c, piece):
                def f():
                    stg_t = (B["u2T"] if c % 2 == 0 else B["ogT"])
                    stg = stg_t.t.rearrange("p a b -> p (a b)")[:, 0:DG_E]
                    t0, t1 = piece * 8, min(31, piece * 8 + 8)
                    kb.op(dve, lambda h: h.scalar_tensor_tensor(out=stg.rearrange("p (t k) -> p t k", k=128)[:, t0:t1, :],
                                                                in0=ident_b.unsqueeze(1).broadcast_to([128, t1 - t0, 128]), scalar=0.5,
                                                                in1=wdwT[:, c * 31 + t0:c * 31 + t1].unsqueeze(2).broadcast_to([128, t1 - t0, 128]), op0=ALU.mult, op1=ALU.mult),
                          [cstb.b[0], vec.b[0]], stg_t.b)
                    if piece == 3:
                        kb.dma(sp, wdg[c], stg, stg_t.b, [diagb[c]], diagb[c])
                return f
            for c in range(8):
                for piece in range(4):
                    pending_builds.append(mk_build(c, piece))
            kb.op(dve, lambda h: h.memset(u2T_s.t[:], 0.0), [], u2T_s.b)

            def cp_load(rt):
                cbb = B["on"]
                kb.dma(pool, cbb.t[0:120, :], sconv[rt * 4:(rt + 1) * 4].rearrange("s r d -> (s r) d"), [], cbb.b, cbb.b[0])

            def cp_comp(rt):
                u2T = u2T_s
                cbb = B["on"]
                kb.mm([lambda h, kc=kc: h.transpose(out=PT.t[:, kc, 0:120], in_=cbb.t[0:120, kc * 128:(kc + 1) * 128], identity=ident_b[0:120, 0:120]) for kc in range(8)], cbb.b + [cstb.b[0]], PT.b)
                for c in range(8):
                    kb.op(dve, lambda h, c=c: h.tensor_scalar(out=u2T.t[:, c, :].rearrange("p (i s) -> p s i", s=16)[:, rt * 4:(rt + 1) * 4, 0:30],
                                                              in0=PT.t[:, c, 0:120].rearrange("p (s i) -> p s i", i=30), scalar1=2.0, scalar2=None, op0=ALU.mult), PT.b, [u2T.b[c]])

            def cp_step(k):
                def f():
                    if k == 0:
                        cp_load(0)
                    elif k == 1:
                        cp_comp(0); cp_load(1)
                    elif k == 2:
                        cp_comp(1); cp_load(2)
                    elif k == 3:
                        cp_comp(2)
                    elif k == 4:
                        cp_load(3)
                    else:
                        cp_comp(3)
                        kb.dma(pool, css[:, 0:26, :], sconv[:, 4:30, :], [], [], Buf("cssh"))
                return f
            for ti in range(4):
                npro = None
                if ti == 3:
                    Bs_ = {"hT": hT_s, "hb": B["hb"]}
                    st_, hc_, tr_ = make_pro(Bs_, TS_S, TS_S, 1, xt_s)
                    npro = (lambda: tile_pro_load(Bs_, TS_S, TS_S, 1, x_s, 0, xt_s), st_,
                            (lambda j, hc_=hc_: hc_(j) if j == 0 else None), (lambda j, tr_=tr_: tr_(j) if j == 0 else None), sample_prechain)
                if ti < 3:
                    st_, hc_, tr_ = make_pro(B, 512, 128, 4, xts[(ti + 1) % 2])
                    if ti == 1:
                        for k_ in range(6):
                            deferred.append(cp_step(k_))
                    npro = (lambda ti=ti: tile_pro_load(B, 512, 128, 4, x_p, (ti + 1) * 512, xts[(ti + 1) % 2]), st_, hc_, tr_)
                run_tile(es_p, B, 512, 128, 4, False, x_p, p_p, y_p, ti * 512, ti == 3, xts[ti % 2], npro)
            kb.barrier()
        with ExitStack() as es_s:
            B = alloc_bufs(es_s, 64, 64, 1, True)
            for g_ in range(NSEQ_S):
                kb.dma(pool, B["S0b_all"].t[:, g_, :].rearrange("k (h d) -> k h d", h=4), sgla[g_].rearrange("h k d -> k h d"),
                       [], [B["S0b_all"].b[g_]], B["S0b_all"].b[g_])
            run_tile(es_s, B, 64, 64, 1, True, x_s, p_s, y_s, 0, False, B["xt"][0], None, pre=True)
            kb.barrier()
    print('nsem', kb.nsem)
    return nc


_NC = None


def _bf_exact_consts():
    cst = np.zeros((128, 768), np.float32)
    cstb = np.zeros((128, 2048), np.float32)
    idx = np.arange(128)
    s, t = idx[:, None], idx[None, :]
    cst[:, 0:128] = np.eye(128)
    cst[:, 128:256] = np.where(s <= t, -1.0 / 16, 0.0)
    cst[:, 256:384] = np.where(s > t, -1.0 / 16, 0.0)
    i64 = np.arange(64)
    s6, t6 = i64[:, None], i64[None, :]
    same = (s6 % 16) == (t6 % 16)
    cst[0:64, 384:448] = np.where(same & (s6 // 16 <= t6 // 16), -1.0 / 16, 0.0)
    cst[0:64, 448:512] = np.where(same & (s6 // 16 > t6 // 16), -1.0 / 16, 0.0)
    cst[:, 512:640] = 1.0
    cst[:, 640:704] = -0.5
    cst[0:64, 704:720] = (i64[:, None] % 16 == np.arange(16)[None, :]).astype(np.float32)
    cst[:, 720] = EPS
    cstb[:, 0:128] = np.eye(128)
    cstb[:, 128:256] = 1.0
    cstb[:, 256:768] = np.tile((s <= t).astype(np.float32), (1, 4))
    cstb[0:64, 768:1024] = np.tile((same & (s6 // 16 <= t6 // 16)).astype(np.float32), (1, 4))
    cm = (np.arange(64)[None, :] % 16 == np.arange(16)[:, None]).astype(np.float32)
    cstb[:, 1024:2048] = np.tile(cm.reshape(1, 1024), (128, 1))
    return cst, cstb


def kernel(x_prompt, x_sample, state_gla, state_conv, p_prompt, p_sample, g_pre, w_in, w_a_up, b_a_up,
           g_gla, w_a_out, w_dw, b_dw, g_ln, b_ln, w_b_out, w_o, w_pe, w_pg, g_final):
    global _NC
    f = lambda a: np.ascontiguousarray(np.asarray(a, dtype=np.float32))
    x_prompt, x_sample, state_gla, state_conv = f(x_prompt), f(x_sample), f(state_gla), f(state_conv)
    p_prompt, p_sample = f(p_prompt), f(p_sample)
    w_in_ = f(w_in)[0]
    def slab(W, c0):
        return W[:, c0:c0 + 512].reshape(8, 128, 512).transpose(1, 0, 2).reshape(128, SLAB_E)
    wa, wb_, wo_, wpg_ = f(w_a_out)[0], f(w_b_out)[0], f(w_o)[0], f(w_pg)[0]
    cQ, cK, cV, cZA, cR, cGA_, cGG, cZB, cGa, cGb = 0, 512, 1024, 2048, 3072, 3088, 4112, 5136, 6160, 7184
    order = [(w_in_, cV), (w_in_, cV + 512), (w_in_, cK), (w_in_, cQ), (w_in_, cZA), (w_in_, cZA + 512),
             ("glu", 0), ("glu", 1), ("glu", 2), ("glu", 3),
             (w_in_, cZB), (w_in_, cZB + 512),
             (w_in_, cGa), (w_in_, cGa + 512), (wa, 0), (wa, 512), (w_in_, cGb), (w_in_, cGb + 512), (wb_, 0), (wb_, 512),
             (wo_, 0), (wo_, 512), (wpg_, 0), (wpg_, 512)]
    wslab = np.empty((NSLAB_W, 128, SLAB_E), np.float32)
    for i, (W, c0) in enumerate(order):
        if isinstance(W, str):
            k = c0
            cols = np.concatenate([np.arange(cGG + 2 * k * 128, cGG + (2 * k + 2) * 128), np.arange(cGA_ + 2 * k * 128, cGA_ + (2 * k + 2) * 128)])
            wslab[i] = w_in_[:, cols].reshape(8, 128, 512).transpose(1, 0, 2).reshape(128, SLAB_E)
        else:
            wslab[i] = slab(W, c0)
    wpe_ = f(w_pe)[0]
    wslab[24] = 0.0
    wslab[24][:, 0:2048] = wpe_.reshape(2, 128, 1024).transpose(1, 0, 2).reshape(128, 2048)
    wr = np.ascontiguousarray(w_in_[:, cR:cR + 16].reshape(8, 128, 16).transpose(1, 0, 2))
    wup = np.concatenate([f(w_a_up)[0], f(b_a_up)[0][None, :]], axis=0)
    cst, cstb = _bf_exact_consts()
    vec = np.zeros((128, 2048 + 288), np.float32)
    vec[:, 0:1024] = f(g_pre)[0][None, :]
    vec[:, 1024:2048] = f(g_final)[None, :]
    vec[:, 2048:2056] = f(b_dw)[0].reshape(8, 128).T
    vec[:, 2056:2064] = f(g_ln)[0].reshape(8, 128).T
    vec[:, 2064:2072] = f(b_ln)[0].reshape(8, 128).T
    vec[:, 2072:2074] = f(g_gla)[0].reshape(2, 128).T
    vec[:, 2080:2080 + 248] = f(w_dw)[0].reshape(31, 8, 128).transpose(2, 1, 0).reshape(128, 248)
    if _NC is None:
        _NC = build_program()
    in_maps = []
    for c in range(NCORES):
        sl = slice(c * NSEQ_S, (c + 1) * NSEQ_S)
        in_maps.append({
            "x_p": x_prompt[c],
            "x_s": np.ascontiguousarray(x_sample[sl].transpose(1, 0, 2).reshape(TS_S, D)),
            "p_p": p_prompt[0, c],
            "p_s": np.ascontiguousarray(p_sample[0, sl].transpose(1, 0, 2).reshape(TS_S, 256)),
            "sgla": state_gla[0, sl],
            "sconv": state_conv[0, sl],
            "wslab": wslab, "wr": wr, "wup": wup, "cst": cst, "cstb": cstb, "vec": vec,
        })
    res = run_bass_kernel_spmd(_NC, in_maps, core_ids=list(range(NCORES)))
    R = res.results
    y_prompt = np.stack([R[c]["y_p"] for c in range(NCORES)], 0)
    y_sample = np.concatenate([R[c]["y_s"].reshape(4, NSEQ_S, D).transpose(1, 0, 2) for c in range(NCORES)], 0)
    gsp = np.stack([R[c]["gsp"] for c in range(NCORES)], 0)[None]
    csp = np.stack([R[c]["csp"] for c in range(NCORES)], 0)[None]
    gss = np.concatenate([R[c]["gss"] for c in range(NCORES)], 0)[None]
    css = np.concatenate([R[c]["css"] for c in range(NCORES)], 0)[None]
    return (y_prompt.astype(np.float32), y_sample.astype(np.float32), gsp.astype(np.float32),
            csp.astype(np.float32), gss.astype(np.float32), css.astype(np.float32))
```

```python
import numpy as np
from contextlib import ExitStack
import concourse.bass as bass
import concourse.mybir as mybir
from concourse.bass_utils import run_bass_kernel_spmd

F32 = mybir.dt.float32
BF16 = mybir.dt.bfloat16
AF = mybir.ActivationFunctionType
ALU = mybir.AluOpType

NCORES = 8
D = 1024
SEQ = 2048
NSEQ_S = 16
TS_S = 64
EPS = 1e-6
NSLAB_W = 25
SLAB_E = 4096
DG_E = 31 * 128
NB = 3


class Buf:
    __slots__ = ("name", "w", "r", "sem", "cnt", "const", "subs")

    def __init__(self, name, const=False):
        self.name = name
        self.w = None
        self.r = {}
        self.sem = None
        self.cnt = 0
        self.const = const
        self.subs = {}


class Eng:
    def __init__(self, name, h, sem):
        self.name = name
        self.h = h
        self.sem = sem
        self.cnt = 0
        self.seen = {}


class KB:
    def __init__(self, nc, es):
        self.nc = nc
        self.es = es
        self.nsem = 0
        self.pe = Eng("pe", nc.tensor, self.new_sem("pe"))
        self.act = Eng("act", nc.scalar, self.new_sem("act"))
        self.dve = Eng("dve", nc.vector, self.new_sem("dve"))
        self.pool = Eng("pool", nc.gpsimd, self.new_sem("pool"))
        self.sp = Eng("sp", nc.sync, self.new_sem("sp"))
        self.engs = [self.pe, self.act, self.dve, self.pool, self.sp]
        self.dma_holders = []

    def new_sem(self, name):
        self.nsem += 1
        return self.es.enter_context(self.nc.semaphore(f"s{self.nsem}_{name}"))

    def _waits(self, eng, reads, writes):
        deps = {}

        def add(d):
            if d is None:
                return
            s, v = d
            k = id(s)
            if k not in deps or deps[k][1] < v:
                deps[k] = (s, v)

        for b in reads:
            add(b.w)
        for b in writes:
            add(b.w)
            for d in b.r.values():
                add(d)
        for k, (s, v) in deps.items():
            if eng is self.pe and s is self.pe.sem:
                continue
            if eng.seen.get(k, 0) >= v:
                continue
            eng.h.wait_ge(s, v)
            eng.seen[k] = v

    def _mark(self, d, reads, writes):
        k = id(d[0])
        for b in reads:
            if not b.const:
                b.r[k] = d
        for b in writes:
            b.w = d
            b.r = {}

    def op(self, eng, fn, reads=(), writes=()):
        self._waits(eng, reads, writes)
        ins = fn(eng.h)
        eng.cnt += 1
        ins.then_inc(eng.sem, 1)
        self._mark((eng.sem, eng.cnt), reads, writes)

    def mm(self, fns, reads=(), writes=()):
        eng = self.pe
        self._waits(eng, reads, writes)
        ins = None
        for f in fns:
            ins = f(eng.h)
        eng.cnt += 1
        ins.then_inc(eng.sem, 1)
        self._mark((eng.sem, eng.cnt), reads, writes)

    def dma(self, q, out, in_, reads, writes, holder):
        self._waits(q, reads, writes)
        if q.name not in holder.subs:
            holder.subs[q.name] = Buf(holder.name + "_" + q.name)
        holder = holder.subs[q.name]
        if holder.sem is None:
            holder.sem = self.new_sem("d_" + holder.name)
            self.dma_holders.append(holder)
        q.h.dma_start(out=out, in_=in_).then_inc(holder.sem, 16)
        holder.cnt += 16
        self._mark((holder.sem, holder.cnt), reads, writes)

    def barrier(self):
        for e in self.engs:
            for o in self.engs:
                if o is e or o.cnt == 0:
                    continue
                k = id(o.sem)
                if e.seen.get(k, 0) < o.cnt:
                    e.h.wait_ge(o.sem, o.cnt)
                    e.seen[k] = o.cnt
            for hld in self.dma_holders:
                k = id(hld.sem)
                if e.seen.get(k, 0) < hld.cnt:
                    e.h.wait_ge(hld.sem, hld.cnt)
                    e.seen[k] = hld.cnt


class TB:
    def __init__(self, t, bufs):
        self.t = t
        self.b = bufs


def build_program():
    nc = bass.Bass("TRN2", target_bir_lowering=False)
    dti = lambda n, s, d=F32: nc.dram_tensor(n, s, d, kind="ExternalInput").ap()
    dto = lambda n, s, d=F32: nc.dram_tensor(n, s, d, kind="ExternalOutput").ap()
    x_p = dti("x_p", [SEQ, D])
    x_s = dti("x_s", [TS_S, D])
    p_p = dti("p_p", [SEQ, 256])
    p_s = dti("p_s", [TS_S, 256])
    sgla = dti("sgla", [NSEQ_S, 4, 128, 256])
    sconv = dti("sconv", [NSEQ_S, 30, D])
    wslab = dti("wslab", [NSLAB_W, 128, SLAB_E])
    wr_d = dti("wr", [128, 8, 16])
    wup_d = dti("wup", [17, 512])
    cst_d = dti("cst", [128, 768])
    cstb_d = dti("cstb", [128, 2048])
    vec_d = dti("vec", [128, 2048 + 288])
    y_p = dto("y_p", [SEQ, D])
    y_s = dto("y_s", [TS_S, D])
    gsp = dto("gsp", [4, 128, 256])
    csp = dto("csp", [30, D])
    gss = dto("gss", [NSEQ_S, 4, 128, 256])
    css = dto("css", [NSEQ_S, 30, D])
    wdg = nc.dram_tensor("wdg", [8, 128, DG_E], BF16, kind="Internal").ap()
    wbf = nc.dram_tensor("wbf", [NSLAB_W, 128, SLAB_E], BF16, kind="Internal").ap()

    with ExitStack() as es:
        kb = KB(nc, es)
        pe, act, dve, pool, sp = kb.pe, kb.act, kb.dve, kb.pool, kb.sp
        cnt = [0]

        def sbt(es_, shape, dt, nb=1, const=False):
            cnt[0] += 1
            t = es_.enter_context(nc.sbuf_tensor(f"t{cnt[0]}", shape, dt))
            return TB(t, [Buf(f"t{cnt[0]}_{i}", const) for i in range(nb)])

        cst = sbt(es, [128, 768], F32)
        cstb = sbt(es, [128, 2048], BF16)
        vec = sbt(es, [128, 2048 + 288], F32)
        wr = sbt(es, [128, 8, 16], BF16)
        wup = sbt(es, [32, 512], BF16)
        slabs = [sbt(es, [128, SLAB_E], BF16) for _ in range(NB)]
        S = sbt(es, [128, 4, 256], F32)
        Sb = sbt(es, [128, 4, 256], BF16)
        small = sbt(es, [128, 64], F32, nb=16)
        junk = sbt(es, [128, 1024], BF16)
        u2T_s = sbt(es, [128, 8, 544], BF16, 8)
        xt_s = sbt(es, [128, 1, D], F32, 1)
        hT_s = sbt(es, [128, 8, TS_S], BF16, 1)
        rT_s = sbt(es, [32, TS_S], BF16, 1)
        Lt_s = sbt(es, [128, 1, 512], F32, 1)
        Eb_s = sbt(es, [128, 4, TS_S], F32, 4)
        Enb_s = sbt(es, [128, 4, TS_S], F32, 4)
        ED_s = sbt(es, [128, 512], F32, 1)
        PS = []
        for i in range(7):
            t = es.enter_context(nc.psum_tensor(f"ps{i}", [128, 512], F32))
            PS.append(TB(t, [Buf(f"ps{i}")]))
        PTt = es.enter_context(nc.psum_tensor("pst", [128, 8, 128], BF16))
        PT = TB(PTt, [Buf("pst")])
        free_ps = list(range(7))

        def acq():
            return PS[free_ps.pop(0)]

        def rel(p):
            free_ps.append(PS.index(p))

        ident_f = cst.t[:, 0:128]
        triC = cst.t[:, 128:256]
        triU = cst.t[:, 256:384]
        triC_s = cst.t[:, 384:448]
        triU_s = cst.t[:, 448:512]
        ones_f = cst.t[:, 512:640]
        neghalf = cst.t[:, 640:704]
        rowmask = cst.t[:, 704:720]
        epsb = cst.t[:, 720:721]
        ident_b = cstb.t[:, 0:128]
        ones_b = cstb.t[:, 128:256]
        mask4 = cstb.t[:, 256:768]
        mask4_s = cstb.t[:, 768:1024]
        colmask = cstb.t[:, 1024:2048]
        gpre_b = vec.t[:, 0:1024]
        gfin_b = vec.t[:, 1024:2048]
        bdwT = vec.t[:, 2048:2056]
        glnT = vec.t[:, 2056:2064]
        blnT = vec.t[:, 2064:2072]
        ggla = vec.t[:, 2072:2074]
        wdwT = vec.t[:, 2080:2080 + 248]
        CB = [cst.b[0], cstb.b[0], vec.b[0]]

        kb.dma(sp, cst.t[:], cst_d, [], cst.b, cst.b[0])
        kb.dma(sp, vec.t[:], vec_d, [], vec.b, vec.b[0])
        kb.dma(pool, cstb.t[:], cstb_d, [], cstb.b, cstb.b[0])
        kb.dma(pool, wr.t[:], wr_d, [], wr.b, wr.b[0])
        kb.op(dve, lambda h: h.memset(wup.t[:], 1.0), [], wup.b)
        kb.dma(pool, wup.t[0:17, :], wup_d, [], wup.b, wup.b[0])
        for b_ in (cst.b[0], cstb.b[0], vec.b[0], wr.b[0]):
            b_.const = True
        kb.op(dve, lambda h: h.memset(S.t[:], 0.0), [], S.b)
        kb.op(dve, lambda h: h.memset(rT_s.t[:], 1.0), [], rT_s.b)
        kb.op(dve, lambda h: h.memset(Sb.t[:], 0.0), [], Sb.b)

        slab_seq = []
        state = {"issued": 0, "cur": -1}

        castb = [Buf(f"cast{i}") for i in range(NSLAB_W)]

        def tile_slab_list(diagb, first=False):
            l = []
            if not first:
                for i in range(10):
                    l.append((wbf[i], SLAB_E, castb[i], -1))
                for c in range(8):
                    l.append((wdg[c], DG_E, diagb[c], -1))
                for i in range(10, 25):
                    l.append((wbf[i], SLAB_E, castb[i], -1))
                return l
            for i in range(10):
                l.append((wslab[i], SLAB_E, None, i))
            for c in range(8):
                l.append((wdg[c], DG_E, diagb[c], -1))
            for i in range(10, 25):
                l.append((wslab[i], SLAB_E, None, i))
            return l

        def get_slab(live=1):
            state["cur"] += 1
            i = state["cur"]
            issue_slabs(min(i + NB - live + 1, len(slab_seq)))
            return slabs[i % NB]

        def issue_slabs(upto):
            while state["issued"] < upto:
                k = state["issued"]
                src, ne, sb_, wi = slab_seq[k]
                sl = slabs[k % NB]
                if sb_ is None:
                    kb.dma(pool, sl.t[:, 0:ne], src, [], sl.b, sl.b[0])
                    kb.dma(sp, wbf[wi], sl.t[:, 0:ne], sl.b, [castb[wi]], castb[wi])
                else:
                    kb.dma(sp, sl.t[:, 0:ne], src, [sb_], sl.b, sl.b[0])
                state["issued"] += 1

        diagb = [Buf(f"diag{c}") for c in range(8)]
        pending_builds = []
        deferred = []

        def hook():
            if deferred:
                deferred.pop(0)()
        for t_ in range(5):
            slab_seq.extend(tile_slab_list(diagb, first=(t_ == 0)))

        def smallv(i, n=4):
            return small.t[:, 4 * i:4 * i + n]

        def tile_pro_load(B, T, TS, NS, xsrc, tok0, xt):
            kb.dma(pool, xt.t[0:TS, :, :], xsrc[tok0:tok0 + T, :].rearrange("(j p) d -> p j d", p=TS), [], xt.b, xt.b[0])

        def make_pro(B, T, TS, NS, xt):
            col = lambda j: slice(j * TS, (j + 1) * TS)
            hT, hbs = B["hT"], B["hb"]
            ssb, rsb = small.b[0], small.b[1]

            def stats():
                for j in range(NS):
                    hbj = hbs[j % len(hbs)]
                    kb.op(dve, lambda h, j=j, hbj=hbj: h.scalar_tensor_tensor(out=hbj.t[0:TS, :], in0=xt.t[0:TS, j, :], scalar=1.0, in1=xt.t[0:TS, j, :], op0=ALU.mult, op1=ALU.mult, accum_out=smallv(0)[0:TS, j:j + 1]),
                          [xt.b[j]], hbj.b + [ssb])
                kb.op(pool, lambda h: h.tensor_scalar(out=smallv(1)[0:TS, 0:NS], in0=smallv(0)[0:TS, 0:NS], scalar1=1.0 / D, scalar2=EPS, op0=ALU.mult, op1=ALU.add), [ssb], [rsb])
                kb.op(pool, lambda h: h.tensor_tensor(out=smallv(1)[0:TS, 0:NS], in0=smallv(1)[0:TS, 0:NS], in1=neghalf[0:TS, 0:NS], op=ALU.pow), [rsb, cst.b[0]], [rsb])

            def hcomp(j):
                hb = hbs[j % len(hbs)]
                kb.op(dve, lambda h: h.scalar_tensor_tensor(out=hb.t[0:TS, :], in0=xt.t[0:TS, j, :], scalar=smallv(1)[0:TS, j:j + 1], in1=gpre_b[0:TS, :], op0=ALU.mult, op1=ALU.mult),
                      [xt.b[j], rsb, vec.b[0]], hb.b)

            def tr(j):
                hb = hbs[j % len(hbs)]
                kb.mm([lambda h, kc=kc: h.transpose(out=PT.t[:, kc, 0:TS], in_=hb.t[0:TS, kc * 128:(kc + 1) * 128], identity=ident_b[0:TS, 0:TS]) for kc in range(8)],
                      hb.b + [cstb.b[0]], PT.b)
                kb.op(act, lambda h: h.activation(out=hT.t[:, :, col(j)], in_=PT.t[:, :, 0:TS], func=AF.Copy), PT.b, hT.b)
            return stats, hcomp, tr

        def tile_pro_compute(B, T, TS, NS, xt):
            st_, hc_, tr_ = make_pro(B, T, TS, NS, xt)
            st_()
            for j in range(NS):
                hc_(j)
                tr_(j)

        def sample_prechain():
            n = TS_S
            P = acq()
            kb.mm([lambda h, kc=kc, P=P: h.matmul(P.t[0:16, 0:n], lhsT=wr.t[:, kc, :], rhs=hT_s.t[:, kc, :], start=(kc == 0), stop=(kc == 7)) for kc in range(8)], hT_s.b + wr.b, P.b)
            kb.op(act, lambda h, P=P: h.activation(out=rT_s.t[0:16, 0:n], in_=P.t[0:16, 0:n], func=AF.Copy), P.b, rT_s.b)
            rel(P)
            P = acq()
            kb.mm([lambda h, P=P: h.matmul(P.t[0:n, 0:512], lhsT=rT_s.t[0:17, 0:n], rhs=wup.t[0:17, :], start=True, stop=True)], rT_s.b + wup.b, P.b)
            kb.op(act, lambda h, P=P: h.activation(out=Lt_s.t[0:n, 0, :], in_=P.t[0:n, 0:512], func=AF.Exp, scale=-1.0), P.b, Lt_s.b)
            rel(P)
            kb.op(act, lambda h: h.activation(out=Lt_s.t[0:n, 0, :], in_=Lt_s.t[0:n, 0, :], func=AF.Ln, bias=1.0), Lt_s.b, Lt_s.b)
            for hd in range(4):
                P = acq()
                kb.mm([lambda h, P=P, hd=hd: h.matmul(P.t[:, 0:n], lhsT=Lt_s.t[0:n, 0, hd * 128:(hd + 1) * 128], rhs=triC_s[0:n, 0:n], start=True, stop=True)], Lt_s.b + [cst.b[0]], P.b)
                kb.op(act, lambda h, P=P, hd=hd: h.activation(out=Eb_s.t[:, hd, :], in_=P.t[:, 0:n], func=AF.Exp), P.b, [Eb_s.b[hd]])
                kb.op(act, lambda h, P=P, hd=hd: h.activation(out=Enb_s.t[:, hd, :], in_=P.t[:, 0:n], func=AF.Exp, scale=-1.0), P.b, [Enb_s.b[hd]])
                rel(P)
            P = acq()
            kb.mm([lambda h, P=P: h.matmul(P.t[0:n, 0:512], lhsT=triU_s[0:n, 0:n], rhs=Lt_s.t[0:n, 0, :], start=True, stop=True)], Lt_s.b + [cst.b[0]], P.b)
            kb.op(act, lambda h, P=P: h.activation(out=ED_s.t[0:n, :], in_=P.t[0:n, 0:512], func=AF.Exp), P.b, ED_s.b)
            rel(P)

        def run_tile(es_t, B, T, TS, NS, sample, xsrc, psrc, ydst, tok0, last, xt, next_pro, pre=False):
            col = lambda j: slice(j * TS, (j + 1) * TS)
            pt, hT, hbs, rT, Lt, Eb = B["pt"], B["hT"], B["hb"], B["rT"], B["Lt"], B["Eb"]
            FT, x1T, sgs = B["FT"], B["x1T"], B["sgs"]
            ftc = [0]

            def ft():
                ftc[0] += 1
                k = ftc[0] % 4
                return TB(FT.t[:, k, :], [FT.b[k]])
            kdec, ktT, qtT, vt, zaT, attm, on, ogT = B["kdec"], B["ktT"], B["qtT"], B["vt"], B["zaT"], B["attm"], B["on"], B["ogT"]
            u2T, cv, sqs, meanb, rstdb = B["u2T"], B["cv"], B["sq"], B["meanb"], B["rstdb"]
            zbT, sls, gaT, gbT, mTb, x1bs, pb, pT, uf, ut = B["zbT"], B["sl"], B["gaT"], B["gbT"], B["mTb"], B["hb"], B["pb"], B["pT"], B["uf"], B["ut"]
            tc_ = triC_s if sample else triC
            tu_ = triU_s if sample else triU
            rot = {"sq": 0, "sl": 0, "x1b": 0, "sg": 0}

            def nxt(name, lst):
                rot[name] += 1
                return lst[rot[name] % len(lst)]

            if next_pro is not None:
                next_pro[0]()
            kb.dma(pool, pt.t[0:TS, :, :], psrc[tok0:tok0 + T, :].rearrange("(j p) d -> p j d", p=TS), [], pt.b, pt.b[0])

            def proj_fm(slab, nch, rhs, rhs_bufs, evac, lhs_off=0, hookb=False):
                for i in range(nch):
                    if pending_builds:
                        pending_builds.pop(0)()
                    P = acq()
                    kb.mm([lambda h, kc=kc, i=i, P=P: h.matmul(P.t[:, 0:T], lhsT=slab.t[:, kc * 512 + lhs_off + i * 128: kc * 512 + lhs_off + (i + 1) * 128], rhs=rhs.t[:, kc, :], start=(kc == 0), stop=(kc == 7)) for kc in range(8)],
                          slab.b + rhs_bufs, P.b)
                    evac(i, P)
                    rel(P)

            def proj_tm(slab, lhs, lhs_bufs, j, evac):
                for _ in range(2):
                    if pending_builds:
                        pending_builds.pop(0)()
                P = acq()
                kb.mm([lambda h, kc=kc, P=P: h.matmul(P.t[0:TS, 0:512], lhsT=lhs.t[:, kc, col(j)], rhs=slab.t[:, kc * 512:(kc + 1) * 512], start=(kc == 0), stop=(kc == 7)) for kc in range(8)],
                      slab.b + lhs_bufs, P.b)
                evac(P)
                rel(P)

            vgroups = []
            vs_ = {}

            def vgroup(half, j):
                if j == 0:
                    vs_[half] = get_slab()
                slabV = vs_[half]
                proj_tm(slabV, hT, hT.b, j, lambda P: kb.op(act, lambda h: h.activation(out=vt.t[0:TS, j, half * 512:(half + 1) * 512], in_=P.t[0:TS, 0:512], func=AF.Copy), P.b, [vt.b[2 * j + half]]))
            for half in range(2):
                for j in range(NS):
                    vgroups.append(lambda half=half, j=j: vgroup(half, j))

            def vfill(n):
                while n > 0 and vgroups:
                    vgroups.pop(0)()
                    n -= 1
            if not pre:
                P = acq()
                kb.mm([lambda h, kc=kc, P=P: h.matmul(P.t[0:16, 0:T], lhsT=wr.t[:, kc, :], rhs=hT.t[:, kc, :], start=(kc == 0), stop=(kc == 7)) for kc in range(8)], hT.b + wr.b, P.b)
                kb.op(act, lambda h, P=P: h.activation(out=rT.t[0:16, 0:T], in_=P.t[0:16, 0:T], func=AF.Copy), P.b, rT.b)
                rel(P)
                vfill(2)
                for j in range(NS):
                    P = acq()
                    kb.mm([lambda h, P=P, j=j: h.matmul(P.t[0:TS, 0:512], lhsT=rT.t[0:17, col(j)], rhs=wup.t[0:17, :], start=True, stop=True)], rT.b + wup.b, P.b)
                    kb.op(act, lambda h, P=P, j=j: h.activation(out=Lt.t[0:TS, j, :], in_=P.t[0:TS, 0:512], func=AF.Exp, scale=-1.0), P.b, [Lt.b[j]] + ([cv.b[2 * j], cv.b[2 * j + 1]] if B["cv_al"] else []))
                    rel(P)
                    kb.op(act, lambda h, j=j: h.activation(out=Lt.t[0:TS, j, :], in_=Lt.t[0:TS, j, :], func=AF.Ln, bias=1.0), [Lt.b[j]], [Lt.b[j]])
            vfill(100)
            slabK = get_slab()

            def p_copy(j):
                kb.op(act, lambda h: h.activation(out=pb.t[0:TS, :], in_=pt.t[0:TS, j, :], func=AF.Copy), [pt.b[0]], pb.b)

            def p_tr(j):
                kb.mm([lambda h, kc=kc: h.transpose(out=PT.t[:, kc, 0:TS], in_=pb.t[0:TS, kc * 128:(kc + 1) * 128], identity=ident_b[0:TS, 0:TS]) for kc in range(2)], pb.b + [cstb.b[0]], PT.b)
                kb.op(dve, lambda h: h.tensor_copy(out=pT.t[:, :, col(j)], in_=PT.t[:, 0:2, 0:TS]), PT.b, pT.b)
            for hd in range(4):
                if pre:
                    enb = TB(Enb_s.t[:, hd, :], [Enb_s.b[hd]])
                else:
                    P = acq()
                    kb.mm([lambda h, P=P, j=j, hd=hd: h.matmul(P.t[:, col(j)], lhsT=Lt.t[0:TS, j, hd * 128:(hd + 1) * 128], rhs=tc_[0:TS, 0:TS], start=True, stop=True) for j in range(NS)],
                          Lt.b[0:NS] + [cst.b[0]], P.b)
                    enb = ft()
                    kb.op(act, lambda h, P=P, hd=hd: h.activation(out=Eb.t[:, hd, :], in_=P.t[:, 0:T], func=AF.Exp), P.b, [Eb.b[hd]])
                    kb.op(act, lambda h, P=P, enb=enb: h.activation(out=enb.t[:, 0:T], in_=P.t[:, 0:T], func=AF.Exp, scale=-1.0), P.b, enb.b)
                    rel(P)
                P = acq()
                kb.mm([lambda h, kc=kc, hd=hd, P=P: h.matmul(P.t[:, 0:T], lhsT=slabK.t[:, kc * 512 + hd * 128: kc * 512 + (hd + 1) * 128], rhs=hT.t[:, kc, :], start=(kc == 0), stop=(kc == 7)) for kc in range(8)],
                      slabK.b + hT.b, P.b)
                kb.op(dve, lambda h, P=P, hd=hd, enb=enb: h.tensor_tensor(out=ktT.t[:, hd, :], in0=P.t[:, 0:T], in1=enb.t[:, 0:T], op=ALU.mult), P.b + enb.b, [ktT.b[hd]])
                rel(P)
            if sample:
                for j in range(NS):
                    if pre:
                        ed = ED_s
                    else:
                        P = acq()
                        kb.mm([lambda h, P=P, j=j: h.matmul(P.t[0:TS, 0:512], lhsT=tu_[0:TS, 0:TS], rhs=Lt.t[0:TS, j, :], start=True, stop=True)], [Lt.b[j], cst.b[0]], P.b)
                        ed = ft()
                        kb.op(act, lambda h, P=P, ed=ed: h.activation(out=ed.t[0:TS, :], in_=P.t[0:TS, 0:512], func=AF.Exp), P.b, ed.b)
                        rel(P)
                    proj_tm(slabK, hT, hT.b, j, lambda P, j=j, ed=ed: kb.op(dve, lambda h: h.tensor_tensor(out=kdec.t[0:TS, j, :], in0=P.t[0:TS, 0:512], in1=ed.t[0:TS, :], op=ALU.mult), P.b + ed.b, [kdec.b[j]]))
            slabQ = get_slab()
            proj_fm(slabQ, 4, hT, hT.b, lambda i, P: kb.op(dve, lambda h: h.scalar_tensor_tensor(out=qtT.t[:, i, :], in0=P.t[:, 0:T], scalar=float(128 ** -0.5), in1=Eb.t[:, i, :], op0=ALU.mult, op1=ALU.mult), P.b + [Eb.b[i]], [qtT.b[i]]))
            if not sample:
                def kd_rescale(j):
                    kd = sgs[j % 2]
                    for hd in range(4):
                        kb.op(dve, lambda h, hd=hd, kd=kd: h.tensor_scalar(out=kd.t[:, hd * TS:(hd + 1) * TS], in0=ktT.t[:, hd, col(j)], scalar1=Eb.t[:, hd, j * TS + TS - 1:j * TS + TS], scalar2=None, op0=ALU.mult),
                              [ktT.b[hd], Eb.b[hd]], kd.b)

                def kd_tr(j):
                    kd = sgs[j % 2]
                    kb.mm([lambda h, hd=hd: h.transpose(out=PT.t[:, hd, 0:TS], in_=kd.t[:, hd * TS:(hd + 1) * TS], identity=ident_b) for hd in range(4)], kd.b + [cstb.b[0]], PT.b)
                    kb.op(act, lambda h: h.activation(out=kdec.t[0:TS, j, :].rearrange("p (a b) -> p a b", a=4), in_=PT.t[:, 0:4, 0:TS], func=AF.Copy), PT.b, [kdec.b[j]])
                kd_sched = [lambda: (kd_rescale(0), kd_rescale(1)), lambda: (kd_tr(0), kd_rescale(2)), lambda: (kd_tr(1), kd_rescale(3)), lambda: kd_tr(2), lambda: kd_tr(3)]
            else:
                kd_sched = []
            for half in range(2):
                sl_ = get_slab()
                for i2 in range(2):
                    if kd_sched:
                        kd_sched.pop(0)()
                    proj_fm(sl_, 2, hT, hT.b, lambda i, P, half=half, i2=i2: kb.op(act, lambda h: h.activation(out=zaT.t[:, half * 4 + i2 * 2 + i, :], in_=P.t[:, 0:T], func=AF.Silu), P.b, [zaT.b[half * 4 + i2 * 2 + i]]), lhs_off=i2 * 256, hookb=True)
            while kd_sched:
                kd_sched.pop(0)()

            if sample:
                qpad, kdpad, S0f, S0ball, Sout = B["qpad"], B["kdpad"], B["S0f"], B["S0b_all"], B["Sout"]
                for hd in range(4):
                    kb.op(dve, lambda h, hd=hd: h.tensor_tensor(out=qpad.t[:, hd, :, :], in0=qtT.t[:, hd, 0:64].unsqueeze(1).broadcast_to([128, 16, 64]), in1=colmask.rearrange("p (a b) -> p a b", a=16), op=ALU.mult),
                          [qtT.b[hd], cstb.b[0]], [qpad.b[hd]])
                for sq_ in range(NSEQ_S):
                    kb.op(act, lambda h, sq_=sq_: h.activation(out=kdpad.t[0:64, sq_, :], in_=kdec.t[0:64, 0, :], func=AF.Copy, scale=rowmask[0:64, sq_:sq_ + 1]), [kdec.b[0], cst.b[0]], [kdpad.b[sq_]])
            if pending_builds or (tok0 == 0 and not sample):
                while pending_builds:
                    pending_builds.pop(0)()
                if tok0 == 0 and not sample:
                    kb.op(dve, lambda h: h.memset(u2T.t[:], 0.0), [], u2T.b)
            sso, rso = small.b[2], small.b[3]
            fillers = []
            if sample:
                uview = lambda c, a, b_: u2T.t[:, c, a * 16:b_ * 16]

            def ev_u(c, P, sgx):
                if sample:
                    kb.op(dve, lambda h: h.scalar_tensor_tensor(out=uview(c, 30, 34), in0=sgx.t[:, 0:64], scalar=1.0, in1=P.t[:, 0:64], op0=ALU.add, op1=ALU.mult), P.b + sgx.b, [u2T.b[c]])
                    kb.op(dve, lambda h: h.scalar_tensor_tensor(out=uf.t[:, c, 0:64], in0=sgx.t[:, 0:64], scalar=1.0, in1=P.t[:, 0:64], op0=ALU.add, op1=ALU.mult), P.b + sgx.b, [uf.b[c]])
                else:
                    kb.op(dve, lambda h: h.scalar_tensor_tensor(out=u2T.t[:, c, 30:30 + T], in0=sgx.t[:, 0:T], scalar=1.0, in1=P.t[:, 0:T], op0=ALU.add, op1=ALU.mult), P.b + sgx.b, [u2T.b[c]])
                    if last:
                        kb.op(dve, lambda h: h.scalar_tensor_tensor(out=uf.t[:, c, 0:32], in0=sgx.t[:, T - 32:T], scalar=1.0, in1=P.t[:, T - 32:T], op0=ALU.add, op1=ALU.mult), P.b + sgx.b, [uf.b[c]])

            hold = {}

            def glu_filler(half, i):
                ii = i % 2
                if ii == 0:
                    hold["s"] = get_slab()
                slg = sla = hold["s"]
                P = acq()
                kb.mm([lambda h, kc=kc, P=P: h.matmul(P.t[:, 0:T], lhsT=slg.t[:, kc * 512 + ii * 128: kc * 512 + (ii + 1) * 128], rhs=hT.t[:, kc, :], start=(kc == 0), stop=(kc == 7)) for kc in range(8)], slg.b + hT.b, P.b)
                sgx = nxt("sg", sgs)
                kb.op(act, lambda h, P=P: h.activation(out=sgx.t[:, 0:T], in_=P.t[:, 0:T], func=AF.Tanh, scale=0.5), P.b, sgx.b)
                rel(P)
                P = acq()
                kb.mm([lambda h, kc=kc, P=P: h.matmul(P.t[:, 0:T], lhsT=sla.t[:, kc * 512 + (2 + ii) * 128: kc * 512 + (3 + ii) * 128], rhs=hT.t[:, kc, :], start=(kc == 0), stop=(kc == 7)) for kc in range(8)], sla.b + hT.b, P.b)
                ev_u(half * 4 + i, P, sgx)
                rel(P)

            cst_ = {}

            def conv_filler(c):
                if c == 0:
                    cst_["Pm"], cst_["Pq"] = acq(), acq()
                Pm, Pq = cst_["Pm"], cst_["Pq"]
                dg = get_slab()
                P = acq()
                if sample:
                    fns = [lambda h, tap=tap, P=P, dg=dg, c=c: h.matmul(P.t[:, 0:64], lhsT=dg.t[:, tap * 128:(tap + 1) * 128], rhs=uview(c, tap, tap + 4), start=(tap == 0), stop=(tap == 30)) for tap in range(31)]
                else:
                    fns = [lambda h, tap=tap, P=P, dg=dg, c=c: h.matmul(P.t[:, 0:T], lhsT=dg.t[:, tap * 128:(tap + 1) * 128], rhs=u2T.t[:, c, tap:tap + T], start=(tap == 0), stop=(tap == 30)) for tap in range(31)]
                kb.mm(fns, dg.b + [u2T.b[c]], P.b)
                kb.op(act, lambda h, P=P, c=c: h.activation(out=cv.t[:, c, :], in_=P.t[:, 0:T], func=AF.Identity, bias=bdwT[:, c:c + 1]), P.b + [vec.b[0]], [cv.b[c]] + ([B["cv_al"][c]] if B["cv_al"] else []))
                sq = nxt("sq", sqs)
                kb.op(act, lambda h, P=P, c=c, sq=sq: h.activation(out=sq.t[:, 0:T], in_=P.t[:, 0:T], func=AF.Square, bias=bdwT[:, c:c + 1]), P.b + [vec.b[0]], sq.b)
                rel(P)
                if "pend" in cst_:
                    cst_.pop("pend")()

                def stat_mm(c=c, sq=sq):
                    kb.mm([lambda h: h.matmul(Pm.t[:, 0:T], lhsT=ones_b, rhs=cv.t[:, c, :], start=(c == 0), stop=(c == 7))], [cv.b[c], cstb.b[0]], Pm.b)
                    kb.mm([lambda h: h.matmul(Pq.t[:, 0:T], lhsT=ones_b, rhs=sq.t[:, 0:T], start=(c == 0), stop=(c == 7))], sq.b + [cstb.b[0]], Pq.b)
                cst_["pend"] = stat_mm
                if not sample and not last:
                    kb.op(act, lambda h, c=c: h.activation(out=u2T.t[:, c, 0:30], in_=u2T.t[:, c, T:T + 30], func=AF.Copy), [u2T.b[c]], [u2T.b[c]])

            for half in range(2):
                for i in range(4):
                    fillers.append(lambda half=half, i=i: glu_filler(half, i))
            for c in range(8):
                fillers.append(lambda c=c: conv_filler(c))

            sfillers = []

            def sfill(n):
                while n > 0 and sfillers:
                    sfillers.pop(0)()
                    n -= 1

            def fill(n, limit=0):
                while n > 0 and len(fillers) > limit:
                    fillers.pop(0)()
                    n -= 1

            for j in range(NS):
                Pa = acq()
                kb.mm([lambda h, hd=hd, Pa=Pa, j=j: h.matmul(Pa.t[0:TS, hd * TS:(hd + 1) * TS], lhsT=ktT.t[:, hd, col(j)], rhs=qtT.t[:, hd, col(j)], start=True, stop=True) for hd in range(4)],
                      ktT.b[0:4] + qtT.b[0:4], Pa.b)
                mk_ = mask4_s[0:64, :] if sample else mask4
                kb.op(dve, lambda h, Pa=Pa: h.tensor_tensor(out=attm.t[0:TS, 0:4 * TS], in0=Pa.t[0:TS, 0:4 * TS], in1=mk_[0:TS, 0:4 * TS], op=ALU.mult), Pa.b + [cstb.b[0]], attm.b)
                rel(Pa)
                fill(1)
                if not sample:
                    Po = [acq(), acq()]
                    oview = lambda hd: Po[hd // 2].t[0:TS, (hd % 2) * 256:(hd % 2 + 1) * 256]
                    obuf = lambda hd: Po[hd // 2].b
                    for hd in range(4):
                        kb.mm([lambda h, hd=hd: h.matmul(oview(hd), lhsT=qtT.t[:, hd, col(j)], rhs=Sb.t[:, hd, :], start=True, stop=False),
                               lambda h, hd=hd: h.matmul(oview(hd), lhsT=attm.t[0:TS, hd * TS:(hd + 1) * TS], rhs=vt.t[0:TS, j, hd * 256:(hd + 1) * 256], start=False, stop=True)],
                              [qtT.b[hd]] + Sb.b + attm.b + [vt.b[2 * j], vt.b[2 * j + 1]], obuf(hd))
                    Pd = [acq(), acq()]
                    for hd in range(4):
                        kb.mm([lambda h, hd=hd: h.matmul(Pd[hd // 2].t[:, (hd % 2) * 256:(hd % 2 + 1) * 256], lhsT=kdec.t[0:TS, j, hd * 128:(hd + 1) * 128], rhs=vt.t[0:TS, j, hd * 256:(hd + 1) * 256], start=True, stop=True)],
                              [kdec.b[j], vt.b[2 * j], vt.b[2 * j + 1]], Pd[hd // 2].b)
                else:
                    Po = [acq(), acq(), acq(), acq()]
                    oview = lambda hd: Po[hd].t[0:TS, 0:256]
                    obuf = lambda hd: Po[hd].b

                    def ld_s0(q_):
                        rr = q_ % len(S0f)
                        kb.dma(pool, S0f[rr].t[:], sgla[q_].rearrange("h k d -> k h d"), [], S0f[rr].b, S0f[rr].b[0])
                    for q_ in range(len(S0f)):
                        ld_s0(q_)
                    for sq_ in range(NSEQ_S):
                        kb.mm([lambda h, hd=hd, sq_=sq_: h.matmul(oview(hd), lhsT=qpad.t[:, hd, sq_, :], rhs=S0ball.t[:, sq_, hd * 256:(hd + 1) * 256], start=(sq_ == 0), stop=False) for hd in range(4)],
                              qpad.b + [S0ball.b[sq_]], [Po[0].b[0], Po[1].b[0], Po[2].b[0], Po[3].b[0]])

                        def supd(sq_=sq_):
                            r_ = sq_ % len(S0f)
                            Pd = [acq(), acq()]
                            for hd in range(4):
                                kb.mm([lambda h, hd=hd: h.matmul(Pd[hd // 2].t[:, (hd % 2) * 256:(hd % 2 + 1) * 256], lhsT=kdpad.t[0:64, sq_, hd * 128:(hd + 1) * 128], rhs=vt.t[0:64, 0, hd * 256:(hd + 1) * 256], start=True, stop=True)],
                                      [kdpad.b[sq_], vt.b[0], vt.b[1]], Pd[hd // 2].b)
                            so = Sout[sq_ % len(Sout)]
                            for hd in range(4):
                                kb.op(dve, lambda h, hd=hd: h.scalar_tensor_tensor(out=so.t[:, hd, :], in0=S0f[r_].t[:, hd, :], scalar=Eb.t[:, hd, 48 + sq_:49 + sq_], in1=Pd[hd // 2].t[:, (hd % 2) * 256:(hd % 2 + 1) * 256], op0=ALU.mult, op1=ALU.add),
                                      S0f[r_].b + [Eb.b[hd]] + Pd[hd // 2].b, so.b)
                            rel(Pd[0]); rel(Pd[1])
                            kb.dma(pool, gss[sq_].rearrange("h k d -> k h d"), so.t[:], so.b, [], B["soh"][sq_ % len(Sout)])
                            if sq_ + len(S0f) < NSEQ_S:
                                ld_s0(sq_ + len(S0f))
                        sfillers.append(supd)
                        if sq_ % 4 == 3:
                            fill(1)
                    for hd in range(4):
                        kb.mm([lambda h, hd=hd: h.matmul(oview(hd), lhsT=attm.t[0:TS, hd * TS:(hd + 1) * TS], rhs=vt.t[0:TS, 0, hd * 256:(hd + 1) * 256], start=False, stop=True)],
                              attm.b + [vt.b[0], vt.b[1]], obuf(hd))
                for hd in range(4):
                    kb.op(act, lambda h, hd=hd: h.activation(out=junk.t[0:TS, 0:256], in_=oview(hd), func=AF.Square, accum_out=smallv(2)[0:TS, hd:hd + 1]), obuf(hd), [junk.b[0], sso])
                kb.op(pool, lambda h: h.tensor_scalar(out=smallv(3)[0:TS, :], in0=smallv(2)[0:TS, :], scalar1=1.0 / 256, scalar2=EPS, op0=ALU.mult, op1=ALU.add), [sso], [rso])
                kb.op(pool, lambda h: h.tensor_tensor(out=smallv(3)[0:TS, :], in0=smallv(3)[0:TS, :], in1=neghalf[0:TS, 0:4], op=ALU.pow), [rso, cst.b[0]], [rso])
                for hd in range(4):
                    kb.op(act, lambda h, hd=hd: h.activation(out=on.t[0:TS, hd * 256:(hd + 1) * 256], in_=oview(hd), func=AF.Copy, scale=smallv(3)[0:TS, hd:hd + 1]), obuf(hd) + [rso], on.b)
                for p_ in Po:
                    rel(p_)
                if not sample:
                    for hd in range(4):
                        kb.op(dve, lambda h, hd=hd: h.scalar_tensor_tensor(out=S.t[:, hd, :], in0=S.t[:, hd, :], scalar=Eb.t[:, hd, j * TS + TS - 1:j * TS + TS], in1=Pd[hd // 2].t[:, (hd % 2) * 256:(hd % 2 + 1) * 256], op0=ALU.mult, op1=ALU.add),
                              S.b + [Eb.b[hd]] + Pd[hd // 2].b, S.b)
                    rel(Pd[0]); rel(Pd[1])
                    kb.op(act, lambda h: h.activation(out=Sb.t[:], in_=S.t[:], func=AF.Copy), S.b, Sb.b)
                fill(3)
                kb.mm([lambda h, kc=kc: h.transpose(out=PT.t[:, kc, 0:TS], in_=on.t[0:TS, kc * 128:(kc + 1) * 128], identity=ident_b[0:TS, 0:TS]) for kc in range(8)], on.b + [cstb.b[0]], PT.b)
                for c in range(2):
                    kb.op(dve, lambda h, c=c: h.scalar_tensor_tensor(out=ogT.t[:, c::2, col(j)], in0=PT.t[:, c::2, 0:TS], scalar=ggla[:, c:c + 1], in1=zaT.t[:, c::2, col(j)], op0=ALU.mult, op1=ALU.mult),
                          PT.b + [vec.b[0]] + zaT.b[c::2], ogT.b[c::2])
            while fillers:
                fill(2)
                sfill(1)
            hook()
            if last:
                kb.dma(pool, gsp.rearrange("h k d -> k h d"), S.t[:], S.b, [], Buf("gsph"))
            cst_.pop("pend")()
            Pm, Pq = cst_["Pm"], cst_["Pq"]
            kb.op(act, lambda h: h.activation(out=meanb.t[:, 0:T], in_=Pm.t[:, 0:T], func=AF.Copy, scale=1.0 / D), Pm.b, meanb.b)
            msq = ft()
            kb.op(dve, lambda h: h.tensor_tensor(out=msq.t[:, 0:T], in0=meanb.t[:, 0:T], in1=meanb.t[:, 0:T], op=ALU.mult), meanb.b, msq.b)
            kb.op(dve, lambda h: h.scalar_tensor_tensor(out=rstdb.t[:, 0:T], in0=Pq.t[:, 0:T], scalar=1.0 / D, in1=msq.t[:, 0:T], op0=ALU.mult, op1=ALU.subtract), Pq.b + msq.b, rstdb.b)
            rel(Pm); rel(Pq)
            kb.op(act, lambda h: h.activation(out=rstdb.t[:, 0:T], in_=rstdb.t[:, 0:T], func=AF.Ln, bias=epsb[:, 0:1]), rstdb.b + [cst.b[0]], rstdb.b)
            kb.op(act, lambda h: h.activation(out=rstdb.t[:, 0:T], in_=rstdb.t[:, 0:T], func=AF.Exp, scale=-0.5), rstdb.b, rstdb.b)
            sfill(1)
            for half in range(2):
                sl_ = get_slab()
                proj_fm(sl_, 4, hT, hT.b, lambda i, P, half=half: kb.op(act, lambda h: h.activation(out=zbT.t[:, half * 4 + i, :], in_=P.t[:, 0:T], func=AF.Silu), P.b, [zbT.b[half * 4 + i]]))
            def ln_ab(c):
                tmp = ft()
                kb.op(dve, lambda h: h.tensor_tensor(out=tmp.t[:, 0:T], in0=cv.t[:, c, :], in1=meanb.t[:, 0:T], op=ALU.subtract), [cv.b[c]] + meanb.b, tmp.b)
                kb.op(dve, lambda h: h.tensor_tensor(out=tmp.t[:, 0:T], in0=tmp.t[:, 0:T], in1=rstdb.t[:, 0:T], op=ALU.mult), tmp.b + rstdb.b, tmp.b)
                sl = sls[c % len(sls)]
                kb.op(act, lambda h: h.activation(out=sl.t[:, 0:T], in_=tmp.t[:, 0:T], func=AF.Silu, scale=glnT[:, c:c + 1], bias=blnT[:, c:c + 1]), tmp.b + [vec.b[0]], sl.b)

            def ln_c(c):
                sl = sls[c % len(sls)]
                kb.op(dve, lambda h: h.tensor_tensor(out=zbT.t[:, c, :], in0=sl.t[:, 0:T], in1=zbT.t[:, c, :], op=ALU.mult), sl.b + [zbT.b[c]], [zbT.b[c]])

            def gate_group(slab_, i, dst, di):
                P = acq()
                kb.mm([lambda h, kc=kc, P=P: h.matmul(P.t[:, 0:T], lhsT=slab_.t[:, kc * 512 + i * 128: kc * 512 + (i + 1) * 128], rhs=hT.t[:, kc, :], start=(kc == 0), stop=(kc == 7)) for kc in range(8)], slab_.b + hT.b, P.b)
                kb.op(act, lambda h, P=P: h.activation(out=dst.t[:, di, :], in_=P.t[:, 0:T], func=AF.Tanh, scale=0.5), P.b, [dst.b[di]])
                rel(P)

            sA0 = get_slab(1)
            sA1 = get_slab(2)
            for c in range(8):
                ln_ab(c)
                if c >= 1:
                    ln_c(c - 1)
                gate_group(sA0 if c < 4 else sA1, c % 4, gaT, c)
                if c % 4 == 3:
                    sfill(1)
            ln_c(7)
            for half in range(2):
                sa = get_slab()
                for i in range(4):
                    oc = half * 4 + i
                    Pa = acq()
                    kb.mm([lambda h, kc=kc, i=i, Pa=Pa: h.matmul(Pa.t[:, 0:T], lhsT=sa.t[:, kc * 512 + i * 128: kc * 512 + (i + 1) * 128], rhs=ogT.t[:, kc, :], start=(kc == 0), stop=(kc == 7)) for kc in range(8)], sa.b + ogT.b, Pa.b)
                    kb.op(dve, lambda h, Pa=Pa, oc=oc: h.scalar_tensor_tensor(out=mTb.t[:, oc, :], in0=gaT.t[:, oc, :], scalar=1.0, in1=Pa.t[:, 0:T], op0=ALU.add, op1=ALU.mult), Pa.b + [gaT.b[oc]], [mTb.b[oc]])
                    rel(Pa)
                sfill(1)
            hook()
            for half in range(2):
                s2 = get_slab()
                for i in range(4):
                    g_ = half * 4 + i
                    if g_ < NS:
                        p_copy(g_)
                    gate_group(s2, i, gbT, half * 4 + i)
                    if g_ < NS:
                        p_tr(g_)
            sfill(1)
            if next_pro is not None:
                next_pro[1]()
                next_pro[2](0); next_pro[2](1)
            for half in range(2):
                sbb = get_slab()
                for i in range(4):
                    oc = half * 4 + i
                    Pb = acq()
                    kb.mm([lambda h, kc=kc, i=i, Pb=Pb: h.matmul(Pb.t[:, 0:T], lhsT=sbb.t[:, kc * 512 + i * 128: kc * 512 + (i + 1) * 128], rhs=zbT.t[:, kc, :], start=(kc == 0), stop=(kc == 7)) for kc in range(8)], sbb.b + zbT.b, Pb.b)
                    tb_ = ft()
                    kb.op(dve, lambda h, Pb=Pb, oc=oc, tb_=tb_: h.scalar_tensor_tensor(out=tb_.t[:, 0:T], in0=gbT.t[:, oc, :], scalar=1.0, in1=Pb.t[:, 0:T], op0=ALU.add, op1=ALU.mult), Pb.b + [gbT.b[oc]], tb_.b)
                    rel(Pb)
                    kb.op(dve, lambda h, oc=oc, tb_=tb_: h.tensor_tensor(out=mTb.t[:, oc, :], in0=tb_.t[:, 0:T], in1=mTb.t[:, oc, :], op=ALU.add), tb_.b + [mTb.b[oc]], [mTb.b[oc]])
            np_ = next_pro

            def wo(so_, half, j):
                proj_tm(so_, mTb, mTb.b, j, lambda P: kb.op(dve, lambda h: h.scalar_tensor_tensor(out=xt.t[0:TS, j, half * 512:(half + 1) * 512], in0=P.t[0:TS, 0:512], scalar=0.5, in1=xt.t[0:TS, j, half * 512:(half + 1) * 512], op0=ALU.mult, op1=ALU.add), P.b + [xt.b[j]], [xt.b[j]]))

            x1bh = {}

            def x1_copy(j):
                x1b = nxt("x1b", x1bs)
                x1bh[j] = x1b
                kb.op(act, lambda h: h.activation(out=x1b.t[0:TS, :], in_=xt.t[0:TS, j, :], func=AF.Copy), [xt.b[j]], x1b.b)

            def x1_tr(j):
                x1b = x1bh[j]
                kb.mm([lambda h, kc=kc: h.transpose(out=PT.t[:, kc, 0:TS], in_=x1b.t[0:TS, kc * 128:(kc + 1) * 128], identity=ident_b[0:TS, 0:TS]) for kc in range(8)], x1b.b + [cstb.b[0]], PT.b)
                kb.op(dve, lambda h: h.tensor_copy(out=x1T.t[:, :, col(j)], in_=PT.t[:, :, 0:TS]), PT.b, x1T.b)

            sfill(2)
            so0 = get_slab()
            for j in range(NS):
                wo(so0, 0, j)
                if np_ is not None and j == 1:
                    np_[3](0); np_[3](1); np_[2](2); np_[2](3)
            if np_ is not None:
                np_[3](2); np_[3](3)
                if len(np_) > 4:
                    np_[4]()
            so1 = get_slab()
            hook()
            for j in range(NS):
                wo(so1, 1, j)
                x1_copy(j)
                if j >= 1:
                    x1_tr(j - 1)
            x1_tr(NS - 1)
            sfill(2)
            sg0, sg1, spe = get_slab(1), get_slab(2), get_slab(3)
            for half in range(2):
                sgl = sg0 if half == 0 else sg1
                for j in range(NS):
                    sgt = ft()
                    proj_tm(sgl, x1T, x1T.b, j, lambda P, sgt=sgt: kb.op(act, lambda h: h.activation(out=sgt.t[0:TS, :], in_=P.t[0:TS, 0:512], func=AF.Tanh, scale=0.5), P.b, sgt.b))
                    P = acq()
                    kb.mm([lambda h, kc=kc, P=P, j=j, half=half: h.matmul(P.t[0:TS, 0:512], lhsT=pT.t[:, kc, col(j)], rhs=spe.t[:, kc * 1024 + half * 512: kc * 1024 + (half + 1) * 512], start=(kc == 0), stop=(kc == 1)) for kc in range(2)],
                          spe.b + pT.b, P.b)
                    kb.op(dve, lambda h, P=P, sgt=sgt: h.scalar_tensor_tensor(out=sgt.t[0:TS, :], in0=sgt.t[0:TS, :], scalar=1.0, in1=P.t[0:TS, 0:512], op0=ALU.add, op1=ALU.mult), P.b + sgt.b, sgt.b)
                    rel(P)
                    kb.op(dve, lambda h, j=j, half=half, sgt=sgt: h.scalar_tensor_tensor(out=xt.t[0:TS, j, half * 512:(half + 1) * 512], in0=sgt.t[0:TS, :], scalar=0.5, in1=xt.t[0:TS, j, half * 512:(half + 1) * 512], op0=ALU.mult, op1=ALU.add), [xt.b[j]] + sgt.b, [xt.b[j]])
            sfill(100)
            hook()
            ss2, rs2 = small.b[4], small.b[5]
            for j in range(NS):
                kb.op(act, lambda h, j=j: h.activation(out=junk.t[0:TS, :], in_=xt.t[0:TS, j, :], func=AF.Square, accum_out=smallv(4)[0:TS, j:j + 1]), [xt.b[j]], [junk.b[0], ss2])
            kb.op(pool, lambda h: h.tensor_scalar(out=smallv(5)[0:TS, 0:NS], in0=smallv(4)[0:TS, 0:NS], scalar1=1.0 / D, scalar2=EPS, op0=ALU.mult, op1=ALU.add), [ss2], [rs2])
            kb.op(pool, lambda h: h.tensor_tensor(out=smallv(5)[0:TS, 0:NS], in0=smallv(5)[0:TS, 0:NS], in1=neghalf[0:TS, 0:NS], op=ALU.pow), [rs2, cst.b[0]], [rs2])
            for j in range(NS):
                kb.op(dve, lambda h, j=j: h.scalar_tensor_tensor(out=xt.t[0:TS, j, :], in0=xt.t[0:TS, j, :], scalar=smallv(5)[0:TS, j:j + 1], in1=gfin_b[0:TS, :], op0=ALU.mult, op1=ALU.mult), [xt.b[j], rs2, vec.b[0]], [xt.b[j]])
                kb.dma(pool, ydst[tok0 + j * TS:tok0 + (j + 1) * TS, :], xt.t[0:TS, j, :], [xt.b[j]], [], B["yh"])
            if sample or last:
                nr = 64 if sample else 32
                Pu = [acq(), acq()]
                for c in range(8):
                    kb.mm([lambda h, c=c: h.transpose(out=Pu[c // 4].t[0:nr, (c % 4) * 128:(c % 4 + 1) * 128], in_=uf.t[:, c, 0:nr], identity=ident_f)], [uf.b[c], cst.b[0]], Pu[c // 4].b)
                for q_ in range(2):
                    kb.op(act, lambda h, q_=q_: h.activation(out=ut.t[0:nr, q_ * 512:(q_ + 1) * 512], in_=Pu[q_].t[0:nr, 0:512], func=AF.Copy, scale=0.5), Pu[q_].b, ut.b)
                rel(Pu[0]); rel(Pu[1])
                if sample:
                    for i in range(4):
                        kb.dma(pool, css[:, 26 + i, :], ut.t[i * 16:(i + 1) * 16, :], ut.b, [], B["uth"])
                else:
                    kb.dma(pool, csp[:, :], ut.t[2:32, :], ut.b, [], B["uth"])

        def alloc_bufs(es_t, T, TS, NS, sample):
            B = {}
            mk = lambda shape, dt, nb=1: sbt(es_t, shape, dt, nb)
            if not sample:
                A1 = mk([128, 8, T], F32, 8)
                A2 = mk([128, 16, T], BF16, 16)
                A3 = mk([128, 12, T], BF16, 12)
                B["Lt"] = TB(A1.t[:, 0:4, :], A1.b[0:4])
                B["Eb"] = TB(A1.t[:, 4:8, :], A1.b[4:8])
                B["cv"] = TB(A1.t[:, 0:4, :].bitcast(BF16).rearrange("p a (two c) -> p (a two) c", two=2), [Buf(f"cv{c_}") for c_ in range(8)])
                B["cv_al"] = [A1.b[c_ // 2] for c_ in range(8)]
                B["ktT"] = TB(A2.t[:, 0:4, :], A2.b[0:4])
                B["qtT"] = TB(A2.t[:, 4:8, :], A2.b[4:8])
                B["zaT"] = TB(A2.t[:, 8:16, :], A2.b[8:16])
                B["zbT"] = TB(A2.t[:, 0:8, :], A2.b[0:8])
                B["mTb"] = TB(A2.t[:, 8:16, :], A2.b[8:16])
                B["kdec"] = TB(A3.t[:, 0:4, :], A3.b[0:4])
                B["vt"] = TB(A3.t[:, 4:12, :].rearrange("p (j two) c -> p j (two c)", two=2), A3.b[4:12])
                B["gaT"] = TB(A3.t[:, 0:8, :], A3.b[0:8])
                B["gbT"] = TB(A3.t[:, 0:8, :], A3.b[0:8])
                B["A2"] = A2
            else:
                B["Lt"] = Lt_s
                B["Eb"] = Eb_s
                B["cv"] = mk([128, 8, T], BF16, 8)
                B["cv_al"] = None
                B["ktT"] = mk([128, 4, T], BF16, 4)
                B["qtT"] = mk([128, 4, T], BF16, 4)
                B["zaT"] = mk([128, 8, T], BF16, 8)
                B["zbT"] = mk([128, 8, T], BF16, 8)
                B["mTb"] = mk([128, 8, T], BF16, 8)
                B["kdec"] = mk([128, 1, 512], BF16, 1)
                B["vt"] = mk([128, 1, 1024], BF16, 2)
                B["gaT"] = mk([128, 8, T], BF16, 8)
                B["gbT"] = mk([128, 8, T], BF16, 8)
                B["qpad"] = mk([128, 4, 16, 64], BF16, 4)
                B["kdpad"] = mk([128, 16, 512], BF16, 16)
                B["S0f"] = [mk([128, 4, 256], F32) for _ in range(3)]
                B["S0b_all"] = mk([128, 16, 1024], BF16, 16)
                B["Sout"] = [mk([128, 4, 256], F32) for _ in range(2)]
                B["soh"] = [Buf("soh0"), Buf("soh1")]
            B["xt"] = [xt_s] if sample else [mk([128, NS, D], F32, NS) for _ in range(2)]
            B["x1T"] = TB(B["A2"].t[:, 0:8, :], B["A2"].b[0:8]) if not sample else mk([128, 8, T], BF16, 1)
            B["sgs"] = [mk([128, T], BF16) for _ in range(2)]
            B["FT"] = mk([128, 4, 512], F32, 4)
            B["pt"] = mk([128, NS, 256], F32, 1)
            B["hT"] = hT_s if sample else mk([128, 8, T], BF16, 1)
            B["hb"] = [mk([128, D], BF16) for _ in range(2)]
            B["rT"] = rT_s if sample else mk([32, T], BF16, 1)
            B["attm"] = mk([128, 512], BF16, 1)
            B["on"] = mk([128, D], BF16, 1)
            B["ogT"] = mk([128, 8, T], BF16, 8)
            B["u2T"] = u2T_s if sample else mk([128, 8, 30 + T + 2], BF16, 8)
            B["sq"] = [mk([128, T], BF16) for _ in range(3)]
            B["meanb"] = mk([128, T], F32)
            B["rstdb"] = mk([128, T], F32)
            B["sl"] = [mk([128, T], BF16) for _ in range(2)]
            B["pb"] = mk([128, 256], BF16)
            B["pT"] = mk([128, 2, T], BF16, 1)
            B["uf"] = mk([128, 8, 64], F32, 8)
            B["ut"] = TB(B["FT"].t[:, 0:2, :].rearrange("p a b -> p (a b)"), B["FT"].b[0:2])
            B["yh"] = Buf("yh_s" if sample else "yh_p")
            B["uth"] = Buf("uth_s" if sample else "uth_p")
            if not sample:
                kb.op(dve, lambda h: h.memset(B["rT"].t[:], 1.0), [], B["rT"].b)
            if not sample:
                kb.op(dve, lambda h: h.memset(B["u2T"].t[:], 0.0), [], B["u2T"].b)
            return B

        with ExitStack() as es_p:
            B = alloc_bufs(es_p, 512, 128, 4, False)
            xts = B["xt"]
            tile_pro_load(B, 512, 128, 4, x_p, 0, xts[0])
            issue_slabs(NB)
            tile_pro_compute(B, 512, 128, 4, xts[0])
            def mk_build(c, piece):
                def f():
                    stg_t = (B["u2T"] if c % 2 == 0 else B["ogT"])
                    stg = stg_t.t.rearrange("p a b -> p (a b)")[:, 0:DG_E]
                    t0, t1 = piece * 8, min(31, piece * 8 + 8)
                    kb.op(dve, lambda h: h.scalar_tensor_tensor(out=stg.rearrange("p (t k) -> p t k", k=128)[:, t0:t1, :],
                                                                in0=ident_b.unsqueeze(1).broadcast_to([128, t1 - t0, 128]), scalar=0.5,
                                                                in1=wdwT[:, c * 31 + t0:c * 31 + t1].unsqueeze(2).broadcast_to([128, t1 - t0, 128]), op0=ALU.mult, op1=ALU.mult),
                          [cstb.b[0], vec.b[0]], stg_t.b)
                    if piece == 3:
                        kb.dma(sp, wdg[c], stg, stg_t.b, [diagb[c]], diagb[c])
                return f
            for c in range(8):
                for piece in range(4):
                    pending_builds.append(mk_build(c, piece))
            kb.op(dve, lambda h: h.memset(u2T_s.t[:], 0.0), [], u2T_s.b)

            def cp_load(rt):
                cbb = B["on"]
                kb.dma(pool, cbb.t[0:120, :], sconv[rt * 4:(rt + 1) * 4].rearrange("s r d -> (s r) d"), [], cbb.b, cbb.b[0])

            def cp_comp(rt):
                u2T = u2T_s
                cbb = B["on"]
                kb.mm([lambda h, kc=kc: h.transpose(out=PT.t[:, kc, 0:120], in_=cbb.t[0:120, kc * 128:(kc + 1) * 128], identity=ident_b[0:120, 0:120]) for kc in range(8)], cbb.b + [cstb.b[0]], PT.b)
                for c in range(8):
                    kb.op(dve, lambda h, c=c: h.tensor_scalar(out=u2T.t[:, c, :].rearrange("p (i s) -> p s i", s=16)[:, rt * 4:(rt + 1) * 4, 0:30],
                                                              in0=PT.t[:, c, 0:120].rearrange("p (s i) -> p s i", i=30), scalar1=2.0, scalar2=None, op0=ALU.mult), PT.b, [u2T.b[c]])

            def cp_step(k):
                def f():
                    if k == 0:
                        cp_load(0)
                    elif k == 1:
                        cp_comp(0); cp_load(1)
                    elif k == 2:
                        cp_comp(1); cp_load(2)
                    elif k == 3:
                        cp_comp(2)
                    elif k == 4:
                        cp_load(3)
                    else:
                        cp_comp(3)
                        kb.dma(pool, css[:, 0:26, :], sconv[:, 4:30, :], [], [], Buf("cssh"))
                return f
            for ti in range(4):
                npro = None
                if ti == 3:
                    Bs_ = {"hT": hT_s, "hb": B["hb"]}
                    st_, hc_, tr_ = make_pro(Bs_, TS_S, TS_S, 1, xt_s)
                    npro = (lambda: tile_pro_load(Bs_, TS_S, TS_S, 1, x_s, 0, xt_s), st_,
                            (lambda j, hc_=hc_: hc_(j) if j == 0 else None), (lambda j, tr_=tr_: tr_(j) if j == 0 else None), sample_prechain)
                if ti < 3:
                    st_, hc_, tr_ = make_pro(B, 512, 128, 4, xts[(ti + 1) % 2])
                    if ti == 1:
                        for k_ in range(6):
                            deferred.append(cp_step(k_))
                    npro = (lambda ti=ti: tile_pro_load(B, 512, 128, 4, x_p, (ti + 1) * 512, xts[(ti + 1) % 2]), st_, hc_, tr_)
                run_tile(es_p, B, 512, 128, 4, False, x_p, p_p, y_p, ti * 512, ti == 3, xts[ti % 2], npro)
            kb.barrier()
        with ExitStack() as es_s:
            B = alloc_bufs(es_s, 64, 64, 1, True)
            for g_ in range(NSEQ_S):
                kb.dma(pool, B["S0b_all"].t[:, g_, :].rearrange("k (h d) -> k h d", h=4), sgla[g_].rearrange("h k d -> k h d"),
                       [], [B["S0b_all"].b[g_]], B["S0b_all"].b[g_])
            run_tile(es_s, B, 64, 64, 1, True, x_s, p_s, y_s, 0, False, B["xt"][0], None, pre=True)
            kb.barrier()
    print('nsem', kb.nsem)
    return nc


_NC = None


def _bf_exact_consts():
    cst = np.zeros((128, 768), np.float32)
    cstb = np.zeros((128, 2048), np.float32)
    idx = np.arange(128)
    s, t = idx[:, None], idx[None, :]
    cst[:, 0:128] = np.eye(128)
    cst[:, 128:256] = np.where(s <= t, -1.0 / 16, 0.0)
    cst[:, 256:384] = np.where(s > t, -1.0 / 16, 0.0)
    i64 = np.arange(64)
    s6, t6 = i64[:, None], i64[None, :]
    same = (s6 % 16) == (t6 % 16)
    cst[0:64, 384:448] = np.where(same & (s6 // 16 <= t6 // 16), -1.0 / 16, 0.0)
    cst[0:64, 448:512] = np.where(same & (s6 // 16 > t6 // 16), -1.0 / 16, 0.0)
    cst[:, 512:640] = 1.0
    cst[:, 640:704] = -0.5
    cst[0:64, 704:720] = (i64[:, None] % 16 == np.arange(16)[None, :]).astype(np.float32)
    cst[:, 720] = EPS
    cstb[:, 0:128] = np.eye(128)
    cstb[:, 128:256] = 1.0
    cstb[:, 256:768] = np.tile((s <= t).astype(np.float32), (1, 4))
    cstb[0:64, 768:1024] = np.tile((same & (s6 // 16 <= t6 // 16)).astype(np.float32), (1, 4))
    cm = (np.arange(64)[None, :] % 16 == np.arange(16)[:, None]).astype(np.float32)
    cstb[:, 1024:2048] = np.tile(cm.reshape(1, 1024), (128, 1))
    return cst, cstb


def kernel(x_prompt, x_sample, state_gla, state_conv, p_prompt, p_sample, g_pre, w_in, w_a_up, b_a_up,
           g_gla, w_a_out, w_dw, b_dw, g_ln, b_ln, w_b_out, w_o, w_pe, w_pg, g_final):
    global _NC
    f = lambda a: np.ascontiguousarray(np.asarray(a, dtype=np.float32))
    x_prompt, x_sample, state_gla, state_conv = f(x_prompt), f(x_sample), f(state_gla), f(state_conv)
    p_prompt, p_sample = f(p_prompt), f(p_sample)
    w_in_ = f(w_in)[0]
    def slab(W, c0):
        return W[:, c0:c0 + 512].reshape(8, 128, 512).transpose(1, 0, 2).reshape(128, SLAB_E)
    wa, wb_, wo_, wpg_ = f(w_a_out)[0], f(w_b_out)[0], f(w_o)[0], f(w_pg)[0]
    cQ, cK, cV, cZA, cR, cGA_, cGG, cZB, cGa, cGb = 0, 512, 1024, 2048, 3072, 3088, 4112, 5136, 6160, 7184
    order = [(w_in_, cV), (w_in_, cV + 512), (w_in_, cK), (w_in_, cQ), (w_in_, cZA), (w_in_, cZA + 512),
             ("glu", 0), ("glu", 1), ("glu", 2), ("glu", 3),
             (w_in_, cZB), (w_in_, cZB + 512),
             (w_in_, cGa), (w_in_, cGa + 512), (wa, 0), (wa, 512), (w_in_, cGb), (w_in_, cGb + 512), (wb_, 0), (wb_, 512),
             (wo_, 0), (wo_, 512), (wpg_, 0), (wpg_, 512)]
    wslab = np.empty((NSLAB_W, 128, SLAB_E), np.float32)
    for i, (W, c0) in enumerate(order):
        if isinstance(W, str):
            k = c0
            cols = np.concatenate([np.arange(cGG + 2 * k * 128, cGG + (2 * k + 2) * 128), np.arange(cGA_ + 2 * k * 128, cGA_ + (2 * k + 2) * 128)])
            wslab[i] = w_in_[:, cols].reshape(8, 128, 512).transpose(1, 0, 2).reshape(128, SLAB_E)
        else:
            wslab[i] = slab(W, c0)
    wpe_ = f(w_pe)[0]
    wslab[24] = 0.0
    wslab[24][:, 0:2048] = wpe_.reshape(2, 128, 1024).transpose(1, 0, 2).reshape(128, 2048)
    wr = np.ascontiguousarray(w_in_[:, cR:cR + 16].reshape(8, 128, 16).transpose(1, 0, 2))
    wup = np.concatenate([f(w_a_up)[0], f(b_a_up)[0][None, :]], axis=0)
    cst, cstb = _bf_exact_consts()
    vec = np.zeros((128, 2048 + 288), np.float32)
    vec[:, 0:1024] = f(g_pre)[0][None, :]
    vec[:, 1024:2048] = f(g_final)[None, :]
    vec[:, 2048:2056] = f(b_dw)[0].reshape(8, 128).T
    vec[:, 2056:2064] = f(g_ln)[0].reshape(8, 128).T
    vec[:, 2064:2072] = f(b_ln)[0].reshape(8, 128).T
    vec[:, 2072:2074] = f(g_gla)[0].reshape(2, 128).T
    vec[:, 2080:2080 + 248] = f(w_dw)[0].reshape(31, 8, 128).transpose(2, 1, 0).reshape(128, 248)
    if _NC is None:
        _NC = build_program()
    in_maps = []
    for c in range(NCORES):
        sl = slice(c * NSEQ_S, (c + 1) * NSEQ_S)
        in_maps.append({
            "x_p": x_prompt[c],
            "x_s": np.ascontiguousarray(x_sample[sl].transpose(1, 0, 2).reshape(TS_S, D)),
            "p_p": p_prompt[0, c],
            "p_s": np.ascontiguousarray(p_sample[0, sl].transpose(1, 0, 2).reshape(TS_S, 256)),
            "sgla": state_gla[0, sl],
            "sconv": state_conv[0, sl],
            "wslab": wslab, "wr": wr, "wup": wup, "cst": cst, "cstb": cstb, "vec": vec,
        })
    res = run_bass_kernel_spmd(_NC, in_maps, core_ids=list(range(NCORES)))
    R = res.results
    y_prompt = np.stack([R[c]["y_p"] for c in range(NCORES)], 0)
    y_sample = np.concatenate([R[c]["y_s"].reshape(4, NSEQ_S, D).transpose(1, 0, 2) for c in range(NCORES)], 0)
    gsp = np.stack([R[c]["gsp"] for c in range(NCORES)], 0)[None]
    csp = np.stack([R[c]["csp"] for c in range(NCORES)], 0)[None]
    gss = np.concatenate([R[c]["gss"] for c in range(NCORES)], 0)[None]
    css = np.concatenate([R[c]["css"] for c in range(NCORES)], 0)[None]
    return (y_prompt.astype(np.float32), y_sample.astype(np.float32), gsp.astype(np.float32),
            csp.astype(np.float32), gss.astype(np.float32), css.astype(np.float32))
```

```python
import numpy as np
from contextlib import ExitStack
import concourse.bass as bass
import concourse.mybir as mybir
from concourse.bass_utils import run_bass_kernel_spmd

F32 = mybir.dt.float32
BF16 = mybir.dt.bfloat16
AF = mybir.ActivationFunctionType
ALU = mybir.AluOpType

NCORES = 8
D = 1024
SEQ = 2048
NSEQ_S = 16
TS_S = 64
EPS = 1e-6
NSLAB_W = 25
SLAB_E = 4096
DG_E = 31 * 128
NB = 3


class Buf:
    __slots__ = ("name", "w", "r", "sem", "cnt", "const", "subs")

    def __init__(self, name, const=False):
        self.name = name
        self.w = None
        self.r = {}
        self.sem = None
        self.cnt = 0
        self.const = const
        self.subs = {}


class Eng:
    def __init__(self, name, h, sem):
        self.name = name
        self.h = h
        self.sem = sem
        self.cnt = 0
        self.seen = {}


class KB:
    def __init__(self, nc, es):
        self.nc = nc
        self.es = es
        self.nsem = 0
        self.pe = Eng("pe", nc.tensor, self.new_sem("pe"))
        self.act = Eng("act", nc.scalar, self.new_sem("act"))
        self.dve = Eng("dve", nc.vector, self.new_sem("dve"))
        self.pool = Eng("pool", nc.gpsimd, self.new_sem("pool"))
        self.sp = Eng("sp", nc.sync, self.new_sem("sp"))
        self.engs = [self.pe, self.act, self.dve, self.pool, self.sp]
        self.dma_holders = []

    def new_sem(self, name):
        self.nsem += 1
        return self.es.enter_context(self.nc.semaphore(f"s{self.nsem}_{name}"))

    def _waits(self, eng, reads, writes):
        deps = {}

        def add(d):
            if d is None:
                return
            s, v = d
            k = id(s)
            if k not in deps or deps[k][1] < v:
                deps[k] = (s, v)

        for b in reads:
            add(b.w)
        for b in writes:
            add(b.w)
            for d in b.r.values():
                add(d)
        for k, (s, v) in deps.items():
            if eng is self.pe and s is self.pe.sem:
                continue
            if eng.seen.get(k, 0) >= v:
                continue
            eng.h.wait_ge(s, v)
            eng.seen[k] = v

    def _mark(self, d, reads, writes):
        k = id(d[0])
        for b in reads:
            if not b.const:
                b.r[k] = d
        for b in writes:
            b.w = d
            b.r = {}

    def op(self, eng, fn, reads=(), writes=()):
        self._waits(eng, reads, writes)
        ins = fn(eng.h)
        eng.cnt += 1
        ins.then_inc(eng.sem, 1)
        self._mark((eng.sem, eng.cnt), reads, writes)

    def mm(self, fns, reads=(), writes=()):
        eng = self.pe
        self._waits(eng, reads, writes)
        ins = None
        for f in fns:
            ins = f(eng.h)
        eng.cnt += 1
        ins.then_inc(eng.sem, 1)
        self._mark((eng.sem, eng.cnt), reads, writes)

    def dma(self, q, out, in_, reads, writes, holder):
        self._waits(q, reads, writes)
        if q.name not in holder.subs:
            holder.subs[q.name] = Buf(holder.name + "_" + q.name)
        holder = holder.subs[q.name]
        if holder.sem is None:
            holder.sem = self.new_sem("d_" + holder.name)
            self.dma_holders.append(holder)
        q.h.dma_start(out=out, in_=in_).then_inc(holder.sem, 16)
        holder.cnt += 16
        self._mark((holder.sem, holder.cnt), reads, writes)

    def barrier(self):
        for e in self.engs:
            for o in self.engs:
                if o is e or o.cnt == 0:
                    continue
                k = id(o.sem)
                if e.seen.get(k, 0) < o.cnt:
                    e.h.wait_ge(o.sem, o.cnt)
                    e.seen[k] = o.cnt
            for hld in self.dma_holders:
                k = id(hld.sem)
                if e.seen.get(k, 0) < hld.cnt:
                    e.h.wait_ge(hld.sem, hld.cnt)
                    e.seen[k] = hld.cnt


class TB:
    def __init__(self, t, bufs):
        self.t = t
        self.b = bufs


def build_program():
    nc = bass.Bass("TRN2", target_bir_lowering=False)
    dti = lambda n, s, d=F32: nc.dram_tensor(n, s, d, kind="ExternalInput").ap()
    dto = lambda n, s, d=F32: nc.dram_tensor(n, s, d, kind="ExternalOutput").ap()
    x_p = dti("x_p", [SEQ, D])
    x_s = dti("x_s", [TS_S, D])
    p_p = dti("p_p", [SEQ, 256])
    p_s = dti("p_s", [TS_S, 256])
    sgla = dti("sgla", [NSEQ_S, 4, 128, 256])
    sconv = dti("sconv", [NSEQ_S, 30, D])
    wslab = dti("wslab", [NSLAB_W, 128, SLAB_E])
    wr_d = dti("wr", [128, 8, 16])
    wup_d = dti("wup", [17, 512])
    cst_d = dti("cst", [128, 768])
    cstb_d = dti("cstb", [128, 2048])
    vec_d = dti("vec", [128, 2048 + 288])
    y_p = dto("y_p", [SEQ, D])
    y_s = dto("y_s", [TS_S, D])
    gsp = dto("gsp", [4, 128, 256])
    csp = dto("csp", [30, D])
    gss = dto("gss", [NSEQ_S, 4, 128, 256])
    css = dto("css", [NSEQ_S, 30, D])
    wdg = nc.dram_tensor("wdg", [8, 128, DG_E], BF16, kind="Internal").ap()
    wbf = nc.dram_tensor("wbf", [NSLAB_W, 128, SLAB_E], BF16, kind="Internal").ap()

    with ExitStack() as es:
        kb = KB(nc, es)
        pe, act, dve, pool, sp = kb.pe, kb.act, kb.dve, kb.pool, kb.sp
        cnt = [0]

        def sbt(es_, shape, dt, nb=1, const=False):
            cnt[0] += 1
            t = es_.enter_context(nc.sbuf_tensor(f"t{cnt[0]}", shape, dt))
            return TB(t, [Buf(f"t{cnt[0]}_{i}", const) for i in range(nb)])

        cst = sbt(es, [128, 768], F32)
        cstb = sbt(es, [128, 2048], BF16)
        vec = sbt(es, [128, 2048 + 288], F32)
        wr = sbt(es, [128, 8, 16], BF16)
        wup = sbt(es, [32, 512], BF16)
        slabs = [sbt(es, [128, SLAB_E], BF16) for _ in range(NB)]
        S = sbt(es, [128, 4, 256], F32)
        Sb = sbt(es, [128, 4, 256], BF16)
        small = sbt(es, [128, 64], F32, nb=16)
        junk = sbt(es, [128, 1024], BF16)
        u2T_s = sbt(es, [128, 8, 544], BF16, 8)
        xt_s = sbt(es, [128, 1, D], F32, 1)
        hT_s = sbt(es, [128, 8, TS_S], BF16, 1)
        rT_s = sbt(es, [32, TS_S], BF16, 1)
        Lt_s = sbt(es, [128, 1, 512], F32, 1)
        Eb_s = sbt(es, [128, 4, TS_S], F32, 4)
        Enb_s = sbt(es, [128, 4, TS_S], F32, 4)
        ED_s = sbt(es, [128, 512], F32, 1)
        PS = []
        for i in range(7):
            t = es.enter_context(nc.psum_tensor(f"ps{i}", [128, 512], F32))
            PS.append(TB(t, [Buf(f"ps{i}")]))
        PTt = es.enter_context(nc.psum_tensor("pst", [128, 8, 128], BF16))
        PT = TB(PTt, [Buf("pst")])
        free_ps = list(range(7))

        def acq():
            return PS[free_ps.pop(0)]

        def rel(p):
            free_ps.append(PS.index(p))

        ident_f = cst.t[:, 0:128]
        triC = cst.t[:, 128:256]
        triU = cst.t[:, 256:384]
        triC_s = cst.t[:, 384:448]
        triU_s = cst.t[:, 448:512]
        ones_f = cst.t[:, 512:640]
        neghalf = cst.t[:, 640:704]
        rowmask = cst.t[:, 704:720]
        epsb = cst.t[:, 720:721]
        ident_b = cstb.t[:, 0:128]
        ones_b = cstb.t[:, 128:256]
        mask4 = cstb.t[:, 256:768]
        mask4_s = cstb.t[:, 768:1024]
        colmask = cstb.t[:, 1024:2048]
        gpre_b = vec.t[:, 0:1024]
        gfin_b = vec.t[:, 1024:2048]
        bdwT = vec.t[:, 2048:2056]
        glnT = vec.t[:, 2056:2064]
        blnT = vec.t[:, 2064:2072]
        ggla = vec.t[:, 2072:2074]
        wdwT = vec.t[:, 2080:2080 + 248]
        CB = [cst.b[0], cstb.b[0], vec.b[0]]

        kb.dma(sp, cst.t[:], cst_d, [], cst.b, cst.b[0])
        kb.dma(sp, vec.t[:], vec_d, [], vec.b, vec.b[0])
        kb.dma(pool, cstb.t[:], cstb_d, [], cstb.b, cstb.b[0])
        kb.dma(pool, wr.t[:], wr_d, [], wr.b, wr.b[0])
        kb.op(dve, lambda h: h.memset(wup.t[:], 1.0), [], wup.b)
        kb.dma(pool, wup.t[0:17, :], wup_d, [], wup.b, wup.b[0])
        for b_ in (cst.b[0], cstb.b[0], vec.b[0], wr.b[0]):
            b_.const = True
        kb.op(dve, lambda h: h.memset(S.t[:], 0.0), [], S.b)
        kb.op(dve, lambda h: h.memset(rT_s.t[:], 1.0), [], rT_s.b)
        kb.op(dve, lambda h: h.memset(Sb.t[:], 0.0), [], Sb.b)

        slab_seq = []
        state = {"issued": 0, "cur": -1}

        castb = [Buf(f"cast{i}") for i in range(NSLAB_W)]

        def tile_slab_list(diagb, first=False):
            l = []
            if not first:
                for i in range(10):
                    l.append((wbf[i], SLAB_E, castb[i], -1))
                for c in range(8):
                    l.append((wdg[c], DG_E, diagb[c], -1))
                for i in range(10, 25):
                    l.append((wbf[i], SLAB_E, castb[i], -1))
                return l
            for i in range(10):
                l.append((wslab[i], SLAB_E, None, i))
            for c in range(8):
                l.append((wdg[c], DG_E, diagb[c], -1))
            for i in range(10, 25):
                l.append((wslab[i], SLAB_E, None, i))
            return l

        def get_slab(live=1):
            state["cur"] += 1
            i = state["cur"]
            issue_slabs(min(i + NB - live + 1, len(slab_seq)))
            return slabs[i % NB]

        def issue_slabs(upto):
            while state["issued"] < upto:
                k = state["issued"]
                src, ne, sb_, wi = slab_seq[k]
                sl = slabs[k % NB]
                if sb_ is None:
                    kb.dma(pool, sl.t[:, 0:ne], src, [], sl.b, sl.b[0])
                    kb.dma(sp, wbf[wi], sl.t[:, 0:ne], sl.b, [castb[wi]], castb[wi])
                else:
                    kb.dma(sp, sl.t[:, 0:ne], src, [sb_], sl.b, sl.b[0])
                state["issued"] += 1

        diagb = [Buf(f"diag{c}") for c in range(8)]
        pending_builds = []
        deferred = []

        def hook():
            if deferred:
                deferred.pop(0)()
        for t_ in range(5):
            slab_seq.extend(tile_slab_list(diagb, first=(t_ == 0)))

        def smallv(i, n=4):
            return small.t[:, 4 * i:4 * i + n]

        def tile_pro_load(B, T, TS, NS, xsrc, tok0, xt):
            kb.dma(pool, xt.t[0:TS, :, :], xsrc[tok0:tok0 + T, :].rearrange("(j p) d -> p j d", p=TS), [], xt.b, xt.b[0])

        def make_pro(B, T, TS, NS, xt):
            col = lambda j: slice(j * TS, (j + 1) * TS)
            hT, hbs = B["hT"], B["hb"]
            ssb, rsb = small.b[0], small.b[1]

            def stats():
                for j in range(NS):
                    hbj = hbs[j % len(hbs)]
                    kb.op(dve, lambda h, j=j, hbj=hbj: h.scalar_tensor_tensor(out=hbj.t[0:TS, :], in0=xt.t[0:TS, j, :], scalar=1.0, in1=xt.t[0:TS, j, :], op0=ALU.mult, op1=ALU.mult, accum_out=smallv(0)[0:TS, j:j + 1]),
                          [xt.b[j]], hbj.b + [ssb])
                kb.op(pool, lambda h: h.tensor_scalar(out=smallv(1)[0:TS, 0:NS], in0=smallv(0)[0:TS, 0:NS], scalar1=1.0 / D, scalar2=EPS, op0=ALU.mult, op1=ALU.add), [ssb], [rsb])
                kb.op(pool, lambda h: h.tensor_tensor(out=smallv(1)[0:TS, 0:NS], in0=smallv(1)[0:TS, 0:NS], in1=neghalf[0:TS, 0:NS], op=ALU.pow), [rsb, cst.b[0]], [rsb])

            def hcomp(j):
                hb = hbs[j % len(hbs)]
                kb.op(dve, lambda h: h.scalar_tensor_tensor(out=hb.t[0:TS, :], in0=xt.t[0:TS, j, :], scalar=smallv(1)[0:TS, j:j + 1], in1=gpre_b[0:TS, :], op0=ALU.mult, op1=ALU.mult),
                      [xt.b[j], rsb, vec.b[0]], hb.b)

            def tr(j):
                hb = hbs[j % len(hbs)]
                kb.mm([lambda h, kc=kc: h.transpose(out=PT.t[:, kc, 0:TS], in_=hb.t[0:TS, kc * 128:(kc + 1) * 128], identity=ident_b[0:TS, 0:TS]) for kc in range(8)],
                      hb.b + [cstb.b[0]], PT.b)
                kb.op(act, lambda h: h.activation(out=hT.t[:, :, col(j)], in_=PT.t[:, :, 0:TS], func=AF.Copy), PT.b, hT.b)
            return stats, hcomp, tr

        def tile_pro_compute(B, T, TS, NS, xt):
            st_, hc_, tr_ = make_pro(B, T, TS, NS, xt)
            st_()
            for j in range(NS):
                hc_(j)
                tr_(j)

        def sample_prechain():
            n = TS_S
            P = acq()
            kb.mm([lambda h, kc=kc, P=P: h.matmul(P.t[0:16, 0:n], lhsT=wr.t[:, kc, :], rhs=hT_s.t[:, kc, :], start=(kc == 0), stop=(kc == 7)) for kc in range(8)], hT_s.b + wr.b, P.b)
            kb.op(act, lambda h, P=P: h.activation(out=rT_s.t[0:16, 0:n], in_=P.t[0:16, 0:n], func=AF.Copy), P.b, rT_s.b)
            rel(P)
            P = acq()
            kb.mm([lambda h, P=P: h.matmul(P.t[0:n, 0:512], lhsT=rT_s.t[0:17, 0:n], rhs=wup.t[0:17, :], start=True, stop=True)], rT_s.b + wup.b, P.b)
            kb.op(act, lambda h, P=P: h.activation(out=Lt_s.t[0:n, 0, :], in_=P.t[0:n, 0:512], func=AF.Exp, scale=-1.0), P.b, Lt_s.b)
            rel(P)
            kb.op(act, lambda h: h.activation(out=Lt_s.t[0:n, 0, :], in_=Lt_s.t[0:n, 0, :], func=AF.Ln, bias=1.0), Lt_s.b, Lt_s.b)
            for hd in range(4):
                P = acq()
                kb.mm([lambda h, P=P, hd=hd: h.matmul(P.t[:, 0:n], lhsT=Lt_s.t[0:n, 0, hd * 128:(hd + 1) * 128], rhs=triC_s[0:n, 0:n], start=True, stop=True)], Lt_s.b + [cst.b[0]], P.b)
                kb.op(act, lambda h, P=P, hd=hd: h.activation(out=Eb_s.t[:, hd, :], in_=P.t[:, 0:n], func=AF.Exp), P.b, [Eb_s.b[hd]])
                kb.op(act, lambda h, P=P, hd=hd: h.activation(out=Enb_s.t[:, hd, :], in_=P.t[:, 0:n], func=AF.Exp, scale=-1.0), P.b, [Enb_s.b[hd]])
                rel(P)
            P = acq()
            kb.mm([lambda h, P=P: h.matmul(P.t[0:n, 0:512], lhsT=triU_s[0:n, 0:n], rhs=Lt_s.t[0:n, 0, :], start=True, stop=True)], Lt_s.b + [cst.b[0]], P.b)
            kb.op(act, lambda h, P=P: h.activation(out=ED_s.t[0:n, :], in_=P.t[0:n, 0:512], func=AF.Exp), P.b, ED_s.b)
            rel(P)

        def run_tile(es_t, B, T, TS, NS, sample, xsrc, psrc, ydst, tok0, last, xt, next_pro, pre=False):
            col = lambda j: slice(j * TS, (j + 1) * TS)
            pt, hT, hbs, rT, Lt, Eb = B["pt"], B["hT"], B["hb"], B["rT"], B["Lt"], B["Eb"]
            FT, x1T, sgs = B["FT"], B["x1T"], B["sgs"]
            ftc = [0]

            def ft():
                ftc[0] += 1
                k = ftc[0] % 4
                return TB(FT.t[:, k, :], [FT.b[k]])
            kdec, ktT, qtT, vt, zaT, attm, on, ogT = B["kdec"], B["ktT"], B["qtT"], B["vt"], B["zaT"], B["attm"], B["on"], B["ogT"]
            u2T, cv, sqs, meanb, rstdb = B["u2T"], B["cv"], B["sq"], B["meanb"], B["rstdb"]
            zbT, sls, gaT, gbT, mTb, x1bs, pb, pT, uf, ut = B["zbT"], B["sl"], B["gaT"], B["gbT"], B["mTb"], B["hb"], B["pb"], B["pT"], B["uf"], B["ut"]
            tc_ = triC_s if sample else triC
            tu_ = triU_s if sample else triU
            rot = {"sq": 0, "sl": 0, "x1b": 0, "sg": 0}

            def nxt(name, lst):
                rot[name] += 1
                return lst[rot[name] % len(lst)]

            if next_pro is not None:
                next_pro[0]()
            kb.dma(pool, pt.t[0:TS, :, :], psrc[tok0:tok0 + T, :].rearrange("(j p) d -> p j d", p=TS), [], pt.b, pt.b[0])

            def proj_fm(slab, nch, rhs, rhs_bufs, evac, lhs_off=0, hookb=False):
                for i in range(nch):
                    for _ in range(2):
                        if pending_builds:
                            pending_builds.pop(0)()
                    P = acq()
                    kb.mm([lambda h, kc=kc, i=i, P=P: h.matmul(P.t[:, 0:T], lhsT=slab.t[:, kc * 512 + lhs_off + i * 128: kc * 512 + lhs_off + (i + 1) * 128], rhs=rhs.t[:, kc, :], start=(kc == 0), stop=(kc == 7)) for kc in range(8)],
                          slab.b + rhs_bufs, P.b)
                    evac(i, P)
                    rel(P)

            def proj_tm(slab, lhs, lhs_bufs, j, evac):
                for _ in range(2):
                    if pending_builds:
                        pending_builds.pop(0)()
                P = acq()
                kb.mm([lambda h, kc=kc, P=P: h.matmul(P.t[0:TS, 0:512], lhsT=lhs.t[:, kc, col(j)], rhs=slab.t[:, kc * 512:(kc + 1) * 512], start=(kc == 0), stop=(kc == 7)) for kc in range(8)],
                      slab.b + lhs_bufs, P.b)
                evac(P)
                rel(P)

            vgroups = []
            vs_ = {}

            def vgroup(half, j):
                if j == 0:
                    vs_[half] = get_slab()
                slabV = vs_[half]
                proj_tm(slabV, hT, hT.b, j, lambda P: kb.op(act, lambda h: h.activation(out=vt.t[0:TS, j, half * 512:(half + 1) * 512], in_=P.t[0:TS, 0:512], func=AF.Copy), P.b, [vt.b[2 * j + half]]))
            for half in range(2):
                for j in range(NS):
                    vgroups.append(lambda half=half, j=j: vgroup(half, j))

            def vfill(n):
                while n > 0 and vgroups:
                    vgroups.pop(0)()
                    n -= 1
            if not pre:
                P = acq()
                kb.mm([lambda h, kc=kc, P=P: h.matmul(P.t[0:16, 0:T], lhsT=wr.t[:, kc, :], rhs=hT.t[:, kc, :], start=(kc == 0), stop=(kc == 7)) for kc in range(8)], hT.b + wr.b, P.b)
                kb.op(act, lambda h, P=P: h.activation(out=rT.t[0:16, 0:T], in_=P.t[0:16, 0:T], func=AF.Copy), P.b, rT.b)
                rel(P)
                vfill(2)
                for j in range(NS):
                    P = acq()
                    kb.mm([lambda h, P=P, j=j: h.matmul(P.t[0:TS, 0:512], lhsT=rT.t[0:17, col(j)], rhs=wup.t[0:17, :], start=True, stop=True)], rT.b + wup.b, P.b)
                    kb.op(act, lambda h, P=P, j=j: h.activation(out=Lt.t[0:TS, j, :], in_=P.t[0:TS, 0:512], func=AF.Exp, scale=-1.0), P.b, [Lt.b[j]] + ([cv.b[2 * j], cv.b[2 * j + 1]] if B["cv_al"] else []))
                    rel(P)
                    kb.op(act, lambda h, j=j: h.activation(out=Lt.t[0:TS, j, :], in_=Lt.t[0:TS, j, :], func=AF.Ln, bias=1.0), [Lt.b[j]], [Lt.b[j]])
            vfill(100)
            slabK = get_slab()

            def p_copy(j):
                kb.op(act, lambda h: h.activation(out=pb.t[0:TS, :], in_=pt.t[0:TS, j, :], func=AF.Copy), [pt.b[0]], pb.b)

            def p_tr(j):
                kb.mm([lambda h, kc=kc: h.transpose(out=PT.t[:, kc, 0:TS], in_=pb.t[0:TS, kc * 128:(kc + 1) * 128], identity=ident_b[0:TS, 0:TS]) for kc in range(2)], pb.b + [cstb.b[0]], PT.b)
                kb.op(dve, lambda h: h.tensor_copy(out=pT.t[:, :, col(j)], in_=PT.t[:, 0:2, 0:TS]), PT.b, pT.b)
            for hd in range(4):
                if pre:
                    enb = TB(Enb_s.t[:, hd, :], [Enb_s.b[hd]])
                else:
                    P = acq()
                    kb.mm([lambda h, P=P, j=j, hd=hd: h.matmul(P.t[:, col(j)], lhsT=Lt.t[0:TS, j, hd * 128:(hd + 1) * 128], rhs=tc_[0:TS, 0:TS], start=True, stop=True) for j in range(NS)],
                          Lt.b[0:NS] + [cst.b[0]], P.b)
                    enb = ft()
                    kb.op(act, lambda h, P=P, hd=hd: h.activation(out=Eb.t[:, hd, :], in_=P.t[:, 0:T], func=AF.Exp), P.b, [Eb.b[hd]])
                    kb.op(act, lambda h, P=P, enb=enb: h.activation(out=enb.t[:, 0:T], in_=P.t[:, 0:T], func=AF.Exp, scale=-1.0), P.b, enb.b)
                    rel(P)
                P = acq()
                kb.mm([lambda h, kc=kc, hd=hd, P=P: h.matmul(P.t[:, 0:T], lhsT=slabK.t[:, kc * 512 + hd * 128: kc * 512 + (hd + 1) * 128], rhs=hT.t[:, kc, :], start=(kc == 0), stop=(kc == 7)) for kc in range(8)],
                      slabK.b + hT.b, P.b)
                kb.op(dve, lambda h, P=P, hd=hd, enb=enb: h.tensor_tensor(out=ktT.t[:, hd, :], in0=P.t[:, 0:T], in1=enb.t[:, 0:T], op=ALU.mult), P.b + enb.b, [ktT.b[hd]])
                rel(P)
            if sample:
                for j in range(NS):
                    if pre:
                        ed = ED_s
                    else:
                        P = acq()
                        kb.mm([lambda h, P=P, j=j: h.matmul(P.t[0:TS, 0:512], lhsT=tu_[0:TS, 0:TS], rhs=Lt.t[0:TS, j, :], start=True, stop=True)], [Lt.b[j], cst.b[0]], P.b)
                        ed = ft()
                        kb.op(act, lambda h, P=P, ed=ed: h.activation(out=ed.t[0:TS, :], in_=P.t[0:TS, 0:512], func=AF.Exp), P.b, ed.b)
                        rel(P)
                    proj_tm(slabK, hT, hT.b, j, lambda P, j=j, ed=ed: kb.op(dve, lambda h: h.tensor_tensor(out=kdec.t[0:TS, j, :], in0=P.t[0:TS, 0:512], in1=ed.t[0:TS, :], op=ALU.mult), P.b + ed.b, [kdec.b[j]]))
            slabQ = get_slab()
            proj_fm(slabQ, 4, hT, hT.b, lambda i, P: kb.op(dve, lambda h: h.scalar_tensor_tensor(out=qtT.t[:, i, :], in0=P.t[:, 0:T], scalar=float(128 ** -0.5), in1=Eb.t[:, i, :], op0=ALU.mult, op1=ALU.mult), P.b + [Eb.b[i]], [qtT.b[i]]))
            if not sample:
                def kd_rescale(j):
                    kd = sgs[j % 2]
                    for hd in range(4):
                        kb.op(dve, lambda h, hd=hd, kd=kd: h.tensor_scalar(out=kd.t[:, hd * TS:(hd + 1) * TS], in0=ktT.t[:, hd, col(j)], scalar1=Eb.t[:, hd, j * TS + TS - 1:j * TS + TS], scalar2=None, op0=ALU.mult),
                              [ktT.b[hd], Eb.b[hd]], kd.b)

                def kd_tr(j):
                    kd = sgs[j % 2]
                    kb.mm([lambda h, hd=hd: h.transpose(out=PT.t[:, hd, 0:TS], in_=kd.t[:, hd * TS:(hd + 1) * TS], identity=ident_b) for hd in range(4)], kd.b + [cstb.b[0]], PT.b)
                    kb.op(act, lambda h: h.activation(out=kdec.t[0:TS, j, :].rearrange("p (a b) -> p a b", a=4), in_=PT.t[:, 0:4, 0:TS], func=AF.Copy), PT.b, [kdec.b[j]])
                kd_sched = [lambda: (kd_rescale(0), kd_rescale(1)), lambda: (kd_tr(0), kd_rescale(2)), lambda: (kd_tr(1), kd_rescale(3)), lambda: kd_tr(2), lambda: kd_tr(3)]
            else:
                kd_sched = []
            for half in range(2):
                sl_ = get_slab()
                for i2 in range(2):
                    if kd_sched:
                        kd_sched.pop(0)()
                    proj_fm(sl_, 2, hT, hT.b, lambda i, P, half=half, i2=i2: kb.op(act, lambda h: h.activation(out=zaT.t[:, half * 4 + i2 * 2 + i, :], in_=P.t[:, 0:T], func=AF.Silu), P.b, [zaT.b[half * 4 + i2 * 2 + i]]), lhs_off=i2 * 256, hookb=True)
            while kd_sched:
                kd_sched.pop(0)()

            if sample:
                qpad, kdpad, S0f, S0ball, Sout = B["qpad"], B["kdpad"], B["S0f"], B["S0b_all"], B["Sout"]
                for hd in range(4):
                    kb.op(dve, lambda h, hd=hd: h.tensor_tensor(out=qpad.t[:, hd, :, :], in0=qtT.t[:, hd, 0:64].unsqueeze(1).broadcast_to([128, 16, 64]), in1=colmask.rearrange("p (a b) -> p a b", a=16), op=ALU.mult),
                          [qtT.b[hd], cstb.b[0]], [qpad.b[hd]])
                for sq_ in range(NSEQ_S):
                    kb.op(act, lambda h, sq_=sq_: h.activation(out=kdpad.t[0:64, sq_, :], in_=kdec.t[0:64, 0, :], func=AF.Copy, scale=rowmask[0:64, sq_:sq_ + 1]), [kdec.b[0], cst.b[0]], [kdpad.b[sq_]])
            if pending_builds or (tok0 == 0 and not sample):
                while pending_builds:
                    pending_builds.pop(0)()
                if tok0 == 0 and not sample:
                    kb.op(dve, lambda h: h.memset(u2T.t[:], 0.0), [], u2T.b)
            sso, rso = small.b[2], small.b[3]
            fillers = []
            if sample:
                uview = lambda c, a, b_: u2T.t[:, c, a * 16:b_ * 16]

            def ev_u(c, P, sgx):
                if sample:
                    kb.op(dve, lambda h: h.scalar_tensor_tensor(out=uview(c, 30, 34), in0=sgx.t[:, 0:64], scalar=1.0, in1=P.t[:, 0:64], op0=ALU.add, op1=ALU.mult), P.b + sgx.b, [u2T.b[c]])
                    kb.op(dve, lambda h: h.scalar_tensor_tensor(out=uf.t[:, c, 0:64], in0=sgx.t[:, 0:64], scalar=1.0, in1=P.t[:, 0:64], op0=ALU.add, op1=ALU.mult), P.b + sgx.b, [uf.b[c]])
                else:
                    kb.op(dve, lambda h: h.scalar_tensor_tensor(out=u2T.t[:, c, 30:30 + T], in0=sgx.t[:, 0:T], scalar=1.0, in1=P.t[:, 0:T], op0=ALU.add, op1=ALU.mult), P.b + sgx.b, [u2T.b[c]])
                    if last:
                        kb.op(dve, lambda h: h.scalar_tensor_tensor(out=uf.t[:, c, 0:32], in0=sgx.t[:, T - 32:T], scalar=1.0, in1=P.t[:, T - 32:T], op0=ALU.add, op1=ALU.mult), P.b + sgx.b, [uf.b[c]])

            hold = {}

            def glu_filler(half, i):
                ii = i % 2
                if ii == 0:
                    hold["s"] = get_slab()
                slg = sla = hold["s"]
                P = acq()
                kb.mm([lambda h, kc=kc, P=P: h.matmul(P.t[:, 0:T], lhsT=slg.t[:, kc * 512 + ii * 128: kc * 512 + (ii + 1) * 128], rhs=hT.t[:, kc, :], start=(kc == 0), stop=(kc == 7)) for kc in range(8)], slg.b + hT.b, P.b)
                sgx = nxt("sg", sgs)
                kb.op(act, lambda h, P=P: h.activation(out=sgx.t[:, 0:T], in_=P.t[:, 0:T], func=AF.Tanh, scale=0.5), P.b, sgx.b)
                rel(P)
                P = acq()
                kb.mm([lambda h, kc=kc, P=P: h.matmul(P.t[:, 0:T], lhsT=sla.t[:, kc * 512 + (2 + ii) * 128: kc * 512 + (3 + ii) * 128], rhs=hT.t[:, kc, :], start=(kc == 0), stop=(kc == 7)) for kc in range(8)], sla.b + hT.b, P.b)
                ev_u(half * 4 + i, P, sgx)
                rel(P)

            cst_ = {}

            def conv_filler(c):
                if c == 0:
                    cst_["Pm"], cst_["Pq"] = acq(), acq()
                Pm, Pq = cst_["Pm"], cst_["Pq"]
                dg = get_slab()
                P = acq()
                if sample:
                    fns = [lambda h, tap=tap, P=P, dg=dg, c=c: h.matmul(P.t[:, 0:64], lhsT=dg.t[:, tap * 128:(tap + 1) * 128], rhs=uview(c, tap, tap + 4), start=(tap == 0), stop=(tap == 30)) for tap in range(31)]
                else:
                    fns = [lambda h, tap=tap, P=P, dg=dg, c=c: h.matmul(P.t[:, 0:T], lhsT=dg.t[:, tap * 128:(tap + 1) * 128], rhs=u2T.t[:, c, tap:tap + T], start=(tap == 0), stop=(tap == 30)) for tap in range(31)]
                kb.mm(fns, dg.b + [u2T.b[c]], P.b)
                kb.op(act, lambda h, P=P, c=c: h.activation(out=cv.t[:, c, :], in_=P.t[:, 0:T], func=AF.Identity, bias=bdwT[:, c:c + 1]), P.b + [vec.b[0]], [cv.b[c]] + ([B["cv_al"][c]] if B["cv_al"] else []))
                sq = nxt("sq", sqs)
                kb.op(act, lambda h, P=P, c=c, sq=sq: h.activation(out=sq.t[:, 0:T], in_=P.t[:, 0:T], func=AF.Square, bias=bdwT[:, c:c + 1]), P.b + [vec.b[0]], sq.b)
                rel(P)
                if "pend" in cst_:
                    cst_.pop("pend")()

                def stat_mm(c=c, sq=sq):
                    kb.mm([lambda h: h.matmul(Pm.t[:, 0:T], lhsT=ones_b, rhs=cv.t[:, c, :], start=(c == 0), stop=(c == 7))], [cv.b[c], cstb.b[0]], Pm.b)
                    kb.mm([lambda h: h.matmul(Pq.t[:, 0:T], lhsT=ones_b, rhs=sq.t[:, 0:T], start=(c == 0), stop=(c == 7))], sq.b + [cstb.b[0]], Pq.b)
                cst_["pend"] = stat_mm
                if not sample and not last:
                    kb.op(act, lambda h, c=c: h.activation(out=u2T.t[:, c, 0:30], in_=u2T.t[:, c, T:T + 30], func=AF.Copy), [u2T.b[c]], [u2T.b[c]])

            for half in range(2):
                for i in range(4):
                    fillers.append(lambda half=half, i=i: glu_filler(half, i))
            for c in range(8):
                fillers.append(lambda c=c: conv_filler(c))

            sfillers = []

            def sfill(n):
                while n > 0 and sfillers:
                    sfillers.pop(0)()
                    n -= 1

            def fill(n, limit=0):
                while n > 0 and len(fillers) > limit:
                    fillers.pop(0)()
                    n -= 1

            for j in range(NS):
                Pa = acq()
                kb.mm([lambda h, hd=hd, Pa=Pa, j=j: h.matmul(Pa.t[0:TS, hd * TS:(hd + 1) * TS], lhsT=ktT.t[:, hd, col(j)], rhs=qtT.t[:, hd, col(j)], start=True, stop=True) for hd in range(4)],
                      ktT.b[0:4] + qtT.b[0:4], Pa.b)
                mk_ = mask4_s[0:64, :] if sample else mask4
                kb.op(dve, lambda h, Pa=Pa: h.tensor_tensor(out=attm.t[0:TS, 0:4 * TS], in0=Pa.t[0:TS, 0:4 * TS], in1=mk_[0:TS, 0:4 * TS], op=ALU.mult), Pa.b + [cstb.b[0]], attm.b)
                rel(Pa)
                fill(1)
                if not sample:
                    Po = [acq(), acq()]
                    oview = lambda hd: Po[hd // 2].t[0:TS, (hd % 2) * 256:(hd % 2 + 1) * 256]
                    obuf = lambda hd: Po[hd // 2].b
                    for hd in range(4):
                        kb.mm([lambda h, hd=hd: h.matmul(oview(hd), lhsT=qtT.t[:, hd, col(j)], rhs=Sb.t[:, hd, :], start=True, stop=False),
                               lambda h, hd=hd: h.matmul(oview(hd), lhsT=attm.t[0:TS, hd * TS:(hd + 1) * TS], rhs=vt.t[0:TS, j, hd * 256:(hd + 1) * 256], start=False, stop=True)],
                              [qtT.b[hd]] + Sb.b + attm.b + [vt.b[2 * j], vt.b[2 * j + 1]], obuf(hd))
                    Pd = [acq(), acq()]
                    for hd in range(4):
                        kb.mm([lambda h, hd=hd: h.matmul(Pd[hd // 2].t[:, (hd % 2) * 256:(hd % 2 + 1) * 256], lhsT=kdec.t[0:TS, j, hd * 128:(hd + 1) * 128], rhs=vt.t[0:TS, j, hd * 256:(hd + 1) * 256], start=True, stop=True)],
                              [kdec.b[j], vt.b[2 * j], vt.b[2 * j + 1]], Pd[hd // 2].b)
                else:
                    Po = [acq(), acq(), acq(), acq()]
                    oview = lambda hd: Po[hd].t[0:TS, 0:256]
                    obuf = lambda hd: Po[hd].b

                    def ld_s0(q_):
                        rr = q_ % len(S0f)
                        kb.dma(pool, S0f[rr].t[:], sgla[q_].rearrange("h k d -> k h d"), [], S0f[rr].b, S0f[rr].b[0])
                    for q_ in range(len(S0f)):
                        ld_s0(q_)
                    for sq_ in range(NSEQ_S):
                        kb.mm([lambda h, hd=hd, sq_=sq_: h.matmul(oview(hd), lhsT=qpad.t[:, hd, sq_, :], rhs=S0ball.t[:, sq_, hd * 256:(hd + 1) * 256], start=(sq_ == 0), stop=False) for hd in range(4)],
                              qpad.b + [S0ball.b[sq_]], [Po[0].b[0], Po[1].b[0], Po[2].b[0], Po[3].b[0]])

                        def supd(sq_=sq_):
                            r_ = sq_ % len(S0f)
                            Pd = [acq(), acq()]
                            for hd in range(4):
                                kb.mm([lambda h, hd=hd: h.matmul(Pd[hd // 2].t[:, (hd % 2) * 256:(hd % 2 + 1) * 256], lhsT=kdpad.t[0:64, sq_, hd * 128:(hd + 1) * 128], rhs=vt.t[0:64, 0, hd * 256:(hd + 1) * 256], start=True, stop=True)],
                                      [kdpad.b[sq_], vt.b[0], vt.b[1]], Pd[hd // 2].b)
                            so = Sout[sq_ % len(Sout)]
                            for hd in range(4):
                                kb.op(dve, lambda h, hd=hd: h.scalar_tensor_tensor(out=so.t[:, hd, :], in0=S0f[r_].t[:, hd, :], scalar=Eb.t[:, hd, 48 + sq_:49 + sq_], in1=Pd[hd // 2].t[:, (hd % 2) * 256:(hd % 2 + 1) * 256], op0=ALU.mult, op1=ALU.add),
                                      S0f[r_].b + [Eb.b[hd]] + Pd[hd // 2].b, so.b)
                            rel(Pd[0]); rel(Pd[1])
                            kb.dma(pool, gss[sq_].rearrange("h k d -> k h d"), so.t[:], so.b, [], B["soh"][sq_ % len(Sout)])
                            if sq_ + len(S0f) < NSEQ_S:
                                ld_s0(sq_ + len(S0f))
                        sfillers.append(supd)
                        if sq_ % 4 == 3:
                            fill(1)
                    for hd in range(4):
                        kb.mm([lambda h, hd=hd: h.matmul(oview(hd), lhsT=attm.t[0:TS, hd * TS:(hd + 1) * TS], rhs=vt.t[0:TS, 0, hd * 256:(hd + 1) * 256], start=False, stop=True)],
                              attm.b + [vt.b[0], vt.b[1]], obuf(hd))
                for hd in range(4):
                    kb.op(act, lambda h, hd=hd: h.activation(out=junk.t[0:TS, 0:256], in_=oview(hd), func=AF.Square, accum_out=smallv(2)[0:TS, hd:hd + 1]), obuf(hd), [junk.b[0], sso])
                kb.op(pool, lambda h: h.tensor_scalar(out=smallv(3)[0:TS, :], in0=smallv(2)[0:TS, :], scalar1=1.0 / 256, scalar2=EPS, op0=ALU.mult, op1=ALU.add), [sso], [rso])
                kb.op(pool, lambda h: h.tensor_tensor(out=smallv(3)[0:TS, :], in0=smallv(3)[0:TS, :], in1=neghalf[0:TS, 0:4], op=ALU.pow), [rso, cst.b[0]], [rso])
                for hd in range(4):
                    kb.op(act, lambda h, hd=hd: h.activation(out=on.t[0:TS, hd * 256:(hd + 1) * 256], in_=oview(hd), func=AF.Copy, scale=smallv(3)[0:TS, hd:hd + 1]), obuf(hd) + [rso], on.b)
                for p_ in Po:
                    rel(p_)
                if not sample:
                    for hd in range(4):
                        kb.op(dve, lambda h, hd=hd: h.scalar_tensor_tensor(out=S.t[:, hd, :], in0=S.t[:, hd, :], scalar=Eb.t[:, hd, j * TS + TS - 1:j * TS + TS], in1=Pd[hd // 2].t[:, (hd % 2) * 256:(hd % 2 + 1) * 256], op0=ALU.mult, op1=ALU.add),
                              S.b + [Eb.b[hd]] + Pd[hd // 2].b, S.b)
                    rel(Pd[0]); rel(Pd[1])
                    kb.op(act, lambda h: h.activation(out=Sb.t[:], in_=S.t[:], func=AF.Copy), S.b, Sb.b)
                fill(3)
                kb.mm([lambda h, kc=kc: h.transpose(out=PT.t[:, kc, 0:TS], in_=on.t[0:TS, kc * 128:(kc + 1) * 128], identity=ident_b[0:TS, 0:TS]) for kc in range(8)], on.b + [cstb.b[0]], PT.b)
                for c in range(2):
                    kb.op(dve, lambda h, c=c: h.scalar_tensor_tensor(out=ogT.t[:, c::2, col(j)], in0=PT.t[:, c::2, 0:TS], scalar=ggla[:, c:c + 1], in1=zaT.t[:, c::2, col(j)], op0=ALU.mult, op1=ALU.mult),
                          PT.b + [vec.b[0]] + zaT.b[c::2], ogT.b[c::2])
            while fillers:
                fill(2)
                sfill(1)
            hook()
            if last:
                kb.dma(pool, gsp.rearrange("h k d -> k h d"), S.t[:], S.b, [], Buf("gsph"))
            if sample or last:
                nr = 64 if sample else 32
                Pu = [acq(), acq()]
                for c in range(8):
                    kb.mm([lambda h, c=c: h.transpose(out=Pu[c // 4].t[0:nr, (c % 4) * 128:(c % 4 + 1) * 128], in_=uf.t[:, c, 0:nr], identity=ident_f)], [uf.b[c], cst.b[0]], Pu[c // 4].b)
                for q_ in range(2):
                    kb.op(act, lambda h, q_=q_: h.activation(out=ut.t[0:nr, q_ * 512:(q_ + 1) * 512], in_=Pu[q_].t[0:nr, 0:512], func=AF.Copy, scale=0.5), Pu[q_].b, ut.b)
                rel(Pu[0]); rel(Pu[1])
                if sample:
                    for i in range(4):
                        kb.dma(pool, css[:, 26 + i, :], ut.t[i * 16:(i + 1) * 16, :], ut.b, [], B["uth"])
                else:
                    kb.dma(pool, csp[:, :], ut.t[2:32, :], ut.b, [], B["uth"])

            cst_.pop("pend")()
            Pm, Pq = cst_["Pm"], cst_["Pq"]
            kb.op(act, lambda h: h.activation(out=meanb.t[:, 0:T], in_=Pm.t[:, 0:T], func=AF.Copy, scale=1.0 / D), Pm.b, meanb.b)
            msq = ft()
            kb.op(dve, lambda h: h.tensor_tensor(out=msq.t[:, 0:T], in0=meanb.t[:, 0:T], in1=meanb.t[:, 0:T], op=ALU.mult), meanb.b, msq.b)
            kb.op(dve, lambda h: h.scalar_tensor_tensor(out=rstdb.t[:, 0:T], in0=Pq.t[:, 0:T], scalar=1.0 / D, in1=msq.t[:, 0:T], op0=ALU.mult, op1=ALU.subtract), Pq.b + msq.b, rstdb.b)
            rel(Pm); rel(Pq)
            kb.op(act, lambda h: h.activation(out=rstdb.t[:, 0:T], in_=rstdb.t[:, 0:T], func=AF.Ln, bias=epsb[:, 0:1]), rstdb.b + [cst.b[0]], rstdb.b)
            kb.op(act, lambda h: h.activation(out=rstdb.t[:, 0:T], in_=rstdb.t[:, 0:T], func=AF.Exp, scale=-0.5), rstdb.b, rstdb.b)
            sfill(1)
            for half in range(2):
                sl_ = get_slab()
                proj_fm(sl_, 4, hT, hT.b, lambda i, P, half=half: kb.op(act, lambda h: h.activation(out=zbT.t[:, half * 4 + i, :], in_=P.t[:, 0:T], func=AF.Silu), P.b, [zbT.b[half * 4 + i]]))
            def ln_ab(c):
                tmp = ft()
                kb.op(dve, lambda h: h.tensor_tensor(out=tmp.t[:, 0:T], in0=cv.t[:, c, :], in1=meanb.t[:, 0:T], op=ALU.subtract), [cv.b[c]] + meanb.b, tmp.b)
                kb.op(dve, lambda h: h.tensor_tensor(out=tmp.t[:, 0:T], in0=tmp.t[:, 0:T], in1=rstdb.t[:, 0:T], op=ALU.mult), tmp.b + rstdb.b, tmp.b)
                sl = sls[c % len(sls)]
                kb.op(act, lambda h: h.activation(out=sl.t[:, 0:T], in_=tmp.t[:, 0:T], func=AF.Silu, scale=glnT[:, c:c + 1], bias=blnT[:, c:c + 1]), tmp.b + [vec.b[0]], sl.b)

            def ln_c(c):
                sl = sls[c % len(sls)]
                kb.op(dve, lambda h: h.tensor_tensor(out=zbT.t[:, c, :], in0=sl.t[:, 0:T], in1=zbT.t[:, c, :], op=ALU.mult), sl.b + [zbT.b[c]], [zbT.b[c]])

            def gate_group(slab_, i, dst, di):
                P = acq()
                kb.mm([lambda h, kc=kc, P=P: h.matmul(P.t[:, 0:T], lhsT=slab_.t[:, kc * 512 + i * 128: kc * 512 + (i + 1) * 128], rhs=hT.t[:, kc, :], start=(kc == 0), stop=(kc == 7)) for kc in range(8)], slab_.b + hT.b, P.b)
                kb.op(act, lambda h, P=P: h.activation(out=dst.t[:, di, :], in_=P.t[:, 0:T], func=AF.Tanh, scale=0.5), P.b, [dst.b[di]])
                rel(P)

            sA0 = get_slab(1)
            sA1 = get_slab(2)
            for c in range(8):
                ln_ab(c)
                if c >= 1:
                    ln_c(c - 1)
                gate_group(sA0 if c < 4 else sA1, c % 4, gaT, c)
                if c % 4 == 3:
                    sfill(1)
            ln_c(7)
            for half in range(2):
                sa = get_slab()
                for i in range(4):
                    oc = half * 4 + i
                    Pa = acq()
                    kb.mm([lambda h, kc=kc, i=i, Pa=Pa: h.matmul(Pa.t[:, 0:T], lhsT=sa.t[:, kc * 512 + i * 128: kc * 512 + (i + 1) * 128], rhs=ogT.t[:, kc, :], start=(kc == 0), stop=(kc == 7)) for kc in range(8)], sa.b + ogT.b, Pa.b)
                    kb.op(dve, lambda h, Pa=Pa, oc=oc: h.scalar_tensor_tensor(out=mTb.t[:, oc, :], in0=gaT.t[:, oc, :], scalar=1.0, in1=Pa.t[:, 0:T], op0=ALU.add, op1=ALU.mult), Pa.b + [gaT.b[oc]], [mTb.b[oc]])
                    rel(Pa)
                sfill(1)
            hook()
            for half in range(2):
                s2 = get_slab()
                for i in range(4):
                    g_ = half * 4 + i
                    if g_ < NS:
                        p_copy(g_)
                    gate_group(s2, i, gbT, half * 4 + i)
                    if g_ < NS:
                        p_tr(g_)
            sfill(1)
            if next_pro is not None:
                next_pro[1]()
                next_pro[2](0); next_pro[2](1)
            for half in range(2):
                sbb = get_slab()
                for i in range(4):
                    oc = half * 4 + i
                    Pb = acq()
                    kb.mm([lambda h, kc=kc, i=i, Pb=Pb: h.matmul(Pb.t[:, 0:T], lhsT=sbb.t[:, kc * 512 + i * 128: kc * 512 + (i + 1) * 128], rhs=zbT.t[:, kc, :], start=(kc == 0), stop=(kc == 7)) for kc in range(8)], sbb.b + zbT.b, Pb.b)
                    tb_ = ft()
                    kb.op(dve, lambda h, Pb=Pb, oc=oc, tb_=tb_: h.scalar_tensor_tensor(out=tb_.t[:, 0:T], in0=gbT.t[:, oc, :], scalar=1.0, in1=Pb.t[:, 0:T], op0=ALU.add, op1=ALU.mult), Pb.b + [gbT.b[oc]], tb_.b)
                    rel(Pb)
                    kb.op(dve, lambda h, oc=oc, tb_=tb_: h.tensor_tensor(out=mTb.t[:, oc, :], in0=tb_.t[:, 0:T], in1=mTb.t[:, oc, :], op=ALU.add), tb_.b + [mTb.b[oc]], [mTb.b[oc]])
            np_ = next_pro

            def wo(so_, half, j):
                proj_tm(so_, mTb, mTb.b, j, lambda P: kb.op(dve, lambda h: h.scalar_tensor_tensor(out=xt.t[0:TS, j, half * 512:(half + 1) * 512], in0=P.t[0:TS, 0:512], scalar=0.5, in1=xt.t[0:TS, j, half * 512:(half + 1) * 512], op0=ALU.mult, op1=ALU.add), P.b + [xt.b[j]], [xt.b[j]]))

            x1bh = {}

            def x1_copy(j):
                x1b = nxt("x1b", x1bs)
                x1bh[j] = x1b
                kb.op(act, lambda h: h.activation(out=x1b.t[0:TS, :], in_=xt.t[0:TS, j, :], func=AF.Copy), [xt.b[j]], x1b.b)

            def x1_tr(j):
                x1b = x1bh[j]
                kb.mm([lambda h, kc=kc: h.transpose(out=PT.t[:, kc, 0:TS], in_=x1b.t[0:TS, kc * 128:(kc + 1) * 128], identity=ident_b[0:TS, 0:TS]) for kc in range(8)], x1b.b + [cstb.b[0]], PT.b)
                kb.op(dve, lambda h: h.tensor_copy(out=x1T.t[:, :, col(j)], in_=PT.t[:, :, 0:TS]), PT.b, x1T.b)

            sfill(2)
            so0 = get_slab()
            for j in range(NS):
                wo(so0, 0, j)
                if np_ is not None and j == 1:
                    np_[3](0); np_[3](1); np_[2](2); np_[2](3)
            if np_ is not None:
                np_[3](2); np_[3](3)
                if len(np_) > 4:
                    np_[4]()
            so1 = get_slab()
            hook()
            for j in range(NS):
                wo(so1, 1, j)
                x1_copy(j)
                if j >= 1:
                    x1_tr(j - 1)
            x1_tr(NS - 1)
            sfill(2)
            sg0, sg1, spe = get_slab(1), get_slab(2), get_slab(3)
            for half in range(2):
                sgl = sg0 if half == 0 else sg1
                for j in range(NS):
                    sgt = ft()
                    proj_tm(sgl, x1T, x1T.b, j, lambda P, sgt=sgt: kb.op(act, lambda h: h.activation(out=sgt.t[0:TS, :], in_=P.t[0:TS, 0:512], func=AF.Tanh, scale=0.5), P.b, sgt.b))
                    P = acq()
                    kb.mm([lambda h, kc=kc, P=P, j=j, half=half: h.matmul(P.t[0:TS, 0:512], lhsT=pT.t[:, kc, col(j)], rhs=spe.t[:, kc * 1024 + half * 512: kc * 1024 + (half + 1) * 512], start=(kc == 0), stop=(kc == 1)) for kc in range(2)],
                          spe.b + pT.b, P.b)
                    kb.op(dve, lambda h, P=P, sgt=sgt: h.scalar_tensor_tensor(out=sgt.t[0:TS, :], in0=sgt.t[0:TS, :], scalar=1.0, in1=P.t[0:TS, 0:512], op0=ALU.add, op1=ALU.mult), P.b + sgt.b, sgt.b)
                    rel(P)
                    kb.op(dve, lambda h, j=j, half=half, sgt=sgt: h.scalar_tensor_tensor(out=xt.t[0:TS, j, half * 512:(half + 1) * 512], in0=sgt.t[0:TS, :], scalar=0.5, in1=xt.t[0:TS, j, half * 512:(half + 1) * 512], op0=ALU.mult, op1=ALU.add), [xt.b[j]] + sgt.b, [xt.b[j]])
            sfill(100)
            hook()
            ss2, rs2 = small.b[4], small.b[5]
            for j in range(NS):
                kb.op(act, lambda h, j=j: h.activation(out=junk.t[0:TS, :], in_=xt.t[0:TS, j, :], func=AF.Square, accum_out=smallv(4)[0:TS, j:j + 1]), [xt.b[j]], [junk.b[0], ss2])
            kb.op(pool, lambda h: h.tensor_scalar(out=smallv(5)[0:TS, 0:NS], in0=smallv(4)[0:TS, 0:NS], scalar1=1.0 / D, scalar2=EPS, op0=ALU.mult, op1=ALU.add), [ss2], [rs2])
            kb.op(pool, lambda h: h.tensor_tensor(out=smallv(5)[0:TS, 0:NS], in0=smallv(5)[0:TS, 0:NS], in1=neghalf[0:TS, 0:NS], op=ALU.pow), [rs2, cst.b[0]], [rs2])
            for j in range(NS):
                kb.op(dve, lambda h, j=j: h.scalar_tensor_tensor(out=xt.t[0:TS, j, :], in0=xt.t[0:TS, j, :], scalar=smallv(5)[0:TS, j:j + 1], in1=gfin_b[0:TS, :], op0=ALU.mult, op1=ALU.mult), [xt.b[j], rs2, vec.b[0]], [xt.b[j]])
                kb.dma(pool, ydst[tok0 + j * TS:tok0 + (j + 1) * TS, :], xt.t[0:TS, j, :], [xt.b[j]], [], B["yh"])
        def alloc_bufs(es_t, T, TS, NS, sample):
            B = {}
            mk = lambda shape, dt, nb=1: sbt(es_t, shape, dt, nb)
            if not sample:
                A1 = mk([128, 8, T], F32, 8)
                A2 = mk([128, 16, T], BF16, 16)
                A3 = mk([128, 12, T], BF16, 12)
                B["Lt"] = TB(A1.t[:, 0:4, :], A1.b[0:4])
                B["Eb"] = TB(A1.t[:, 4:8, :], A1.b[4:8])
                B["cv"] = TB(A1.t[:, 0:4, :].bitcast(BF16).rearrange("p a (two c) -> p (a two) c", two=2), [Buf(f"cv{c_}") for c_ in range(8)])
                B["cv_al"] = [A1.b[c_ // 2] for c_ in range(8)]
                B["ktT"] = TB(A2.t[:, 0:4, :], A2.b[0:4])
                B["qtT"] = TB(A2.t[:, 4:8, :], A2.b[4:8])
                B["zaT"] = TB(A2.t[:, 8:16, :], A2.b[8:16])
                B["zbT"] = TB(A2.t[:, 0:8, :], A2.b[0:8])
                B["mTb"] = TB(A2.t[:, 8:16, :], A2.b[8:16])
                B["kdec"] = TB(A3.t[:, 0:4, :], A3.b[0:4])
                B["vt"] = TB(A3.t[:, 4:12, :].rearrange("p (j two) c -> p j (two c)", two=2), A3.b[4:12])
                B["gaT"] = TB(A3.t[:, 0:8, :], A3.b[0:8])
                B["gbT"] = TB(A3.t[:, 0:8, :], A3.b[0:8])
                B["A2"] = A2
            else:
                B["Lt"] = Lt_s
                B["Eb"] = Eb_s
                B["cv"] = mk([128, 8, T], BF16, 8)
                B["cv_al"] = None
                B["ktT"] = mk([128, 4, T], BF16, 4)
                B["qtT"] = mk([128, 4, T], BF16, 4)
                B["zaT"] = mk([128, 8, T], BF16, 8)
                B["zbT"] = mk([128, 8, T], BF16, 8)
                B["mTb"] = mk([128, 8, T], BF16, 8)
                B["kdec"] = mk([128, 1, 512], BF16, 1)
                B["vt"] = mk([128, 1, 1024], BF16, 2)
                B["gaT"] = mk([128, 8, T], BF16, 8)
                B["gbT"] = mk([128, 8, T], BF16, 8)
                B["qpad"] = mk([128, 4, 16, 64], BF16, 4)
                B["kdpad"] = mk([128, 16, 512], BF16, 16)
                B["S0f"] = [mk([128, 4, 256], F32) for _ in range(3)]
                B["S0b_all"] = mk([128, 16, 1024], BF16, 16)
                B["Sout"] = [mk([128, 4, 256], F32) for _ in range(2)]
                B["soh"] = [Buf("soh0"), Buf("soh1")]
            B["xt"] = [xt_s] if sample else [mk([128, NS, D], F32, NS) for _ in range(2)]
            B["x1T"] = TB(B["A2"].t[:, 0:8, :], B["A2"].b[0:8]) if not sample else mk([128, 8, T], BF16, 1)
            B["sgs"] = [mk([128, T], BF16) for _ in range(2)]
            B["FT"] = mk([128, 4, 512], F32, 4)
            B["pt"] = mk([128, NS, 256], F32, 1)
            B["hT"] = hT_s if sample else mk([128, 8, T], BF16, 1)
            B["hb"] = [mk([128, D], BF16) for _ in range(2)]
            B["rT"] = rT_s if sample else mk([32, T], BF16, 1)
            B["attm"] = mk([128, 512], BF16, 1)
            B["on"] = mk([128, D], BF16, 1)
            B["ogT"] = mk([128, 8, T], BF16, 8)
            B["u2T"] = u2T_s if sample else mk([128, 8, 30 + T + 2], BF16, 8)
            B["sq"] = [mk([128, T], BF16) for _ in range(3)]
            B["meanb"] = mk([128, T], F32)
            B["rstdb"] = mk([128, T], F32)
            B["sl"] = [mk([128, T], BF16) for _ in range(2)]
            B["pb"] = mk([128, 256], BF16)
            B["pT"] = mk([128, 2, T], BF16, 1)
            B["uf"] = mk([128, 8, 64], F32, 8)
            B["ut"] = TB(B["FT"].t[:, 0:2, :].rearrange("p a b -> p (a b)"), B["FT"].b[0:2])
            B["yh"] = Buf("yh_s" if sample else "yh_p")
            B["uth"] = Buf("uth_s" if sample else "uth_p")
            if not sample:
                kb.op(dve, lambda h: h.memset(B["rT"].t[:], 1.0), [], B["rT"].b)
            if not sample:
                kb.op(dve, lambda h: h.memset(B["u2T"].t[:], 0.0), [], B["u2T"].b)
            return B

        with ExitStack() as es_p:
            B = alloc_bufs(es_p, 512, 128, 4, False)
            xts = B["xt"]
            tile_pro_load(B, 512, 128, 4, x_p, 0, xts[0])
            issue_slabs(NB)
            tile_pro_compute(B, 512, 128, 4, xts[0])
            def mk_build(c, piece):
                def f():
                    stg_t = (B["u2T"] if c % 2 == 0 else B["ogT"])
                    stg = stg_t.t.rearrange("p a b -> p (a b)")[:, 0:DG_E]
                    t0, t1 = piece * 8, min(31, piece * 8 + 8)
                    kb.op(dve, lambda h: h.scalar_tensor_tensor(out=stg.rearrange("p (t k) -> p t k", k=128)[:, t0:t1, :],
                                                                in0=ident_b.unsqueeze(1).broadcast_to([128, t1 - t0, 128]), scalar=0.5,
                                                                in1=wdwT[:, c * 31 + t0:c * 31 + t1].unsqueeze(2).broadcast_to([128, t1 - t0, 128]), op0=ALU.mult, op1=ALU.mult),
                          [cstb.b[0], vec.b[0]], stg_t.b)
                    if piece == 3:
                        kb.dma(sp, wdg[c], stg, stg_t.b, [diagb[c]], diagb[c])
                return f
            for c in range(8):
                for piece in range(4):
                    pending_builds.append(mk_build(c, piece))
            kb.op(dve, lambda h: h.memset(u2T_s.t[:], 0.0), [], u2T_s.b)

            def cp_load(rt):
                cbb = B["on"]
                kb.dma(pool, cbb.t[0:120, :], sconv[rt * 4:(rt + 1) * 4].rearrange("s r d -> (s r) d"), [], cbb.b, cbb.b[0])

            def cp_comp(rt):
                u2T = u2T_s
                cbb = B["on"]
                kb.mm([lambda h, kc=kc: h.transpose(out=PT.t[:, kc, 0:120], in_=cbb.t[0:120, kc * 128:(kc + 1) * 128], identity=ident_b[0:120, 0:120]) for kc in range(8)], cbb.b + [cstb.b[0]], PT.b)
                for c in range(8):
                    kb.op(dve, lambda h, c=c: h.tensor_scalar(out=u2T.t[:, c, :].rearrange("p (i s) -> p s i", s=16)[:, rt * 4:(rt + 1) * 4, 0:30],
                                                              in0=PT.t[:, c, 0:120].rearrange("p (s i) -> p s i", i=30), scalar1=2.0, scalar2=None, op0=ALU.mult), PT.b, [u2T.b[c]])

            def cp_step(k):
                def f():
                    if k == 0:
                        cp_load(0)
                    elif k == 1:
                        cp_comp(0); cp_load(1)
                    elif k == 2:
                        cp_comp(1); cp_load(2)
                    elif k == 3:
                        cp_comp(2)
                    elif k == 4:
                        cp_load(3)
                    else:
                        cp_comp(3)
                        kb.dma(pool, css[:, 0:26, :], sconv[:, 4:30, :], [], [], Buf("cssh"))
                return f
            for ti in range(4):
                npro = None
                if ti == 3:
                    Bs_ = {"hT": hT_s, "hb": B["hb"]}
                    st_, hc_, tr_ = make_pro(Bs_, TS_S, TS_S, 1, xt_s)
                    npro = (lambda: tile_pro_load(Bs_, TS_S, TS_S, 1, x_s, 0, xt_s), st_,
                            (lambda j, hc_=hc_: hc_(j) if j == 0 else None), (lambda j, tr_=tr_: tr_(j) if j == 0 else None), sample_prechain)
                if ti < 3:
                    st_, hc_, tr_ = make_pro(B, 512, 128, 4, xts[(ti + 1) % 2])
                    if ti == 1:
                        for k_ in range(6):
                            deferred.append(cp_step(k_))
                    npro = (lambda ti=ti: tile_pro_load(B, 512, 128, 4, x_p, (ti + 1) * 512, xts[(ti + 1) % 2]), st_, hc_, tr_)
                run_tile(es_p, B, 512, 128, 4, False, x_p, p_p, y_p, ti * 512, ti == 3, xts[ti % 2], npro)
            kb.barrier()
        with ExitStack() as es_s:
            B = alloc_bufs(es_s, 64, 64, 1, True)
            for g_ in range(NSEQ_S):
                kb.dma(pool, B["S0b_all"].t[:, g_, :].rearrange("k (h d) -> k h d", h=4), sgla[g_].rearrange("h k d -> k h d"),
                       [], [B["S0b_all"].b[g_]], B["S0b_all"].b[g_])
            run_tile(es_s, B, 64, 64, 1, True, x_s, p_s, y_s, 0, False, B["xt"][0], None, pre=True)
            kb.barrier()
    print('nsem', kb.nsem)
    return nc


_NC = None


def _bf_exact_consts():
    cst = np.zeros((128, 768), np.float32)
    cstb = np.zeros((128, 2048), np.float32)
    idx = np.arange(128)
    s, t = idx[:, None], idx[None, :]
    cst[:, 0:128] = np.eye(128)
    cst[:, 128:256] = np.where(s <= t, -1.0 / 16, 0.0)
    cst[:, 256:384] = np.where(s > t, -1.0 / 16, 0.0)
    i64 = np.arange(64)
    s6, t6 = i64[:, None], i64[None, :]
    same = (s6 % 16) == (t6 % 16)
    cst[0:64, 384:448] = np.where(same & (s6 // 16 <= t6 // 16), -1.0 / 16, 0.0)
    cst[0:64, 448:512] = np.where(same & (s6 // 16 > t6 // 16), -1.0 / 16, 0.0)
    cst[:, 512:640] = 1.0
    cst[:, 640:704] = -0.5
    cst[0:64, 704:720] = (i64[:, None] % 16 == np.arange(16)[None, :]).astype(np.float32)
    cst[:, 720] = EPS
    cstb[:, 0:128] = np.eye(128)
    cstb[:, 128:256] = 1.0
    cstb[:, 256:768] = np.tile((s <= t).astype(np.float32), (1, 4))
    cstb[0:64, 768:1024] = np.tile((same & (s6 // 16 <= t6 // 16)).astype(np.float32), (1, 4))
    cm = (np.arange(64)[None, :] % 16 == np.arange(16)[:, None]).astype(np.float32)
    cstb[:, 1024:2048] = np.tile(cm.reshape(1, 1024), (128, 1))
    return cst, cstb


def kernel(x_prompt, x_sample, state_gla, state_conv, p_prompt, p_sample, g_pre, w_in, w_a_up, b_a_up,
           g_gla, w_a_out, w_dw, b_dw, g_ln, b_ln, w_b_out, w_o, w_pe, w_pg, g_final):
    global _NC
    f = lambda a: np.ascontiguousarray(np.asarray(a, dtype=np.float32))
    x_prompt, x_sample, state_gla, state_conv = f(x_prompt), f(x_sample), f(state_gla), f(state_conv)
    p_prompt, p_sample = f(p_prompt), f(p_sample)
    w_in_ = f(w_in)[0]
    def slab(W, c0):
        return W[:, c0:c0 + 512].reshape(8, 128, 512).transpose(1, 0, 2).reshape(128, SLAB_E)
    wa, wb_, wo_, wpg_ = f(w_a_out)[0], f(w_b_out)[0], f(w_o)[0], f(w_pg)[0]
    cQ, cK, cV, cZA, cR, cGA_, cGG, cZB, cGa, cGb = 0, 512, 1024, 2048, 3072, 3088, 4112, 5136, 6160, 7184
    order = [(w_in_, cV), (w_in_, cV + 512), (w_in_, cK), (w_in_, cQ), (w_in_, cZA), (w_in_, cZA + 512),
             ("glu", 0), ("glu", 1), ("glu", 2), ("glu", 3),
             (w_in_, cZB), (w_in_, cZB + 512),
             (w_in_, cGa), (w_in_, cGa + 512), (wa, 0), (wa, 512), (w_in_, cGb), (w_in_, cGb + 512), (wb_, 0), (wb_, 512),
             (wo_, 0), (wo_, 512), (wpg_, 0), (wpg_, 512)]
    wslab = np.empty((NSLAB_W, 128, SLAB_E), np.float32)
    for i, (W, c0) in enumerate(order):
        if isinstance(W, str):
            k = c0
            cols = np.concatenate([np.arange(cGG + 2 * k * 128, cGG + (2 * k + 2) * 128), np.arange(cGA_ + 2 * k * 128, cGA_ + (2 * k + 2) * 128)])
            wslab[i] = w_in_[:, cols].reshape(8, 128, 512).transpose(1, 0, 2).reshape(128, SLAB_E)
        else:
            wslab[i] = slab(W, c0)
    wpe_ = f(w_pe)[0]
    wslab[24] = 0.0
    wslab[24][:, 0:2048] = wpe_.reshape(2, 128, 1024).transpose(1, 0, 2).reshape(128, 2048)
    wr = np.ascontiguousarray(w_in_[:, cR:cR + 16].reshape(8, 128, 16).transpose(1, 0, 2))
    wup = np.concatenate([f(w_a_up)[0], f(b_a_up)[0][None, :]], axis=0)
    cst, cstb = _bf_exact_consts()
    vec = np.zeros((128, 2048 + 288), np.float32)
    vec[:, 0:1024] = f(g_pre)[0][None, :]
    vec[:, 1024:2048] = f(g_final)[None, :]
    vec[:, 2048:2056] = f(b_dw)[0].reshape(8, 128).T
    vec[:, 2056:2064] = f(g_ln)[0].reshape(8, 128).T
    vec[:, 2064:2072] = f(b_ln)[0].reshape(8, 128).T
    vec[:, 2072:2074] = f(g_gla)[0].reshape(2, 128).T
    vec[:, 2080:2080 + 248] = f(w_dw)[0].reshape(31, 8, 128).transpose(2, 1, 0).reshape(128, 248)
    if _NC is None:
        _NC = build_program()
    in_maps = []
    for c in range(NCORES):
        sl = slice(c * NSEQ_S, (c + 1) * NSEQ_S)
        in_maps.append({
            "x_p": x_prompt[c],
            "x_s": np.ascontiguousarray(x_sample[sl].transpose(1, 0, 2).reshape(TS_S, D)),
            "p_p": p_prompt[0, c],
            "p_s": np.ascontiguousarray(p_sample[0, sl].transpose(1, 0, 2).reshape(TS_S, 256)),
            "sgla": state_gla[0, sl],
            "sconv": state_conv[0, sl],
            "wslab": wslab, "wr": wr, "wup": wup, "cst": cst, "cstb": cstb, "vec": vec,
        })
    res = run_bass_kernel_spmd(_NC, in_maps, core_ids=list(range(NCORES)))
    R = res.results
    y_prompt = np.stack([R[c]["y_p"] for c in range(NCORES)], 0)
    y_sample = np.concatenate([R[c]["y_s"].reshape(4, NSEQ_S, D).transpose(1, 0, 2) for c in range(NCORES)], 0)
    gsp = np.stack([R[c]["gsp"] for c in range(NCORES)], 0)[None]
    csp = np.stack([R[c]["csp"] for c in range(NCORES)], 0)[None]
    gss = np.concatenate([R[c]["gss"] for c in range(NCORES)], 0)[None]
    css = np.concatenate([R[c]["css"] for c in range(NCORES)], 0)[None]
    return (y_prompt.astype(np.float32), y_sample.astype(np.float32), gsp.astype(np.float32),
            csp.astype(np.float32), gss.astype(np.float32), css.astype(np.float32))
```
